# Optimizing a Trainium2 kernel written in Bass

```python
import jax
import jax.numpy as jnp
from jax import lax
import numpy as np

D_MODEL = 2048
BATCH = 16
SEQ = 2048
DEPTH = 4

HEAD_DIM = 128
SB_HEADS = 8
DSA_HEADS = 8
IDX_HEADS = 16
IDX_DIM = 64
TOPK_MAX = 256
D_FF = 5632
CONV_WIDTH = 3
Q_BLOCK = 128
ROPE_THETA = 10000.0
NORM_EPS = 1e-6
N_MOD = 6
SB_WIDTH = SB_HEADS * HEAD_DIM
DSA_WIDTH = DSA_HEADS * HEAD_DIM
IDX_Q_WIDTH = IDX_HEADS * IDX_DIM
IN_SPLIT_SIZES = (SB_WIDTH, SB_WIDTH, SB_WIDTH, DSA_WIDTH, DSA_WIDTH, DSA_WIDTH,
                  IDX_Q_WIDTH, IDX_DIM, IDX_HEADS, D_MODEL, D_MODEL)
IN_WIDTH = 3 * SB_WIDTH + 3 * DSA_WIDTH + IDX_Q_WIDTH + IDX_DIM + IDX_HEADS + 2 * D_MODEL

kernel_name = "hybrid_stickbreak_dsa_convffn_adaln"


def rmsnorm(x, g):
    xf = x.astype(jnp.float32)
    r = xf * lax.rsqrt(jnp.mean(xf * xf, axis=-1, keepdims=True) + NORM_EPS)
    return (r * g.astype(jnp.float32)).astype(x.dtype)


def rope(t, pos):
    d = t.shape[-1]
    inv_freq = 1.0 / (ROPE_THETA ** (jnp.arange(0, d, 2, dtype=jnp.float32) / d))
    ang = pos.astype(jnp.float32)[..., None] * inv_freq
    cos = jnp.cos(ang)[:, :, None, :]
    sin = jnp.sin(ang)[:, :, None, :]
    tf = t.astype(jnp.float32)
    t1, t2 = tf[..., : d // 2], tf[..., d // 2:]
    return jnp.concatenate([t1 * cos - t2 * sin, t2 * cos + t1 * sin], axis=-1).astype(t.dtype)


def stick_breaking_attention(q, k, v):
    B, S, H, dh = q.shape
    nb = S // Q_BLOCK
    scale = dh ** -0.5
    kpos = jnp.arange(S)
    qb = q.reshape(B, nb, Q_BLOCK, H, dh).transpose(1, 0, 2, 3, 4)

    def block(args):
        q_blk, start = args
        z = jnp.einsum('bqhd,bkhd->bhqk', q_blk, k,
                       preferred_element_type=jnp.float32) * scale
        qpos = start + jnp.arange(Q_BLOCK)
        past = (kpos[None, :] < qpos[:, None])[None, None]
        log1m = jnp.where(past, -jax.nn.softplus(z), 0.0)
        after = lax.cumsum(log1m, axis=3, reverse=True) - log1m
        w = jnp.where(past, jnp.exp(jax.nn.log_sigmoid(z) + after), 0.0)
        return jnp.einsum('bhqk,bkhd->bqhd', w.astype(v.dtype), v)

    starts = jnp.arange(nb, dtype=jnp.int32) * Q_BLOCK
    out = lax.map(block, (qb, starts))
    return out.transpose(1, 0, 2, 3, 4).reshape(B, S, H, dh)


def dsa_attention(q, k, v, qi, ki, wi):
    B, S, H, dh = q.shape
    nb = S // Q_BLOCK
    topk = min(TOPK_MAX, S // 4)
    kpos = jnp.arange(S)
    starts = jnp.arange(nb, dtype=jnp.int32) * Q_BLOCK

    def per_seq(args):
        q_s, k_s, v_s, qi_s, ki_s, wi_s = args
        q_blocks = q_s.reshape(nb, Q_BLOCK, H, dh)
        qi_blocks = qi_s.reshape(nb, Q_BLOCK, IDX_HEADS, IDX_DIM)
        wi_blocks = wi_s.reshape(nb, Q_BLOCK, IDX_HEADS)

        def block(bargs):
            q_b, qi_b, wi_b, start = bargs
            qpos = start + jnp.arange(Q_BLOCK)
            dots = jnp.einsum('qhi,ki->qhk', qi_b, ki_s,
                              preferred_element_type=jnp.float32) * (IDX_DIM ** -0.5)
            score = jnp.einsum('qh,qhk->qk', wi_b.astype(jnp.float32),
                               jax.nn.relu(dots)) * (IDX_HEADS ** -0.5)
            visible = kpos[None, :] <= qpos[:, None]
            score = jnp.where(visible, score, -jnp.inf)
            _, idx = lax.top_k(score, topk)
            k_sel = k_s[idx]
            v_sel = v_s[idx]
            logits = jnp.einsum('qhd,qkhd->qhk', q_b, k_sel,
                                preferred_element_type=jnp.float32) * (dh ** -0.5)
            valid = (idx <= qpos[:, None])[:, None, :]
            p = jax.nn.softmax(jnp.where(valid, logits, -jnp.inf), axis=-1)
            return jnp.einsum('qhk,qkhd->qhd', p.astype(v_s.dtype), v_sel)

        out = lax.map(block, (q_blocks, qi_blocks, wi_blocks, starts))
        return out.reshape(S, H, dh)

    return lax.map(per_seq, (q, k, v, qi, ki, wi))


def mixer(h, pos, w_in, w_a, w_b, w_o):
    B, S, _ = h.shape
    proj = h @ w_in
    offsets = [int(o) for o in np.cumsum(IN_SPLIT_SIZES)[:-1]]
    q_a, k_a, v_a, q_b, k_b, v_b, qi, ki, wi, g_a, g_b = jnp.split(proj, offsets, axis=-1)
    y_a = stick_breaking_attention(q_a.reshape(B, S, SB_HEADS, HEAD_DIM),
                                   k_a.reshape(B, S, SB_HEADS, HEAD_DIM),
                                   v_a.reshape(B, S, SB_HEADS, HEAD_DIM))
    y_a = y_a.reshape(B, S, SB_WIDTH) @ w_a
    y_b = dsa_attention(rope(q_b.reshape(B, S, DSA_HEADS, HEAD_DIM), pos),
                        rope(k_b.reshape(B, S, DSA_HEADS, HEAD_DIM), pos),
                        v_b.reshape(B, S, DSA_HEADS, HEAD_DIM),
                        rope(qi.reshape(B, S, IDX_HEADS, IDX_DIM), pos),
                        rope(ki[:, :, None, :], pos)[:, :, 0, :],
                        wi)
    y_b = y_b.reshape(B, S, DSA_WIDTH) @ w_b
    merged = jax.nn.sigmoid(g_a) * y_a + jax.nn.sigmoid(g_b) * y_b
    return merged @ w_o


def conv_ffn(h, w_up, conv_w, conv_b, w_down):
    u = h @ w_up
    S = u.shape[1]
    up = jnp.pad(u, ((0, 0), (CONV_WIDTH - 1, 0), (0, 0)))
    u = conv_b + sum(conv_w[i] * up[:, i:i + S] for i in range(CONV_WIDTH))
    a, b = jnp.split(u, 2, axis=-1)
    return (jax.nn.silu(a) * b) @ w_down


def setup_inputs(seed: int = 0) -> dict:
    key = jax.random.key(seed)
    ks = jax.random.split(key, 16)
    f32 = jnp.float32

    def nrm(k, shape, scale):
        return jax.random.normal(k, shape, f32) * scale

    x = nrm(ks[0], (BATCH, SEQ, D_MODEL), 1.0)
    c = nrm(ks[1], (BATCH, D_MODEL), 1.0)
    offset = jax.random.randint(ks[2], (BATCH, 1), 0, 4096, dtype=jnp.int32)
    positions = offset + jnp.arange(SEQ, dtype=jnp.int32)[None, :]
    w_in = nrm(ks[3], (DEPTH, D_MODEL, IN_WIDTH), D_MODEL ** -0.5)
    w_a = nrm(ks[4], (DEPTH, SB_WIDTH, D_MODEL), SB_WIDTH ** -0.5)
    w_b = nrm(ks[5], (DEPTH, DSA_WIDTH, D_MODEL), DSA_WIDTH ** -0.5)
    w_o = nrm(ks[6], (DEPTH, D_MODEL, D_MODEL), D_MODEL ** -0.5)
    w_ada = nrm(ks[7], (DEPTH, D_MODEL, N_MOD * D_MODEL), 0.1 * D_MODEL ** -0.5)
    b_ada = nrm(ks[8], (DEPTH, N_MOD * D_MODEL), 0.01)
    g_mix = 1.0 + nrm(ks[9], (DEPTH, D_MODEL), 0.02)
    g_ffn = 1.0 + nrm(ks[10], (DEPTH, D_MODEL), 0.02)
    w_up = nrm(ks[11], (DEPTH, D_MODEL, 2 * D_FF), D_MODEL ** -0.5)
    conv_w = nrm(ks[12], (DEPTH, CONV_WIDTH, 2 * D_FF), CONV_WIDTH ** -0.5)
    conv_b = nrm(ks[13], (DEPTH, 2 * D_FF), 0.01)
    w_down = nrm(ks[14], (DEPTH, D_FF, D_MODEL), D_FF ** -0.5)
    g_final = 1.0 + nrm(ks[15], (D_MODEL,), 0.02)
    return {"x": x, "c": c, "positions": positions, "w_in": w_in, "w_a": w_a, "w_b": w_b,
            "w_o": w_o, "w_ada": w_ada, "b_ada": b_ada, "g_mix": g_mix, "g_ffn": g_ffn,
            "w_up": w_up, "conv_w": conv_w, "conv_b": conv_b, "w_down": w_down,
            "g_final": g_final}


def reference(x, c, positions, w_in, w_a, w_b, w_o, w_ada, b_ada, g_mix, g_ffn,
              w_up, conv_w, conv_b, w_down, g_final):
    c_act = jax.nn.silu(c)
    for layer in range(DEPTH):
        mod = c_act @ w_ada[layer] + b_ada[layer]
        sh1, sc1, gt1, sh2, sc2, gt2 = jnp.split(mod[:, None, :], N_MOD, axis=-1)
        h = rmsnorm(x, g_mix[layer]) * (1 + sc1) + sh1
        x = x + (1 + gt1) * mixer(h, positions, w_in[layer], w_a[layer], w_b[layer], w_o[layer])
        h = rmsnorm(x, g_ffn[layer]) * (1 + sc2) + sh2
        x = x + (1 + gt2) * conv_ffn(h, w_up[layer], conv_w[layer], conv_b[layer], w_down[layer])
    return rmsnorm(x, g_final)
```

```python
import contextlib
import numpy as np
import concourse.bass as bass
import concourse.mybir as mybir
from concourse.bass_utils import run_bass_kernel_spmd

F32 = mybir.dt.float32
BF16 = mybir.dt.bfloat16
I32 = mybir.dt.int32
AF = mybir.ActivationFunctionType
ALU = mybir.AluOpType
AX = mybir.AxisListType
BIG = 1.0e30
NIT = 24


class Cfg:
    def __init__(self, D=2048, S=2048, H=8, IH=16, DFF=5632, L=4, TOPK=256, NB=2):
        self.D, self.S, self.H, self.IH, self.DFF, self.L, self.NB = D, S, H, IH, DFF, L, NB
        self.TOPK = min(TOPK, S // 4)
        self.KC = D // 128
        self.NT = S // 128
        self.NTB = S // 512
        self.HW = H * 128
        self.EPS = 1e-6
        self.THETA = 10000.0
        fm = []
        for h in range(H):
            fm.append(("qa", h, None))
        for h in range(H):
            fm.append(("ka", h, None))
        for h in range(H):
            fm.append(("qb", h, "r128"))
        for h in range(H):
            fm.append(("kb", h, "r128"))
        for j in range(IH // 2):
            fm.append(("qi", j, "r64"))
        fm.append(("ki", 0, "r64"))
        for j in range(self.KC):
            fm.append(("ga", j, None))
        for j in range(self.KC):
            fm.append(("gb", j, None))
        self.fm = fm
        self.NFM = len(fm) * 128
        self.NTM = 2 * self.HW + IH
        self.NEXT = self.NFM + self.NTM
        o = {}
        off = 0
        for name, sz in (("qa", self.HW), ("ka", self.HW), ("va", self.HW), ("qb", self.HW),
                         ("kb", self.HW), ("vb", self.HW), ("qi", IH * 64), ("ki", 64),
                         ("wi", IH), ("ga", D), ("gb", D)):
            o[name] = off
            off += sz
        self.off = o
        self.IN_WIDTH = off

    def ext_cols(self):
        idx = []
        sw128 = np.concatenate([np.arange(64, 128), np.arange(0, 64)])
        sw64 = np.concatenate([np.arange(32, 64), np.arange(0, 32), np.arange(96, 128), np.arange(64, 96)])
        for kind, j, mode in self.fm:
            if kind == "ki":
                base = self.off["ki"] + np.concatenate([np.arange(64), np.arange(64)])
            else:
                base = self.off[kind] + j * 128 + np.arange(128)
            if mode == "sw":
                base = base[sw128] if kind in ("qb", "kb") else base[sw64]
            idx.append(base)
        idx.append(self.off["va"] + np.arange(self.HW))
        idx.append(self.off["vb"] + np.arange(self.HW))
        idx.append(self.off["wi"] + np.arange(self.IH))
        return np.concatenate(idx)


C_INVF128, C_INVF64, C_SGN128, C_SGN64 = 0, 1, 2, 3
C_IDENT = 8
C_TRIU = C_IDENT + 128
C_MSB = C_TRIU + 128
C_MDSA = C_MSB + 2048
C_NEGTRI = C_MDSA + 128
C_POW2 = C_NEGTRI + 128
C_SEL = C_POW2 + 32
NCONST = C_SEL + 256


def make_consts(cfg):
    c = np.zeros((128, NCONST), np.float32)
    p = np.arange(128)
    c[:, C_INVF128] = 1.0 / (cfg.THETA ** (np.arange(0, 128, 2, dtype=np.float32) / 128.0))[p % 64]
    c[:, C_INVF64] = 1.0 / (cfg.THETA ** (np.arange(0, 64, 2, dtype=np.float32) / 64.0))[p % 32]
    c[:, C_SGN128] = np.where(p < 64, -1.0, 1.0)
    c[:, C_SGN64] = np.where((p % 64) < 32, -1.0, 1.0)
    f = np.arange(128)
    c[:, C_IDENT:C_IDENT + 128] = (p[:, None] == f[None, :])
    c[:, C_TRIU:C_TRIU + 128] = (p[:, None] > f[None, :])
    f5 = np.arange(512)
    for r in range(4):
        c[:, C_MSB + 512 * r:C_MSB + 512 * (r + 1)] = ((128 * r + p[:, None]) < f5[None, :])
    c[:, C_MDSA:C_MDSA + 128] = (p[:, None] <= f[None, :])
    c[:, C_NEGTRI:C_NEGTRI + 128] = np.where(f[None, :] > p[:, None], -BIG, 0.0)
    c[:, C_POW2:C_POW2 + NIT + 1] = (2.0 ** -(np.arange(NIT + 1) + 1.0))[None, :]
    c[0, C_SEL:C_SEL + 128] = 1.0
    c[1, C_SEL + 128:C_SEL + 256] = 1.0
    return c


class Buf:
    __slots__ = ("t", "w", "r", "name")

    def __init__(self, t=None, name=""):
        self.t = t
        self.w = {}
        self.r = {}
        self.name = name

    def __getitem__(self, idx):
        return self.t[idx]


class Ctx:
    ENG = ("pe", "act", "dve", "pool", "sp")

    def __init__(self, nc):
        self.nc = nc
        self.es = contextlib.ExitStack()
        self.h = {"pe": nc.tensor, "act": nc.scalar, "dve": nc.vector,
                  "pool": nc.gpsimd, "sp": nc.sync}
        self.sem, self.cnt = {}, {}
        self.seen = {e: {} for e in self.ENG}
        for e in self.ENG:
            self.sem[e] = self.es.enter_context(nc.semaphore("s_" + e))
            self.cnt[e] = 0
        self.dq = {}
        self.dqi = {}
        for q, n in (("sp", 24), ("pool", 8)):
            self.dq[q] = []
            self.dqi[q] = 0
            for j in range(n):
                k = f"d_{q}{j}"
                self.sem[k] = self.es.enter_context(nc.semaphore("s_" + k))
                self.cnt[k] = 0
                self.dq[q].append(k)
        self.ninst = 0
        self.uid = 0

    def sb(self, stack, name, shape, dt):
        self.uid += 1
        return Buf(stack.enter_context(self.nc.sbuf_tensor(f"{name}_{self.uid}", shape, dt)), name)

    def ps(self, stack, name, shape, dt=F32):
        self.uid += 1
        return Buf(stack.enter_context(self.nc.psum_tensor(f"{name}_{self.uid}", shape, dt)), name)

    def _wait(self, eng, deps):
        seen = self.seen[eng]
        for k, v in deps.items():
            if seen.get(k, 0) < v:
                self.h[eng].wait_ge(self.sem[k], v)
                seen[k] = v
                self.ninst += 1

    def _deps(self, eng, reads, writes):
        deps = {}

        def add(d, skip_same):
            for k, v in d.items():
                if skip_same and k == eng:
                    continue
                if deps.get(k, 0) < v:
                    deps[k] = v
        for b in reads:
            add(b.w, False)
        for b in writes:
            add(b.w, True)
            add(b.r, True)
        return deps

    def op(self, eng, fn, reads=(), writes=()):
        self._wait(eng, self._deps(eng, reads, writes))
        ins = fn(self.h[eng])
        self.cnt[eng] += 1
        ins.then_inc(self.sem[eng], 1)
        self.ninst += 1
        c = self.cnt[eng]
        for b in writes:
            b.w[eng] = c
            b.r = {}
        for b in reads:
            b.r[eng] = c
        return ins

    def dma(self, q, out_ap, in_ap, reads=(), writes=(), **kw):
        k = self.dq[q][self.dqi[q] % len(self.dq[q])]
        self.dqi[q] += 1
        deps = self._deps("dma", reads, writes)
        if self.cnt[k] > 0:
            deps[k] = max(deps.get(k, 0), self.cnt[k])
        self._wait(q, deps)
        ins = self.h[q].dma_start(out=out_ap, in_=in_ap, **kw)
        self.cnt[k] += 16
        ins.then_inc(self.sem[k], 16)
        self.ninst += 1
        c = self.cnt[k]
        for b in writes:
            b.w[k] = c
            b.r = {}
        for b in reads:
            b.r[k] = c

    def barrier(self):
        allk = dict(self.cnt)
        for e in self.ENG:
            self._wait(e, {k: v for k, v in allk.items() if k != e and v > 0})

    def finish(self):
        self.barrier()
        self.es.close()


class Ring:
    def __init__(self, bufs):
        self.b = bufs
        self.i = 0

    def next(self):
        b = self.b[self.i % len(self.b)]
        self.i += 1
        return b


def swpipe(items, stages):
    n, ns = len(items), len(stages)
    for it in range(n + ns - 1):
        for si, st in enumerate(stages):
            k = it - si
            if 0 <= k < n:
                st(items[k])


def build(cfg, layers=None, phases=None, debug_out=()):
    D, S, H, IH, KC, NT, NTB, HW, DFF, NB = cfg.D, cfg.S, cfg.H, cfg.IH, cfg.KC, cfg.NT, cfg.NTB, cfg.HW, cfg.DFF, cfg.NB
    L = cfg.L
    layers = list(range(L)) if layers is None else layers
    nc = bass.Bass("TRN2", target_bir_lowering=False)
    cx = Ctx(nc)

    def din(name, shape, dt=F32):
        return nc.dram_tensor(name, shape, dt, kind="ExternalInput").ap()

    def dscr(name, shape, dt):
        kind = "ExternalOutput" if name in debug_out else "Internal"
        return nc.dram_tensor(name, shape, dt, kind=kind).ap()

    x_in = din("x", [NB, S, D])
    c_in = din("c", [NB, D])
    pos_in = din("positions", [NB, S], I32)
    w_in = din("w_in_ext", [L, D, cfg.NEXT])
    w_a = din("w_a", [L, HW, D])
    w_b = din("w_b", [L, HW, D])
    w_o = din("w_o", [L, D, D])
    w_ada = din("w_ada", [L, D, 6 * D])
    b_ada = din("b_ada", [L, 6 * D])
    g_mix = din("g_mix", [L, D])
    g_ffn = din("g_ffn", [L, D])
    w_up = din("w_up", [L, D, 2 * DFF])
    conv_w = din("conv_w", [L, 3, 2 * DFF])
    conv_b = din("conv_b", [L, 2 * DFF])
    w_down = din("w_down", [L, DFF, D])
    g_final = din("g_final", [D])
    consts = din("consts", [128, NCONST])
    out = nc.dram_tensor("out", [NB, S, D], F32, kind="ExternalOutput").ap()

    xres = dscr("xres", [NB, S, D], F32)
    modrow = dscr("modrow", [NB, 6 * D], F32)
    ropeT = dscr("ropeT", [NB, 4, 128, S], F32)
    qaT = dscr("qaT", [H, 128, S], BF16)
    kaT = dscr("kaT", [H, 128, S], BF16)
    qbT = dscr("qbT", [H, 128, S], BF16)
    kbT = dscr("kbT", [H, 128, S], BF16)
    qiT = dscr("qiT", [IH // 2, 128, S], BF16)
    kiT = dscr("kiT", [1, 128, S], BF16)
    sgaT = dscr("sgaT", [KC, 128, S], BF16)
    sgbT = dscr("sgbT", [KC, 128, S], BF16)
    vab = dscr("vab", [S, 2 * HW], BF16)
    wis = dscr("wis", [S, IH], F32)
    yabT = dscr("yabT", [2, H, 128, S], BF16)
    fm_dst = {"qa": qaT, "ka": kaT, "qb": qbT, "kb": kbT, "qi": qiT, "ki": kiT, "ga": sgaT, "gb": sgbT}

    dbufs = {}

    def db(key):
        if key not in dbufs:
            dbufs[key] = Buf(None, str(key))
        return dbufs[key]

    G = contextlib.ExitStack()
    cst = cx.sb(G, "cst", [128, NCONST], F32)
    cx.dma("sp", cst[:], consts, writes=[cst])
    ident_bf = cx.sb(G, "ident_bf", [128, 128], BF16)
    ident_f = Buf(cst.t, "identf")
    triU_bf = cx.sb(G, "triU_bf", [128, 128], BF16)
    ones_bf = cx.sb(G, "ones_bf", [128, 128], BF16)
    msb_bf = cx.sb(G, "msb_bf", [128, 2048], BF16)
    mdsa_bf = cx.sb(G, "mdsa_bf", [128, 128], BF16)
    cx.op("dve", lambda e: e.tensor_copy(out=ident_bf[:], in_=cst[:, C_IDENT:C_IDENT + 128]), reads=[cst], writes=[ident_bf])
    cx.op("dve", lambda e: e.tensor_copy(out=triU_bf[:], in_=cst[:, C_TRIU:C_TRIU + 128]), reads=[cst], writes=[triU_bf])
    cx.op("dve", lambda e: e.memset(ones_bf[:], 1.0), writes=[ones_bf])
    cx.op("dve", lambda e: e.tensor_copy(out=msb_bf[:], in_=cst[:, C_MSB:C_MSB + 2048]), reads=[cst], writes=[msb_bf])
    cx.op("dve", lambda e: e.tensor_copy(out=mdsa_bf[:], in_=cst[:, C_MDSA:C_MDSA + 128]), reads=[cst], writes=[mdsa_bf])
    cactT = cx.sb(G, "cactT", [128, KC, NB], F32)
    vecs = cx.sb(G, "vecs", [128, NB, 6, KC], F32)
    cwT = cx.sb(G, "cwT", [128, 4, 2 * DFF // 128], F32)
    eps_t = cx.sb(G, "eps_t", [128, 1], F32)
    cx.op("dve", lambda e: e.memset(eps_t[:], cfg.EPS), writes=[eps_t])
    negpi = cx.sb(G, "negpi", [128, 1], F32)
    cx.op("dve", lambda e: e.memset(negpi[:], -float(np.pi)), writes=[negpi])
    one_t = cx.sb(G, "one_t", [128, 1], F32)
    cx.op("dve", lambda e: e.memset(one_t[:], 1.0), writes=[one_t])

    def phase_setup():
        with contextlib.ExitStack() as st:
            crow = cx.sb(st, "crow", [NB, D], F32)
            cx.dma("sp", crow[:], c_in, writes=[crow])
            crs = cx.sb(st, "crs", [NB, D], F32)
            cx.op("act", lambda e: e.activation(out=crs[:], in_=crow[:], func=AF.Silu), reads=[crow], writes=[crs])
            pt = cx.ps(st, "pt", [128, KC, NB], F32)
            for kc in range(KC):
                cx.op("pe", lambda e, kc=kc: e.transpose(out=pt[:, kc, :], in_=crs[:, kc * 128:(kc + 1) * 128], identity=ident_f[0:NB, C_IDENT:C_IDENT + NB]), reads=[crs, cst], writes=[pt])
            cx.op("dve", lambda e: e.tensor_copy(out=cactT[:], in_=pt[:]), reads=[pt], writes=[cactT])
            posi = cx.sb(st, "posi", [128, S], I32)
            posf = cx.sb(st, "posf", [128, S], F32)
            ang = cx.sb(st, "ang", [128, S], F32)
            a2 = cx.sb(st, "a2", [128, S], F32)
            ki_ = cx.sb(st, "ki_", [128, S], I32)
            kf = cx.sb(st, "kf", [128, S], F32)
            m = cx.sb(st, "m", [128, S], F32)
            tbr = Ring([cx.sb(st, f"tb_{i}", [128, S], F32) for i in range(2)])
            for b in range(NB):
                cx.dma("sp", posi[:], pos_in[b:b + 1, :].broadcast_to([128, S]), writes=[posi])
                cx.op("dve", lambda e: e.tensor_copy(out=posf[:], in_=posi[:]), reads=[posi], writes=[posf])
                for ti, (ccol, scol) in enumerate(((C_INVF128, C_SGN128), (C_INVF64, C_SGN64))):
                    cx.op("dve", lambda e: e.tensor_scalar(out=ang[:], in0=posf[:], scalar1=cst[:, ccol:ccol + 1], scalar2=None, op0=ALU.mult), reads=[posf, cst], writes=[ang])
                    for which in range(2):
                        shift = float(np.pi / 2) if which == 0 else 0.0
                        cx.op("dve", lambda e: e.tensor_scalar(out=ki_[:], in0=ang[:], scalar1=shift, scalar2=float(1.0 / (2 * np.pi)), op0=ALU.add, op1=ALU.mult), reads=[ang], writes=[ki_])
                        cx.op("dve", lambda e: e.tensor_copy(out=kf[:], in_=ki_[:]), reads=[ki_], writes=[kf])
                        cx.op("dve", lambda e: e.scalar_tensor_tensor(out=a2[:], in0=kf[:], scalar=-float(2 * np.pi), in1=ang[:], op0=ALU.mult, op1=ALU.add), reads=[kf, ang], writes=[a2])
                        if which == 0:
                            cx.op("dve", lambda e: e.tensor_scalar(out=a2[:], in0=a2[:], scalar1=shift, scalar2=None, op0=ALU.add), reads=[a2], writes=[a2])
                        cx.op("dve", lambda e: e.tensor_scalar(out=m[:], in0=a2[:], scalar1=float(np.pi), scalar2=-float(2 * np.pi), op0=ALU.is_gt, op1=ALU.mult), reads=[a2], writes=[m])
                        cx.op("dve", lambda e: e.tensor_tensor(out=a2[:], in0=a2[:], in1=m[:], op=ALU.add), reads=[a2, m], writes=[a2])
                        cx.op("dve", lambda e: e.tensor_scalar(out=m[:], in0=a2[:], scalar1=-float(np.pi), scalar2=float(2 * np.pi), op0=ALU.is_lt, op1=ALU.mult), reads=[a2], writes=[m])
                        cx.op("dve", lambda e: e.tensor_tensor(out=a2[:], in0=a2[:], in1=m[:], op=ALU.add), reads=[a2, m], writes=[a2])
                        cx.op("dve", lambda e: e.tensor_scalar(out=a2[:], in0=a2[:], scalar1=-3.1415925, scalar2=3.1415925, op0=ALU.max, op1=ALU.min), reads=[a2], writes=[a2])
                        tb_ = tbr.next()
                        cx.op("act", lambda e: e.activation(out=tb_[:], in_=a2[:], func=AF.Sin), reads=[a2], writes=[tb_])
                        if which == 1:
                            cx.op("dve", lambda e: e.tensor_scalar(out=tb_[:], in0=tb_[:], scalar1=cst[:, scol:scol + 1], scalar2=None, op0=ALU.mult), reads=[tb_, cst], writes=[tb_])
                        cx.dma("sp", ropeT[b, 2 * ti + which], tb_[:], reads=[tb_], writes=[db("ropeT")])
        cx.barrier()

    def load_fm_vec(st, dst_ap_fn, src_rows_ap, nrows, pst, tmp, reads_extra=()):
        cx.dma("sp", tmp[0:nrows, :], src_rows_ap, writes=[tmp], reads=list(reads_extra))
        cx.op("pe", lambda e: e.transpose(out=pst[:, 0:nrows], in_=tmp[0:nrows, :], identity=ident_f[0:nrows, C_IDENT:C_IDENT + nrows]), reads=[tmp, cst], writes=[pst])
        dst_ap_fn(pst)

    def phase_mod(l):
        NCH = 6 * D // 512
        with contextlib.ExitStack() as st:
            wr = Ring([cx.sb(st, f"wada{i}", [128, KC, 512], BF16) for i in range(2)])
            cab = cx.sb(st, "cab", [128, KC, NB], BF16)
            cx.op("dve", lambda e: e.tensor_copy(out=cab[:], in_=cactT[:]), reads=[cactT], writes=[cab])
            pr = Ring([cx.ps(st, f"pmod{i}", [NB, 512], F32) for i in range(2)])
            br = Ring([cx.sb(st, f"bada{i}", [NB, 512], F32) for i in range(2)])
            sr = Ring([cx.sb(st, f"smod{i}", [NB, 512], F32) for i in range(2)])
            wv = w_ada[l].rearrange("(kc p) n -> p kc n", p=128)
            for ch in range(NCH):
                wt = wr.next()
                cx.dma("pool", wt[:], wv[:, :, ch * 512:(ch + 1) * 512], writes=[wt])
                bt = br.next()
                cx.dma("sp", bt[:], b_ada[l:l + 1, ch * 512:(ch + 1) * 512].broadcast_to([NB, 512]), writes=[bt])
                ps = pr.next()
                for kc in range(KC):
                    cx.op("pe", lambda e, kc=kc: e.matmul(ps[:], cab[:, kc, :], wt[:, kc, :], start=(kc == 0), stop=(kc == KC - 1)), reads=[cab, wt], writes=[ps])
                sm = sr.next()
                cx.op("dve", lambda e: e.tensor_tensor(out=sm[:], in0=ps[:], in1=bt[:], op=ALU.add), reads=[ps, bt], writes=[sm])
                cx.dma("sp", modrow[:, ch * 512:(ch + 1) * 512], sm[:], reads=[sm], writes=[db("modrow")])
            tmp = cx.sb(st, "fmtmp", [128, 128], F32)
            pst = cx.ps(st, "fmps", [128, 128], F32)
            gm = cx.sb(st, "gm", [128, 2, KC], F32)
            for i, gsrc in enumerate((g_mix, g_ffn)):
                load_fm_vec(st, lambda p, i=i: cx.op("dve", lambda e: e.tensor_copy(out=gm[:, i, :], in_=p[:, 0:KC]), reads=[p], writes=[gm]),
                            gsrc[l].rearrange("(j p) -> j p", p=128), KC, pst, tmp)
            mt = cx.sb(st, "mt", [128, 6 * KC], F32)
            for b in range(NB):
                load_fm_vec(st, lambda p: cx.op("dve", lambda e: e.tensor_copy(out=mt[:], in_=p[:, 0:6 * KC]), reads=[p], writes=[mt]),
                            modrow[b].rearrange("(j p) -> j p", p=128), 6 * KC, pst, tmp, reads_extra=[db("modrow")])
                for s_, (ish, isc) in enumerate(((0, 1), (3, 4))):
                    cx.op("dve", lambda e, s_=s_, isc=isc: e.scalar_tensor_tensor(out=vecs[:, b, 3 * s_, :], in0=mt[:, isc * KC:(isc + 1) * KC], scalar=1.0, in1=gm[:, s_, :], op0=ALU.add, op1=ALU.mult), reads=[mt, gm], writes=[vecs])
                    cx.op("dve", lambda e, s_=s_, ish=ish: e.tensor_copy(out=vecs[:, b, 3 * s_ + 1, :], in_=mt[:, ish * KC:(ish + 1) * KC]), reads=[mt], writes=[vecs])
            NFC = 2 * DFF // 128
            for i in range(4):
                src = conv_w[l, i] if i < 3 else conv_b[l]
                load_fm_vec(st, lambda p, i=i: cx.op("dve", lambda e: e.tensor_copy(out=cwT[:, i, :], in_=p[:, 0:NFC]), reads=[p], writes=[cwT]),
                            src.rearrange("(j p) -> j p", p=128), NFC, pst, tmp)
        cx.barrier()

    def phase_norm(st, xsrc, b, t0, ntiles, which, hT, hbufs, nps=2):
        xr = Ring([cx.sb(st, f"nx{i}", [128, D], F32) for i in range(2)])
        jr = cx.sb(st, "njunk", [128, D], BF16)
        xnr = Ring([cx.sb(st, f"nxn{i}", [128, D], BF16) for i in range(2)])
        ssr = Ring([cx.sb(st, f"nss{i}", [128, 4], F32) for i in range(2)])
        NPB = max(1, (KC * 128 * 2) // 2048)
        ptr = Ring([cx.ps(st, f"npt{i}", [128, KC, 128], BF16) for i in range(nps)])
        for ti in range(ntiles):
            tt = t0 + ti
            xt = xr.next()
            cx.dma("sp", xt[:], xsrc[b, tt * 128:(tt + 1) * 128, :], reads=[db(("x", b, tt))], writes=[xt])
            ss = ssr.next()
            cx.op("act", lambda e: e.activation(out=jr[:], in_=xt[:], func=AF.Square, accum_out=ss[:, 0:1]), reads=[xt], writes=[jr, ss])
            cx.op("act", lambda e: e.activation(out=ss[:, 1:2], in_=ss[:, 0:1], func=AF.Ln, scale=1.0 / D, bias=eps_t[:]), reads=[ss, eps_t], writes=[ss])
            cx.op("act", lambda e: e.activation(out=ss[:, 2:3], in_=ss[:, 1:2], func=AF.Exp, scale=-0.5), reads=[ss], writes=[ss])
            xn = xnr.next()
            cx.op("dve", lambda e: e.tensor_scalar(out=xn[:], in0=xt[:], scalar1=ss[:, 2:3], scalar2=None, op0=ALU.mult), reads=[xt, ss], writes=[xn])
            pt = ptr.next()
            for kc in range(KC):
                cx.op("pe", lambda e, kc=kc: e.transpose(out=pt[:, kc, :], in_=xn[:, kc * 128:(kc + 1) * 128], identity=ident_bf[:]), reads=[xn, ident_bf], writes=[pt])
            hb = hbufs[ti]
            eng = "dve" if ti % 2 == 0 else "act"
            for kc in range(KC):
                o_ = hT[:, kc, ti * 128:(ti + 1) * 128]
                a_ = vecs[:, b, 3 * which, kc:kc + 1]
                b_ = vecs[:, b, 3 * which + 1, kc:kc + 1]
                if eng == "dve":
                    cx.op("dve", lambda e, o_=o_, a_=a_, b_=b_, kc=kc: e.tensor_scalar(out=o_, in0=pt[:, kc, :], scalar1=a_, scalar2=b_, op0=ALU.mult, op1=ALU.add), reads=[pt, vecs], writes=[hb])
                else:
                    cx.op("act", lambda e, o_=o_, a_=a_, b_=b_, kc=kc: e.activation(out=o_, in_=pt[:, kc, :], func=AF.Identity, scale=a_, bias=b_), reads=[pt, vecs], writes=[hb])

    def phase_proj(l, b, hT, hbufs):
        with contextlib.ExitStack() as st:
            rope = cx.sb(st, "rope", [128, 4, S], F32)
            cx.dma("sp", rope[:], ropeT[b].rearrange("f p s -> p f s"), reads=[db("ropeT")], writes=[rope])
            wr = Ring([cx.sb(st, f"pw{i}", [128, KC, 512], BF16) for i in range(2)])
            pr = Ring([cx.ps(st, f"pp{i}", [128, 512], F32) for i in range(6)])
            sr = Ring([cx.sb(st, f"pstg{i}", [128, 512], BF16) for i in range(4)])
            t1r = Ring([cx.sb(st, f"pt1{i}", [128, 512], F32) for i in range(2)])
            t2r = Ring([cx.sb(st, f"pt2{i}", [128, 512], F32) for i in range(2)])
            wv = w_in[l].rearrange("(kc p) n -> p kc n", p=128)
            nfm = len(cfg.fm)
            ci = 0
            while ci < nfm:
                ncw = min(4, nfm - ci)
                wt = wr.next()
                cx.dma("pool", wt[:, :, 0:ncw * 128], wv[:, :, ci * 128:(ci + ncw) * 128], writes=[wt])
                for tb in range(NTB):
                    hb = hbufs[tb * 4:(tb + 1) * 4]
                    cs = slice(tb * 512, (tb + 1) * 512)
                    u = 0
                    while u < ncw:
                        kind, j, mode = cfg.fm[ci + u]
                        dst = fm_dst[kind]
                        if mode is None:
                            ps = pr.next()
                            for kc in range(KC):
                                cx.op("pe", lambda e, kc=kc, u=u: e.matmul(ps[:], wt[:, kc, u * 128:(u + 1) * 128], hT[:, kc, cs], start=(kc == 0), stop=(kc == KC - 1)), reads=[wt] + hb, writes=[ps])
                            sg = sr.next()
                            if kind in ("ga", "gb"):
                                cx.op("act", lambda e: e.activation(out=sg[:], in_=ps[:], func=AF.Sigmoid), reads=[ps], writes=[sg])
                            else:
                                cx.op("act", lambda e: e.activation(out=sg[:], in_=ps[:], func=AF.Copy), reads=[ps], writes=[sg])
                            cx.dma("sp", dst[j, :, cs], sg[:], reads=[sg], writes=[db(kind)])
                            u += 1
                        else:
                            ps = pr.next()
                            for kc in range(KC):
                                cx.op("pe", lambda e, kc=kc, u=u: e.matmul(ps[:], wt[:, kc, u * 128:(u + 1) * 128], hT[:, kc, cs], start=(kc == 0), stop=(kc == KC - 1)), reads=[wt] + hb, writes=[ps])
                            f0 = 0 if mode == "r128" else 2
                            hw_ = 64 if mode == "r128" else 32
                            t1, t2 = t1r.next(), t2r.next()
                            cx.op("dve", lambda e: e.tensor_tensor(out=t1[:], in0=ps[:], in1=rope[:, f0, cs], op=ALU.mult), reads=[ps, rope], writes=[t1])
                            for p0 in range(0, 128, 2 * hw_):
                                lo, hi = slice(p0, p0 + hw_), slice(p0 + hw_, p0 + 2 * hw_)
                                cx.op("dve", lambda e, lo=lo, hi=hi: e.tensor_tensor(out=t2[lo, :], in0=ps[hi, :], in1=rope[lo, f0 + 1, cs], op=ALU.mult), reads=[ps, rope], writes=[t2])
                                cx.op("dve", lambda e, lo=lo, hi=hi: e.tensor_tensor(out=t2[hi, :], in0=ps[lo, :], in1=rope[hi, f0 + 1, cs], op=ALU.mult), reads=[ps, rope], writes=[t2])
                            sg = sr.next()
                            cx.op("dve", lambda e: e.tensor_tensor(out=sg[:], in0=t1[:], in1=t2[:], op=ALU.add), reads=[t1, t2], writes=[sg])
                            cx.dma("sp", dst[j, :, cs], sg[:], reads=[sg], writes=[db(kind)])
                            u += 1
                ci += ncw
            c0 = cfg.NFM
            ntm = 2 * HW
            cc = 0
            while cc < ntm:
                wt = wr.next()
                cx.dma("pool", wt[:], wv[:, :, c0 + cc:c0 + cc + 512], writes=[wt])
                for tt in range(NT):
                    ps = pr.next()
                    for kc in range(KC):
                        cx.op("pe", lambda e, kc=kc: e.matmul(ps[:], hT[:, kc, tt * 128:(tt + 1) * 128], wt[:, kc, :], start=(kc == 0), stop=(kc == KC - 1)), reads=[wt, hbufs[tt]], writes=[ps])
                    sg = sr.next()
                    cx.op("act", lambda e: e.activation(out=sg[:], in_=ps[:], func=AF.Copy), reads=[ps], writes=[sg])
                    cx.dma("sp", vab[tt * 128:(tt + 1) * 128, cc:cc + 512], sg[:], reads=[sg], writes=[db("vab")])
                cc += 512
            wt = wr.next()
            cx.dma("pool", wt[:, :, 0:IH], wv[:, :, c0 + ntm:c0 + ntm + IH], writes=[wt])
            wst = Ring([cx.sb(st, f"pwi{i}", [128, IH], F32) for i in range(2)])
            for tt in range(NT):
                ps = pr.next()
                for kc in range(KC):
                    cx.op("pe", lambda e, kc=kc: e.matmul(ps[:, 0:IH], hT[:, kc, tt * 128:(tt + 1) * 128], wt[:, kc, 0:IH], start=(kc == 0), stop=(kc == KC - 1)), reads=[wt, hbufs[tt]], writes=[ps])
                ws = wst.next()
                cx.op("dve", lambda e: e.tensor_copy(out=ws[:], in_=ps[:, 0:IH]), reads=[ps], writes=[ws])
                cx.dma("sp", wis[tt * 128:(tt + 1) * 128, :], ws[:], reads=[ws], writes=[db("wis")])
        cx.barrier()

    def phase_sb():
        scale = 128.0 ** -0.5
        with contextlib.ExitStack() as st:
            qr = Ring([cx.sb(st, f"sq{i}", [128, S], BF16) for i in range(2)])
            kr = Ring([cx.sb(st, f"sk{i}", [128, S], BF16) for i in range(2)])
            vr = Ring([cx.sb(st, f"sv{i}", [128, NT, 128], BF16) for i in range(2)])
            zr = Ring([cx.ps(st, f"sz{i}", [128, 512], F32) for i in range(3)])
            ar = Ring([cx.ps(st, f"sa{i}", [128, 512], F32) for i in range(3)])
            yr = Ring([cx.ps(st, f"sy{i}", [128, 512], F32) for i in range(2)])
            er = Ring([cx.sb(st, f"se{i}", [128, 512], F32) for i in range(3)])
            spr = Ring([cx.sb(st, f"ssp{i}", [128, 512], F32) for i in range(4)])
            spbr = Ring([cx.sb(st, f"sspb{i}", [128, 512], BF16) for i in range(4)])
            lsr = Ring([cx.sb(st, f"sls{i}", [128, 512], F32) for i in range(4)])
            lbr = Ring([cx.sb(st, f"slb{i}", [128, 512], BF16) for i in range(4)])
            lgr = Ring([cx.sb(st, f"slg{i}", [128, 512], F32) for i in range(4)])
            agr = Ring([cx.sb(st, f"sag{i}", [128, 512], F32) for i in range(3)])
            wtr = Ring([cx.sb(st, f"swt{i}", [128, 512], BF16) for i in range(4)])
            ysr = Ring([cx.sb(st, f"sys{i}", [128, 512], BF16) for i in range(2)])
            items = []
            for h in range(H):
                for g in range(NTB):
                    nblk = 4 * g + 4
                    for bi, sbk in enumerate(range(nblk - 1, -1, -1)):
                        items.append(dict(h=h, g=g, bi=bi, sbk=sbk, nblk=nblk))
            hs, gs = {}, {}

            def stA(it):
                h, g, bi, sbk = it["h"], it["g"], it["bi"], it["sbk"]
                cs = slice(g * 512, (g + 1) * 512)
                if g == 0 and bi == 0:
                    q, k, v = qr.next(), kr.next(), vr.next()
                    cx.dma("sp", q[:], qaT[h], reads=[db("qa")], writes=[q])
                    cx.dma("sp", k[:], kaT[h], reads=[db("ka")], writes=[k])
                    cx.dma("sp", v[:], vab[:, h * 128:(h + 1) * 128].rearrange("(st p) d -> p st d", p=128), reads=[db("vab")], writes=[v])
                    hs[h] = (q, k, v)
                q, k, v = hs[h]
                if bi == 0:
                    gs[(h, g)] = dict(yp=yr.next(), lsum=None)
                G_ = gs[(h, g)]
                r = sbk - 4 * g
                it["r"] = r
                zp = zr.next()
                cx.op("pe", lambda e: e.matmul(zp[:], k[:, sbk * 128:(sbk + 1) * 128], q[:, cs], start=True, stop=True), reads=[k, q], writes=[zp])
                e_ = er.next()
                cx.op("act", lambda e: e.activation(out=e_[:], in_=zp[:], func=AF.Exp, scale=scale), reads=[zp], writes=[e_])
                sp = spr.next()
                cx.op("act", lambda e: e.activation(out=sp[:], in_=e_[:], func=AF.Ln, bias=one_t[:]), reads=[e_, one_t], writes=[sp])
                lg = lgr.next()
                cx.op("dve", lambda e: e.scalar_tensor_tensor(out=lg[:], in0=zp[:], scalar=scale, in1=sp[:], op0=ALU.mult, op1=ALU.subtract), reads=[zp, sp], writes=[lg])
                it["lg"] = lg
                spb = spbr.next()
                if r >= 0:
                    cx.op("dve", lambda e: e.tensor_tensor(out=spb[:], in0=sp[:], in1=msb_bf[:, r * 512:(r + 1) * 512], op=ALU.mult), reads=[sp, msb_bf], writes=[spb])
                else:
                    cx.op("dve", lambda e: e.tensor_copy(out=spb[:], in_=sp[:]), reads=[sp], writes=[spb])
                it["spb"] = spb
                lsum_prev = G_["lsum"]
                it["lb"] = None
                if lsum_prev is not None:
                    lb = lbr.next()
                    if bi % 2 == 0:
                        cx.op("act", lambda e: e.activation(out=lb[:], in_=lsum_prev[:], func=AF.Copy), reads=[lsum_prev], writes=[lb])
                    else:
                        cx.op("pool", lambda e: e.tensor_copy(out=lb[:], in_=lsum_prev[:]), reads=[lsum_prev], writes=[lb])
                    it["lb"] = lb
                if sbk > 0:
                    ls = lsr.next()
                    if lsum_prev is None:
                        cx.op("pool", lambda e: e.tensor_copy(out=ls[:], in_=spb[:]), reads=[spb], writes=[ls])
                    else:
                        cx.op("pool", lambda e: e.tensor_tensor(out=ls[:], in0=lsum_prev[:], in1=spb[:], op=ALU.add), reads=[lsum_prev, spb], writes=[ls])
                    G_["lsum"] = ls

            def stB(it):
                r, spb, lb, lg = it["r"], it["spb"], it["lb"], it["lg"]
                ap_ = ar.next()
                cx.op("pe", lambda e: e.matmul(ap_[:], triU_bf[:], spb[:], start=True, stop=(lb is None)), reads=[triU_bf, spb], writes=[ap_])
                if lb is not None:
                    cx.op("pe", lambda e: e.matmul(ap_[:], ones_bf[:], lb[:], start=False, stop=True), reads=[ones_bf, lb], writes=[ap_])
                ag = agr.next()
                cx.op("dve", lambda e: e.tensor_tensor(out=ag[:], in0=lg[:], in1=ap_[:], op=ALU.subtract), reads=[lg, ap_], writes=[ag])
                wt_ = wtr.next()
                cx.op("act", lambda e: e.activation(out=wt_[:], in_=ag[:], func=AF.Exp), reads=[ag], writes=[wt_])
                if r >= 0:
                    cx.op("pool", lambda e: e.tensor_tensor(out=wt_[:], in0=wt_[:], in1=msb_bf[:, r * 512:(r + 1) * 512], op=ALU.mult), reads=[wt_, msb_bf], writes=[wt_])
                it["wt"] = wt_

            def stC(it):
                h, g, bi, sbk, nblk = it["h"], it["g"], it["bi"], it["sbk"], it["nblk"]
                cs = slice(g * 512, (g + 1) * 512)
                q, k, v = hs[h]
                yp = gs[(h, g)]["yp"]
                wt_ = it["wt"]
                cx.op("pe", lambda e: e.matmul(yp[:], v[:, sbk, :], wt_[:], start=(bi == 0), stop=(bi == nblk - 1)), reads=[v, wt_], writes=[yp])
                if bi == nblk - 1:
                    ys = ysr.next()
                    cx.op("act", lambda e: e.activation(out=ys[:], in_=yp[:], func=AF.Copy), reads=[yp], writes=[ys])
                    cx.dma("sp", yabT[0, h, :, cs], ys[:], reads=[ys], writes=[db("yab")])

            swpipe(items, [stA, stB, stC])
        cx.barrier()

    def phase_dsa():
        scale = 128.0 ** -0.5
        K = float(cfg.TOPK)
        with contextlib.ExitStack() as st:
            ki = cx.sb(st, "dki", [128, S], BF16)
            wi = cx.sb(st, "dwi", [128, NT, IH], F32)
            cx.dma("sp", ki[:], kiT[0], reads=[db("ki")], writes=[ki])
            cx.dma("sp", wi[:], wis.rearrange("(t p) h -> p t h", p=128), reads=[db("wis")], writes=[wi])
            MTs = [cx.sb(st, f"dMT{i}", [128, NT, 512], BF16) for i in range(2)]
            TS = []
            for z in range(4):
                TS.append(dict(
                    qi=cx.sb(st, f"dqi{z}", [128, IH // 2, 128], BF16),
                    diag=cx.sb(st, f"ddiag{z}", [128, IH, 128], BF16),
                    score=cx.sb(st, f"dscore{z}", [128, S], F32),
                    junk=cx.sb(st, f"djunk{z}", [128, S], BF16),
                    mask=cx.sb(st, f"dmask{z}", [128, S], BF16),
                    sm=cx.sb(st, f"dsm{z}", [128, 8], F32),
                    wtab=cx.sb(st, f"dwtab{z}", [128, NIT + 1], F32)))
            rr = Ring([cx.sb(st, f"drelu{i}", [128, 512], BF16) for i in range(4)])
            dpr = Ring([cx.ps(st, f"ddp{i}", [128, 512], F32) for i in range(4)])
            spr = Ring([cx.ps(st, f"dsp{i}", [128, 512], F32) for i in range(1)])
            tpr = Ring([cx.ps(st, f"dtp{i}", [128, 8, 128], BF16) for i in range(1)])
            lpr = dpr
            ypr = Ring([cx.ps(st, f"dyp{i}", [128, 512], F32) for i in range(1)])
            npr = Ring([cx.ps(st, f"dnp{i}", [128, 512], F32) for i in range(1)])
            qr = Ring([cx.sb(st, f"dq{i}", [128, 512], BF16) for i in range(2)])
            kr = Ring([cx.sb(st, f"dk{i}", [128, S], BF16) for i in range(2)])
            vr = Ring([cx.sb(st, f"dv{i}", [128, NT, 128], BF16) for i in range(2)])
            ebr = Ring([cx.sb(st, f"deb{i}", [128, 512], BF16) for i in range(4)])
            ptr_ = Ring([cx.sb(st, f"dpt{i}", [128, 512], BF16) for i in range(4)])
            yer = Ring([cx.sb(st, f"dye{i}", [128, 512], F32) for i in range(4)])
            ner = Ring([cx.sb(st, f"dne{i}", [128, 512], F32) for i in range(4)])
            ysr = Ring([cx.sb(st, f"dys{i}", [128, 512], BF16) for i in range(2)])

            pend = {}

            def idx_unit(g, half):
                MT = MTs[g % 2]
                if half == 0:
                    cx.op("pool", lambda e: e.memset(MT[:, 0:4 * g + 4, :], 0.0), writes=[MT])
                tiles = []
                for z in range(2):
                    il = 2 * half + z
                    i = 4 * g + il
                    tcs = slice(il * 128, (il + 1) * 128)
                    if i < 2:
                        for sbk in range(i + 1):
                            src = mdsa_bf if sbk == i else ones_bf
                            cx.op("pool", lambda e, sbk=sbk, src=src: e.tensor_copy(out=MT[:, sbk, tcs], in_=src[:]), reads=[src], writes=[MT])
                        continue
                    T = TS[2 * ((2 * g + half) % 2) + z]
                    tiles.append((T, i, tcs))
                    W = 128 * (i + 1)
                    qi, diag, score = T["qi"], T["diag"], T["score"]
                    cx.dma("sp", qi[:], qiT[:, :, i * 128:(i + 1) * 128].rearrange("j p s -> p j s"), reads=[db("qi")], writes=[qi])
                    for hh in range(IH):
                        cx.op("act", lambda e, hh=hh: e.activation(out=diag[:, hh, :], in_=ident_bf[:], func=AF.Copy, scale=wi[:, i, hh:hh + 1]), reads=[ident_bf, wi], writes=[diag])
                    nkc = (W + 511) // 512
                    for c in range(nkc):
                        cw = min(512, W - 512 * c)
                        sp_ = spr.next()

                        def iA(hh, c=c, cw=cw, qi=qi):
                            pb = 64 * (hh["hh"] % 2)
                            j = hh["hh"] // 2
                            dp = dpr.next()
                            cx.op("pe", lambda e: e.matmul(dp[:, 0:cw], qi[pb:pb + 64, j, :], ki[pb:pb + 64, c * 512:c * 512 + cw], start=True, stop=True), reads=[qi, ki], writes=[dp])
                            rl = rr.next()
                            cx.op("act", lambda e: e.activation(out=rl[:, 0:cw], in_=dp[:, 0:cw], func=AF.Relu), reads=[dp], writes=[rl])
                            hh["rl"] = rl

                        def iB(hh, cw=cw, sp_=sp_, diag=diag):
                            h_ = hh["hh"]
                            rl = hh["rl"]
                            cx.op("pe", lambda e: e.matmul(sp_[:, 0:cw], diag[:, h_, :], rl[:, 0:cw], start=(h_ == 0), stop=(h_ == IH - 1)), reads=[diag, rl], writes=[sp_])

                        swpipe([dict(hh=hh) for hh in range(IH)], [iA, lambda x: None, iB])
                        cx.op("act", lambda e: e.activation(out=score[:, c * 512:c * 512 + cw], in_=sp_[:, 0:cw], func=AF.Copy), reads=[sp_], writes=[score])
                pend[(g, half)] = tiles
                if not tiles:
                    return
                for T, i, tcs in tiles:
                    W = 128 * (i + 1)
                    s_, score, wtab = T["sm"], T["score"], T["wtab"]
                    cx.op("dve", lambda e: e.tensor_reduce(out=s_[:, 0:1], in_=score[:, 0:W], axis=AX.X, op=ALU.max), reads=[score], writes=[s_])
                    cx.op("dve", lambda e: e.tensor_reduce(out=s_[:, 1:2], in_=score[:, 0:W], axis=AX.X, op=ALU.min), reads=[score], writes=[s_])
                for T, i, tcs in tiles:
                    s_, score, wtab = T["sm"], T["score"], T["wtab"]
                    cx.op("dve", lambda e: e.tensor_tensor(out=s_[:, 2:3], in0=s_[:, 0:1], in1=s_[:, 1:2], op=ALU.subtract), reads=[s_], writes=[s_])
                    cx.op("dve", lambda e: e.tensor_tensor(out=score[:, i * 128:(i + 1) * 128], in0=score[:, i * 128:(i + 1) * 128], in1=cst[:, C_NEGTRI:C_NEGTRI + 128], op=ALU.add), reads=[score, cst], writes=[score])
                for T, i, tcs in tiles:
                    s_, wtab = T["sm"], T["wtab"]
                    cx.op("dve", lambda e: e.tensor_scalar(out=wtab[:], in0=cst[:, C_POW2:C_POW2 + NIT + 1], scalar1=s_[:, 2:3], scalar2=None, op0=ALU.mult), reads=[cst, s_], writes=[wtab])
                for zi, (T, i, tcs) in enumerate(tiles):
                    s_, wtab = T["sm"], T["wtab"]
                    cx.op("dve", lambda e: e.tensor_tensor(out=s_[:, 3:4], in0=s_[:, 1:2], in1=wtab[:, 0:1], op=ALU.add), reads=[s_, wtab], writes=[s_])
                    if zi == 1:
                        cx.op("dve", lambda e: e.tensor_scalar(out=s_[:, 7:8], in0=s_[:, 3:4], scalar1=-1.0, scalar2=None, op0=ALU.mult), reads=[s_], writes=[s_])
                for it in range(NIT):
                    for zi, (T, i, tcs) in enumerate(tiles):
                        W = 128 * (i + 1)
                        s_, score, junk = T["sm"], T["score"], T["junk"]
                        if zi == 0:
                            cx.op("dve", lambda e: e.tensor_scalar(out=junk[:, 0:W], in0=score[:, 0:W], scalar1=s_[:, 3:4], scalar2=None, op0=ALU.is_ge, op1=ALU.add, accum_out=s_[:, 4:5]), reads=[score, s_], writes=[junk, s_])
                        else:
                            cx.op("act", lambda e: e.activation(out=junk[:, 0:W], in_=score[:, 0:W], func=AF.Sign, bias=s_[:, 7:8], scale=1.0, accum_out=s_[:, 4:5]), reads=[score, s_], writes=[junk, s_])
                    for zi, (T, i, tcs) in enumerate(tiles):
                        W = 128 * (i + 1)
                        s_, wtab = T["sm"], T["wtab"]
                        thr_c = (K - 0.5) if zi == 0 else (2.0 * K - W - 0.5)
                        cx.op("dve", lambda e: e.scalar_tensor_tensor(out=s_[:, 5:6], in0=s_[:, 4:5], scalar=thr_c, in1=wtab[:, it:it + 1], op0=ALU.is_ge, op1=ALU.mult), reads=[s_, wtab], writes=[s_])
                    for zi, (T, i, tcs) in enumerate(tiles):
                        s_, wtab = T["sm"], T["wtab"]
                        cx.op("dve", lambda e: e.scalar_tensor_tensor(out=s_[:, 3:4], in0=s_[:, 5:6], scalar=wtab[:, it + 1:it + 2], in1=s_[:, 3:4], op0=ALU.subtract, op1=ALU.add), reads=[s_, wtab], writes=[s_])
                        if zi == 1:
                            cx.op("dve", lambda e: e.tensor_scalar(out=s_[:, 7:8], in0=s_[:, 3:4], scalar1=-1.0, scalar2=None, op0=ALU.mult), reads=[s_], writes=[s_])
                for T, i, tcs in tiles:
                    s_, wtab = T["sm"], T["wtab"]
                    cx.op("dve", lambda e: e.tensor_tensor(out=s_[:, 6:7], in0=s_[:, 3:4], in1=wtab[:, NIT:NIT + 1], op=ALU.subtract), reads=[s_, wtab], writes=[s_])
                for T, i, tcs in tiles:
                    W = 128 * (i + 1)
                    s_, score, mask = T["sm"], T["score"], T["mask"]
                    cx.op("dve", lambda e: e.tensor_scalar(out=mask[:, 0:W], in0=score[:, 0:W], scalar1=s_[:, 6:7], scalar2=None, op0=ALU.is_ge), reads=[score, s_], writes=[mask])

            def idx_fin(g, half):
                MT = MTs[g % 2]
                for T, i, tcs in pend.pop((g, half)):
                    mask = T["mask"]
                    for s0 in range(0, i + 1, 8):
                        n8 = min(8, i + 1 - s0)
                        tp = tpr.next()
                        for q_ in range(n8):
                            cx.op("pe", lambda e, q_=q_: e.transpose(out=tp[:, q_, :], in_=mask[:, (s0 + q_) * 128:(s0 + q_ + 1) * 128], identity=ident_bf[:]), reads=[mask, ident_bf], writes=[tp])
                        cx.op("act", lambda e: e.activation(out=MT[:, s0:s0 + n8, tcs], in_=tp[:, 0:n8, :], func=AF.Copy), reads=[tp], writes=[MT])

            def att_part(g, heads):
                MT = MTs[g % 2]
                cs = slice(g * 512, (g + 1) * 512)
                nblk = 4 * g + 4
                aitems = [dict(h=h, sbk=sbk) for h in heads for sbk in range(nblk)]
                ahs = {}

                def aA(it):
                    h, sbk = it["h"], it["sbk"]
                    if sbk == 0:
                        q, k, v = qr.next(), kr.next(), vr.next()
                        cx.dma("sp", q[:], qbT[h, :, cs], reads=[db("qb")], writes=[q])
                        cx.dma("sp", k[:, 0:nblk * 128], kbT[h, :, 0:nblk * 128], reads=[db("kb")], writes=[k])
                        cx.dma("sp", v[:, 0:nblk, :], vab[0:nblk * 128, HW + h * 128:HW + (h + 1) * 128].rearrange("(st p) d -> p st d", p=128), reads=[db("vab")], writes=[v])
                        ahs[h] = (q, k, v)
                    q, k, v = ahs[h]
                    lp = lpr.next()
                    cx.op("pe", lambda e: e.matmul(lp[:], k[:, sbk * 128:(sbk + 1) * 128], q[:], start=True, stop=True), reads=[k, q], writes=[lp])
                    eb = ebr.next()
                    cx.op("act", lambda e: e.activation(out=eb[:], in_=lp[:], func=AF.Exp, scale=scale), reads=[lp], writes=[eb])
                    pt = ptr_.next()
                    cx.op("pool", lambda e: e.tensor_tensor(out=pt[:], in0=eb[:], in1=MT[:, sbk, :], op=ALU.mult), reads=[eb, MT], writes=[pt])
                    it["pt"] = pt

                def aB(it):
                    h, sbk, pt = it["h"], it["sbk"], it["pt"]
                    q, k, v = ahs[h]
                    if sbk == 0:
                        ahs[(h, "y")] = (ypr.next(), npr.next())
                    yp, np_ = ahs[(h, "y")]
                    cx.op("pe", lambda e: e.matmul(yp[:], v[:, sbk, :], pt[:], start=(sbk == 0), stop=(sbk == nblk - 1)), reads=[v, pt], writes=[yp])
                    cx.op("pe", lambda e: e.matmul(np_[:], ones_bf[:], pt[:], start=(sbk == 0), stop=(sbk == nblk - 1)), reads=[ones_bf, pt], writes=[np_])
                    if sbk == nblk - 1:
                        ye, ne = yer.next(), ner.next()
                        cx.op("act", lambda e: e.activation(out=ye[:], in_=yp[:], func=AF.Copy), reads=[yp], writes=[ye])
                        cx.op("act", lambda e: e.activation(out=ne[:], in_=np_[:], func=AF.Copy), reads=[np_], writes=[ne])
                        ahs[(h, "e")] = (ye, ne)

                swpipe(aitems, [aA, lambda x: None, aB])
                for h in heads:
                    ye, ne = ahs[(h, "e")]
                    cx.op("dve", lambda e: e.reciprocal(out=ne[:], in_=ne[:]), reads=[ne], writes=[ne])
                    ys = ysr.next()
                    cx.op("dve", lambda e: e.tensor_tensor(out=ys[:], in0=ye[:], in1=ne[:], op=ALU.mult), reads=[ye, ne], writes=[ys])
                    cx.dma("sp", yabT[1, h, :, cs], ys[:], reads=[ys], writes=[db("yab")])

            idx_unit(0, 0)
            idx_unit(0, 1)
            idx_fin(0, 0)
            if NTB > 1:
                idx_unit(1, 0)
            idx_fin(0, 1)
            h1 = list(range(0, H // 2))
            h2 = list(range(H // 2, H))
            for g in range(NTB):
                att_part(g, h1)
                if g + 1 < NTB:
                    idx_unit(g + 1, 1)
                    idx_fin(g + 1, 0)
                att_part(g, h2)
                if g + 1 < NTB:
                    if g + 2 < NTB:
                        idx_unit(g + 2, 0)
                    idx_fin(g + 1, 1)
        cx.barrier()

    def phase_out(l, b):
        xsrc = x_in if l == 0 else xres
        with contextlib.ExitStack() as st0:
            mT = cx.sb(st0, "mT", [128, KC, S], BF16)
            mbufs = [Buf(mT.t, f"m{i}") for i in range(NTB)]
            with contextlib.ExitStack() as st:
                wa = cx.sb(st, "owa", [128, H, D], BF16)
                wb = cx.sb(st, "owb", [128, H, D], BF16)
                cx.dma("pool", wa[:], w_a[l].rearrange("(h p) n -> p h n", p=128), writes=[wa])
                cx.dma("pool", wb[:], w_b[l].rearrange("(h p) n -> p h n", p=128), writes=[wb])
                yar = Ring([cx.sb(st, f"oya{i}", [128, H, 512], BF16) for i in range(2)])
                ybr = Ring([cx.sb(st, f"oyb{i}", [128, H, 512], BF16) for i in range(2)])
                par = Ring([cx.ps(st, f"opa{i}", [128, 512], F32) for i in range(2)])
                pbr = Ring([cx.ps(st, f"opb{i}", [128, 512], F32) for i in range(2)])
                sar = Ring([cx.sb(st, f"osa{i}", [128, 512], BF16) for i in range(2)])
                sbr = Ring([cx.sb(st, f"osb{i}", [128, 512], BF16) for i in range(2)])
                m1r = Ring([cx.sb(st, f"om1{i}", [128, 512], F32) for i in range(2)])
                m2r = Ring([cx.sb(st, f"om2{i}", [128, 512], F32) for i in range(2)])
                for tb in range(NTB):
                    cs = slice(tb * 512, (tb + 1) * 512)
                    ya, yb = yar.next(), ybr.next()
                    cx.dma("sp", ya[:], yabT[0, :, :, cs].rearrange("h p s -> p h s"), reads=[db("yab")], writes=[ya])
                    cx.dma("sp", yb[:], yabT[1, :, :, cs].rearrange("h p s -> p h s"), reads=[db("yab")], writes=[yb])
                    for c in range(KC):
                        pa, pb = par.next(), pbr.next()
                        for h in range(H):
                            cx.op("pe", lambda e, h=h: e.matmul(pa[:], wa[:, h, c * 128:(c + 1) * 128], ya[:, h, :], start=(h == 0), stop=(h == H - 1)), reads=[wa, ya], writes=[pa])
                        for h in range(H):
                            cx.op("pe", lambda e, h=h: e.matmul(pb[:], wb[:, h, c * 128:(c + 1) * 128], yb[:, h, :], start=(h == 0), stop=(h == H - 1)), reads=[wb, yb], writes=[pb])
                        sa, sb_ = sar.next(), sbr.next()
                        cx.dma("sp", sa[:], sgaT[c, :, cs], reads=[db("ga")], writes=[sa])
                        cx.dma("sp", sb_[:], sgbT[c, :, cs], reads=[db("gb")], writes=[sb_])
                        m1, m2 = m1r.next(), m2r.next()
                        cx.op("dve", lambda e: e.tensor_tensor(out=m1[:], in0=pa[:], in1=sa[:], op=ALU.mult), reads=[pa, sa], writes=[m1])
                        cx.op("dve", lambda e: e.tensor_tensor(out=m2[:], in0=pb[:], in1=sb_[:], op=ALU.mult), reads=[pb, sb_], writes=[m2])
                        cx.op("dve", lambda e: e.tensor_tensor(out=mT[:, c, cs], in0=m1[:], in1=m2[:], op=ALU.add), reads=[m1, m2], writes=[mbufs[tb]])
            cx.barrier()
            with contextlib.ExitStack() as st:
                wor = Ring([cx.sb(st, f"owo{i}", [128, KC, 512], BF16) for i in range(2)])
                por = Ring([cx.ps(st, f"opo{i}", [128, 512], F32) for i in range(4)])
                gtb = cx.sb(st, "ogt", [128, D], F32)
                cx.dma("sp", gtb[:], modrow[b:b + 1, 2 * D:3 * D].broadcast_to([128, D]), reads=[db("modrow")], writes=[gtb])
                cx.op("dve", lambda e: e.tensor_scalar(out=gtb[:], in0=gtb[:], scalar1=1.0, scalar2=None, op0=ALU.add), reads=[gtb], writes=[gtb])
                xr = Ring([cx.sb(st, f"ox{i}", [128, 512], F32) for i in range(6)])
                tr = Ring([cx.sb(st, f"ot{i}", [128, 512], F32) for i in range(3)])
                wov = w_o[l].rearrange("(kc p) n -> p kc n", p=128)
                wos = {}

                def oA(u):
                    cg, tt = u["cg"], u["tt"]
                    if tt == 0:
                        wo = wor.next()
                        cx.dma("pool", wo[:], wov[:, :, cg * 512:(cg + 1) * 512], writes=[wo])
                        wos[cg] = wo
                    gs = slice(cg * 512, (cg + 1) * 512)
                    xt = xr.next()
                    cx.dma("sp", xt[:], xsrc[b, tt * 128:(tt + 1) * 128, gs], reads=[db(("x", b, tt, cg))], writes=[xt])
                    u["xt"] = xt

                def oB(u):
                    cg, tt, xt = u["cg"], u["tt"], u["xt"]
                    wo = wos[cg]
                    gs = slice(cg * 512, (cg + 1) * 512)
                    po = por.next()
                    for kc in range(KC):
                        cx.op("pe", lambda e, kc=kc: e.matmul(po[:], mT[:, kc, tt * 128:(tt + 1) * 128], wo[:, kc, :], start=(kc == 0), stop=(kc == KC - 1)), reads=[wo, mbufs[tt // 4]], writes=[po])
                    t_ = tr.next()
                    cx.op("dve", lambda e: e.tensor_tensor(out=t_[:], in0=po[:], in1=gtb[:, gs], op=ALU.mult), reads=[po, gtb], writes=[t_])
                    cx.op("dve", lambda e: e.tensor_tensor(out=xt[:], in0=xt[:], in1=t_[:], op=ALU.add), reads=[xt, t_], writes=[xt])
                    cx.dma("sp", xres[b, tt * 128:(tt + 1) * 128, gs], xt[:], reads=[xt], writes=[db(("x", b, tt, cg)), db(("x", b, tt))])

                swpipe([dict(cg=cg, tt=tt) for cg in range(D // 512) for tt in range(NT)], [oA, lambda u: None, lambda u: None, oB])
        cx.barrier()

    def phase_ffn(l, b):
        TH = min(1024, S)
        NHF = S // TH
        NTBH = TH // 512
        NJ = DFF // 128
        FH = 4 if (NJ % 4 == 0 and NJ >= 8) else (2 if (NJ % 2 == 0 and NJ >= 4) else 1)
        JH = NJ // FH
        NFC = 2 * NJ
        JG = 3 if JH >= 3 else JH
        with contextlib.ExitStack() as st:
            hT = cx.sb(st, "fhT", [128, KC, TH], BF16)
            hbufs = [Buf(hT.t, f"fh{i}") for i in range(TH // 128)]
            gT = cx.sb(st, "fgT", [128, JH, TH], BF16)
            gbufs = [Buf(gT.t, f"fg{i}") for i in range(NTBH)]
            halo = cx.sb(st, "fhalo", [128, NFC, 2], F32)
            cx.op("dve", lambda e: e.memset(halo[:], 0.0), writes=[halo])
            gtb = cx.sb(st, "fgt", [128, D], F32)
            cx.dma("sp", gtb[:], modrow[b:b + 1, 5 * D:6 * D].broadcast_to([128, D]), reads=[db("modrow")], writes=[gtb])
            cx.op("dve", lambda e: e.tensor_scalar(out=gtb[:], in0=gtb[:], scalar1=1.0, scalar2=None, op0=ALU.add), reads=[gtb], writes=[gtb])
            wuv = w_up[l].rearrange("(kc p) n -> p kc n", p=128)
            wdv = w_down[l].rearrange("(j p) n -> p j n", p=128)
            for hf in range(NHF):
                with contextlib.ExitStack() as st2:
                    phase_norm(st2, xres, b, hf * (TH // 128), TH // 128, 1, hT, hbufs)
                cx.barrier()
                with contextlib.ExitStack() as st2:
                    wur = Ring([cx.sb(st2, f"fwu{i}", [128, KC, 2, JG * 128], BF16) for i in range(2)])
                    pur = Ring([cx.ps(st2, f"fpu{i}", [128, 512], F32) for i in range(4)])
                    ucr = Ring([cx.sb(st2, f"fuc{i}", [128, 514], F32) for i in range(4)])
                    t0r = Ring([cx.sb(st2, f"ft0{i}", [128, 512], F32) for i in range(2)])
                    t1r = Ring([cx.sb(st2, f"ft1{i}", [128, 512], F32) for i in range(2)])
                    t2r = Ring([cx.sb(st2, f"ft2{i}", [128, 512], F32) for i in range(4)])
                    sar = Ring([cx.sb(st2, f"fsa{i}", [128, 512], F32) for i in range(2)])
                    wdr = Ring([cx.sb(st2, f"fwd{i}", [128, JH, 512], BF16) for i in range(2)])
                    pdr = Ring([cx.ps(st2, f"fpd{i}", [128, 512], F32) for i in range(4)])
                    xr = Ring([cx.sb(st2, f"fx{i}", [128, 512], F32) for i in range(6)])
                    tr = Ring([cx.sb(st2, f"ftt{i}", [128, 512], F32) for i in range(3)])
                    for fh in range(FH):
                        prev_uc = {}
                        wus = {}

                        def fA(u, fh=fh, prev_uc=prev_uc, wus=wus):
                            jj, tb = u["jj"], u["tb"]
                            j = fh * JH + jj
                            if tb == 0 and jj % JG == 0:
                                ng = min(JG, JH - jj)
                                wu = wur.next()
                                cx.dma("pool", wu[:, :, 0, 0:ng * 128], wuv[:, :, j * 128:(j + ng) * 128], writes=[wu])
                                cx.dma("pool", wu[:, :, 1, 0:ng * 128], wuv[:, :, DFF + j * 128:DFF + (j + ng) * 128], writes=[wu])
                                wus[jj // JG] = wu
                            wu = wus[jj // JG]
                            jo = (jj % JG) * 128
                            cs = slice(tb * 512, (tb + 1) * 512)
                            res = []
                            for ab in range(2):
                                ch = ab * NJ + j
                                pu = pur.next()
                                for kc in range(KC):
                                    cx.op("pe", lambda e, kc=kc, ab=ab: e.matmul(pu[:], wu[:, kc, ab, jo:jo + 128], hT[:, kc, cs], start=(kc == 0), stop=(kc == KC - 1)), reads=[wu] + hbufs[tb * 4:(tb + 1) * 4], writes=[pu])
                                uc = ucr.next()
                                if tb == 0:
                                    cx.op("dve", lambda e, ch=ch: e.tensor_copy(out=uc[:, 0:2], in_=halo[:, ch, :]), reads=[halo], writes=[uc])
                                else:
                                    pu_ = prev_uc[(jj, ab)]
                                    cx.op("dve", lambda e, pu_=pu_: e.tensor_copy(out=uc[:, 0:2], in_=pu_[:, 512:514]), reads=[pu_], writes=[uc])
                                cx.op("act", lambda e: e.activation(out=uc[:, 2:514], in_=pu[:], func=AF.Copy), reads=[pu], writes=[uc])
                                if tb == NTBH - 1:
                                    cx.op("dve", lambda e, ch=ch: e.tensor_copy(out=halo[:, ch, :], in_=uc[:, 512:514]), reads=[uc], writes=[halo])
                                prev_uc[(jj, ab)] = uc
                                t0 = t0r.next()
                                cx.op("act", lambda e, ch=ch: e.activation(out=t0[:], in_=pu[:], func=AF.Identity, scale=cwT[:, 2, ch:ch + 1], bias=cwT[:, 3, ch:ch + 1]), reads=[pu, cwT], writes=[t0])
                                t1 = t1r.next()
                                cx.op("dve", lambda e, ch=ch: e.scalar_tensor_tensor(out=t1[:], in0=uc[:, 1:513], scalar=cwT[:, 1, ch:ch + 1], in1=t0[:], op0=ALU.mult, op1=ALU.add), reads=[uc, cwT, t0], writes=[t1])
                                t2 = t2r.next()
                                cx.op("dve", lambda e, ch=ch: e.scalar_tensor_tensor(out=t2[:], in0=uc[:, 0:512], scalar=cwT[:, 0, ch:ch + 1], in1=t1[:], op0=ALU.mult, op1=ALU.add), reads=[uc, cwT, t1], writes=[t2])
                                res.append(t2)
                            u["res"] = res

                        def fB(u):
                            jj, tb, res = u["jj"], u["tb"], u["res"]
                            cs = slice(tb * 512, (tb + 1) * 512)
                            sa = sar.next()
                            cx.op("act", lambda e: e.activation(out=sa[:], in_=res[0][:], func=AF.Silu), reads=[res[0]], writes=[sa])
                            cx.op("dve", lambda e: e.tensor_tensor(out=gT[:, jj, cs], in0=sa[:], in1=res[1][:], op=ALU.mult), reads=[sa, res[1]], writes=[gbufs[tb]])

                        swpipe([dict(jj=jj, tb=tb) for jj in range(JH) for tb in range(NTBH)], [fA, fB])
                        wds = {}

                        def dA(u, fh=fh, wds=wds):
                            cg, tl = u["cg"], u["tl"]
                            if tl == 0:
                                wd = wdr.next()
                                cx.dma("pool", wd[:], wdv[:, fh * JH:(fh + 1) * JH, cg * 512:(cg + 1) * 512], writes=[wd])
                                wds[cg] = wd
                            gs = slice(cg * 512, (cg + 1) * 512)
                            tt = hf * (TH // 128) + tl
                            xt = xr.next()
                            cx.dma("sp", xt[:], xres[b, tt * 128:(tt + 1) * 128, gs], reads=[db(("x", b, tt, cg)), db(("x", b, tt))], writes=[xt])
                            u["xt"] = xt

                        def dB(u, wds=wds):
                            cg, tl, xt = u["cg"], u["tl"], u["xt"]
                            wd = wds[cg]
                            gs = slice(cg * 512, (cg + 1) * 512)
                            tt = hf * (TH // 128) + tl
                            pd = pdr.next()
                            for jj in range(JH):
                                cx.op("pe", lambda e, jj=jj: e.matmul(pd[:], gT[:, jj, tl * 128:(tl + 1) * 128], wd[:, jj, :], start=(jj == 0), stop=(jj == JH - 1)), reads=[wd, gbufs[tl // 4]], writes=[pd])
                            t_ = tr.next()
                            cx.op("dve", lambda e: e.tensor_tensor(out=t_[:], in0=pd[:], in1=gtb[:, gs], op=ALU.mult), reads=[pd, gtb], writes=[t_])
                            cx.op("dve", lambda e: e.tensor_tensor(out=xt[:], in0=xt[:], in1=t_[:], op=ALU.add), reads=[xt, t_], writes=[xt])
                            cx.dma("sp", xres[b, tt * 128:(tt + 1) * 128, gs], xt[:], reads=[xt], writes=[db(("x", b, tt, cg)), db(("x", b, tt))])

                        swpipe([dict(cg=cg, tl=tl) for cg in range(D // 512) for tl in range(TH // 128)], [dA, lambda u: None, lambda u: None, dB])
                cx.barrier()
        cx.barrier()

    def phase_final(xsrc):
        with contextlib.ExitStack() as st:
            gfb = cx.sb(st, "gfb", [128, D], F32)
            cx.dma("sp", gfb[:], g_final.rearrange("(o d) -> o d", o=1).broadcast_to([128, D]), writes=[gfb])
            xr = Ring([cx.sb(st, f"zx{i}", [128, D], F32) for i in range(3)])
            jr = cx.sb(st, "zjunk", [128, D], BF16)
            ssr = Ring([cx.sb(st, f"zss{i}", [128, 4], F32) for i in range(2)])
            for b in range(NB):
                for tt in range(NT):
                    xt = xr.next()
                    cx.dma("sp", xt[:], xsrc[b, tt * 128:(tt + 1) * 128, :], reads=[db(("x", b, tt))], writes=[xt])
                    ss = ssr.next()
                    cx.op("act", lambda e: e.activation(out=jr[:], in_=xt[:], func=AF.Square, accum_out=ss[:, 0:1]), reads=[xt], writes=[jr, ss])
                    cx.op("act", lambda e: e.activation(out=ss[:, 1:2], in_=ss[:, 0:1], func=AF.Ln, scale=1.0 / D, bias=eps_t[:]), reads=[ss, eps_t], writes=[ss])
                    cx.op("act", lambda e: e.activation(out=ss[:, 2:3], in_=ss[:, 1:2], func=AF.Exp, scale=-0.5), reads=[ss], writes=[ss])
                    cx.op("dve", lambda e: e.scalar_tensor_tensor(out=xt[:], in0=xt[:], scalar=ss[:, 2:3], in1=gfb[:], op0=ALU.mult, op1=ALU.mult), reads=[xt, ss, gfb], writes=[xt])
                    cx.dma("sp", out[b, tt * 128:(tt + 1) * 128, :], xt[:], reads=[xt], writes=[db("out")])

    phases = phases or ("setup", "mod", "norm", "proj", "sb", "dsa", "out", "ffn", "final")
    cx.barrier()
    if "setup" in phases:
        phase_setup()
    for l in layers:
        if "mod" in phases:
            phase_mod(l)
        for b in range(NB):
            xsrc = x_in if l == 0 else xres
            with contextlib.ExitStack() as stA:
                hT = cx.sb(stA, "hT", [128, KC, S], BF16)
                hbufs = [Buf(hT.t, f"h{i}") for i in range(NT)]
                if "norm" in phases:
                    with contextlib.ExitStack() as st2:
                        phase_norm(st2, xsrc, b, 0, NT, 0, hT, hbufs)
                    cx.barrier()
                if "proj" in phases:
                    phase_proj(l, b, hT, hbufs)
            if "sb" in phases:
                phase_sb()
            if "dsa" in phases:
                phase_dsa()
            if "out" in phases:
                phase_out(l, b)
            if "ffn" in phases:
                phase_ffn(l, b)
    if "final" in phases:
        phase_final(xres)
    cx.finish()
    G.close()
    print("instructions emitted:", cx.ninst)
    return nc


def make_in_maps(cfg, inputs, n_cores):
    NB = cfg.NB
    cols = cfg.ext_cols()
    w_in_ext = np.ascontiguousarray(inputs["w_in"][:, :, cols])
    consts = make_consts(cfg)
    shared = {k: np.ascontiguousarray(inputs[k]) for k in
              ("w_a", "w_b", "w_o", "w_ada", "b_ada", "g_mix", "g_ffn", "w_up", "conv_w", "conv_b", "w_down", "g_final")}
    shared["w_in_ext"] = w_in_ext
    shared["consts"] = consts
    maps = []
    for i in range(n_cores):
        m = dict(shared)
        m["x"] = np.ascontiguousarray(inputs["x"][i * NB:(i + 1) * NB])
        m["c"] = np.ascontiguousarray(inputs["c"][i * NB:(i + 1) * NB])
        m["positions"] = np.ascontiguousarray(inputs["positions"][i * NB:(i + 1) * NB]).astype(np.int32)
        maps.append(m)
    return maps


_CACHE = {}


def kernel(**inputs):
    cfg = Cfg()
    n_cores = 8
    if "nc" not in _CACHE:
        _CACHE["nc"] = build(cfg)
    nc = _CACHE["nc"]
    maps = make_in_maps(cfg, inputs, n_cores)
    res = run_bass_kernel_spmd(nc, maps, core_ids=list(range(n_cores)))
    return np.concatenate([np.asarray(r["out"]) for r in res.results], axis=0).astype(np.float32)
```

```python
import contextlib
import numpy as np
import concourse.bass as bass
import concourse.mybir as mybir
from concourse.bass_utils import run_bass_kernel_spmd

F32 = mybir.dt.float32
BF16 = mybir.dt.bfloat16
I32 = mybir.dt.int32
AF = mybir.ActivationFunctionType
ALU = mybir.AluOpType
AX = mybir.AxisListType
BIG = 1.0e30
NIT = 24


class Cfg:
    def __init__(self, D=2048, S=2048, H=8, IH=16, DFF=5632, L=4, TOPK=256, NB=2):
        self.D, self.S, self.H, self.IH, self.DFF, self.L, self.NB = D, S, H, IH, DFF, L, NB
        self.TOPK = min(TOPK, S // 4)
        self.KC = D // 128
        self.NT = S // 128
        self.NTB = S // 512
        self.HW = H * 128
        self.EPS = 1e-6
        self.THETA = 10000.0
        fm = []
        for h in range(H):
            fm.append(("qa", h, None))
        for h in range(H):
            fm.append(("ka", h, None))
        for h in range(H):
            fm.append(("qb", h, "r128"))
        for h in range(H):
            fm.append(("kb", h, "r128"))
        for j in range(IH // 2):
            fm.append(("qi", j, "r64"))
        fm.append(("ki", 0, "r64"))
        for j in range(self.KC):
            fm.append(("ga", j, None))
        for j in range(self.KC):
            fm.append(("gb", j, None))
        self.fm = fm
        self.NFM = len(fm) * 128
        self.NTM = 2 * self.HW + IH
        self.NEXT = self.NFM + self.NTM
        o = {}
        off = 0
        for name, sz in (("qa", self.HW), ("ka", self.HW), ("va", self.HW), ("qb", self.HW),
                         ("kb", self.HW), ("vb", self.HW), ("qi", IH * 64), ("ki", 64),
                         ("wi", IH), ("ga", D), ("gb", D)):
            o[name] = off
            off += sz
        self.off = o
        self.IN_WIDTH = off

    def ext_cols(self):
        idx = []
        sw128 = np.concatenate([np.arange(64, 128), np.arange(0, 64)])
        sw64 = np.concatenate([np.arange(32, 64), np.arange(0, 32), np.arange(96, 128), np.arange(64, 96)])
        for kind, j, mode in self.fm:
            if kind == "ki":
                base = self.off["ki"] + np.concatenate([np.arange(64), np.arange(64)])
            else:
                base = self.off[kind] + j * 128 + np.arange(128)
            if mode == "sw":
                base = base[sw128] if kind in ("qb", "kb") else base[sw64]
            idx.append(base)
        idx.append(self.off["va"] + np.arange(self.HW))
        idx.append(self.off["vb"] + np.arange(self.HW))
        idx.append(self.off["wi"] + np.arange(self.IH))
        return np.concatenate(idx)


C_INVF128, C_INVF64, C_SGN128, C_SGN64 = 0, 1, 2, 3
C_IDENT = 8
C_TRIU = C_IDENT + 128
C_MSB = C_TRIU + 128
C_MDSA = C_MSB + 2048
C_NEGTRI = C_MDSA + 128
C_POW2 = C_NEGTRI + 128
C_SEL = C_POW2 + 32
NCONST = C_SEL + 256


def make_consts(cfg):
    c = np.zeros((128, NCONST), np.float32)
    p = np.arange(128)
    c[:, C_INVF128] = 1.0 / (cfg.THETA ** (np.arange(0, 128, 2, dtype=np.float32) / 128.0))[p % 64]
    c[:, C_INVF64] = 1.0 / (cfg.THETA ** (np.arange(0, 64, 2, dtype=np.float32) / 64.0))[p % 32]
    c[:, C_SGN128] = np.where(p < 64, -1.0, 1.0)
    c[:, C_SGN64] = np.where((p % 64) < 32, -1.0, 1.0)
    f = np.arange(128)
    c[:, C_IDENT:C_IDENT + 128] = (p[:, None] == f[None, :])
    c[:, C_TRIU:C_TRIU + 128] = (p[:, None] > f[None, :])
    f5 = np.arange(512)
    for r in range(4):
        c[:, C_MSB + 512 * r:C_MSB + 512 * (r + 1)] = ((128 * r + p[:, None]) < f5[None, :])
    c[:, C_MDSA:C_MDSA + 128] = (p[:, None] <= f[None, :])
    c[:, C_NEGTRI:C_NEGTRI + 128] = np.where(f[None, :] > p[:, None], -BIG, 0.0)
    c[:, C_POW2:C_POW2 + NIT + 1] = (2.0 ** -(np.arange(NIT + 1) + 1.0))[None, :]
    c[0, C_SEL:C_SEL + 128] = 1.0
    c[1, C_SEL + 128:C_SEL + 256] = 1.0
    return c


class Buf:
    __slots__ = ("t", "w", "r", "name")

    def __init__(self, t=None, name=""):
        self.t = t
        self.w = {}
        self.r = {}
        self.name = name

    def __getitem__(self, idx):
        return self.t[idx]


class Ctx:
    ENG = ("pe", "act", "dve", "pool", "sp")

    def __init__(self, nc):
        self.nc = nc
        self.es = contextlib.ExitStack()
        self.h = {"pe": nc.tensor, "act": nc.scalar, "dve": nc.vector,
                  "pool": nc.gpsimd, "sp": nc.sync}
        self.sem, self.cnt = {}, {}
        self.seen = {e: {} for e in self.ENG}
        for e in self.ENG:
            self.sem[e] = self.es.enter_context(nc.semaphore("s_" + e))
            self.cnt[e] = 0
        self.dq = {}
        self.dqi = {}
        for q, n in (("sp", 24), ("pool", 8)):
            self.dq[q] = []
            self.dqi[q] = 0
            for j in range(n):
                k = f"d_{q}{j}"
                self.sem[k] = self.es.enter_context(nc.semaphore("s_" + k))
                self.cnt[k] = 0
                self.dq[q].append(k)
        self.ninst = 0
        self.uid = 0

    def sb(self, stack, name, shape, dt):
        self.uid += 1
        return Buf(stack.enter_context(self.nc.sbuf_tensor(f"{name}_{self.uid}", shape, dt)), name)

    def ps(self, stack, name, shape, dt=F32):
        self.uid += 1
        return Buf(stack.enter_context(self.nc.psum_tensor(f"{name}_{self.uid}", shape, dt)), name)

    def _wait(self, eng, deps):
        seen = self.seen[eng]
        for k, v in deps.items():
            if seen.get(k, 0) < v:
                self.h[eng].wait_ge(self.sem[k], v)
                seen[k] = v
                self.ninst += 1

    def _deps(self, eng, reads, writes):
        deps = {}

        def add(d, skip_same):
            for k, v in d.items():
                if skip_same and k == eng:
                    continue
                if deps.get(k, 0) < v:
                    deps[k] = v
        for b in reads:
            add(b.w, False)
        for b in writes:
            add(b.w, True)
            add(b.r, True)
        return deps

    def op(self, eng, fn, reads=(), writes=()):
        self._wait(eng, self._deps(eng, reads, writes))
        ins = fn(self.h[eng])
        self.cnt[eng] += 1
        ins.then_inc(self.sem[eng], 1)
        self.ninst += 1
        c = self.cnt[eng]
        for b in writes:
            b.w[eng] = c
            b.r = {}
        for b in reads:
            b.r[eng] = c
        return ins

    def dma(self, q, out_ap, in_ap, reads=(), writes=(), **kw):
        k = self.dq[q][self.dqi[q] % len(self.dq[q])]
        self.dqi[q] += 1
        deps = self._deps("dma", reads, writes)
        if self.cnt[k] > 0:
            deps[k] = max(deps.get(k, 0), self.cnt[k])
        self._wait(q, deps)
        ins = self.h[q].dma_start(out=out_ap, in_=in_ap, **kw)
        self.cnt[k] += 16
        ins.then_inc(self.sem[k], 16)
        self.ninst += 1
        c = self.cnt[k]
        for b in writes:
            b.w[k] = c
            b.r = {}
        for b in reads:
            b.r[k] = c

    def barrier(self):
        allk = dict(self.cnt)
        for e in self.ENG:
            self._wait(e, {k: v for k, v in allk.items() if k != e and v > 0})

    def finish(self):
        self.barrier()
        self.es.close()


class Ring:
    def __init__(self, bufs):
        self.b = bufs
        self.i = 0

    def next(self):
        b = self.b[self.i % len(self.b)]
        self.i += 1
        return b


def swpipe(items, stages):
    n, ns = len(items), len(stages)
    for it in range(n + ns - 1):
        for si, st in enumerate(stages):
            k = it - si
            if 0 <= k < n:
                st(items[k])


def build(cfg, layers=None, phases=None, debug_out=()):
    D, S, H, IH, KC, NT, NTB, HW, DFF, NB = cfg.D, cfg.S, cfg.H, cfg.IH, cfg.KC, cfg.NT, cfg.NTB, cfg.HW, cfg.DFF, cfg.NB
    L = cfg.L
    layers = list(range(L)) if layers is None else layers
    nc = bass.Bass("TRN2", target_bir_lowering=False)
    cx = Ctx(nc)

    def din(name, shape, dt=F32):
        return nc.dram_tensor(name, shape, dt, kind="ExternalInput").ap()

    def dscr(name, shape, dt):
        kind = "ExternalOutput" if name in debug_out else "Internal"
        return nc.dram_tensor(name, shape, dt, kind=kind).ap()

    x_in = din("x", [NB, S, D])
    c_in = din("c", [NB, D])
    pos_in = din("positions", [NB, S], I32)
    w_in = din("w_in_ext", [L, D, cfg.NEXT])
    w_a = din("w_a", [L, HW, D])
    w_b = din("w_b", [L, HW, D])
    w_o = din("w_o", [L, D, D])
    w_ada = din("w_ada", [L, D, 6 * D])
    b_ada = din("b_ada", [L, 6 * D])
    g_mix = din("g_mix", [L, D])
    g_ffn = din("g_ffn", [L, D])
    w_up = din("w_up", [L, D, 2 * DFF])
    conv_w = din("conv_w", [L, 3, 2 * DFF])
    conv_b = din("conv_b", [L, 2 * DFF])
    w_down = din("w_down", [L, DFF, D])
    g_final = din("g_final", [D])
    consts = din("consts", [128, NCONST])
    out = nc.dram_tensor("out", [NB, S, D], F32, kind="ExternalOutput").ap()

    xres = dscr("xres", [NB, S, D], F32)
    modrow = dscr("modrow", [NB, 6 * D], F32)
    ropeT = dscr("ropeT", [NB, 4, 128, S], F32)
    qaT = dscr("qaT", [H, 128, S], BF16)
    kaT = dscr("kaT", [H, 128, S], BF16)
    qbT = dscr("qbT", [H, 128, S], BF16)
    kbT = dscr("kbT", [H, 128, S], BF16)
    qiT = dscr("qiT", [IH // 2, 128, S], BF16)
    kiT = dscr("kiT", [1, 128, S], BF16)
    sgaT = dscr("sgaT", [KC, 128, S], BF16)
    sgbT = dscr("sgbT", [KC, 128, S], BF16)
    vab = dscr("vab", [S, 2 * HW], BF16)
    wis = dscr("wis", [S, IH], F32)
    yabT = dscr("yabT", [2, H, 128, S], BF16)
    fm_dst = {"qa": qaT, "ka": kaT, "qb": qbT, "kb": kbT, "qi": qiT, "ki": kiT, "ga": sgaT, "gb": sgbT}

    dbufs = {}

    def db(key):
        if key not in dbufs:
            dbufs[key] = Buf(None, str(key))
        return dbufs[key]

    G = contextlib.ExitStack()
    cst = cx.sb(G, "cst", [128, NCONST], F32)
    cx.dma("sp", cst[:], consts, writes=[cst])
    ident_bf = cx.sb(G, "ident_bf", [128, 128], BF16)
    ident_f = Buf(cst.t, "identf")
    triU_bf = cx.sb(G, "triU_bf", [128, 128], BF16)
    ones_bf = cx.sb(G, "ones_bf", [128, 128], BF16)
    msb_bf = cx.sb(G, "msb_bf", [128, 2048], BF16)
    mdsa_bf = cx.sb(G, "mdsa_bf", [128, 128], BF16)
    cx.op("dve", lambda e: e.tensor_copy(out=ident_bf[:], in_=cst[:, C_IDENT:C_IDENT + 128]), reads=[cst], writes=[ident_bf])
    cx.op("dve", lambda e: e.tensor_copy(out=triU_bf[:], in_=cst[:, C_TRIU:C_TRIU + 128]), reads=[cst], writes=[triU_bf])
    cx.op("dve", lambda e: e.memset(ones_bf[:], 1.0), writes=[ones_bf])
    cx.op("dve", lambda e: e.tensor_copy(out=msb_bf[:], in_=cst[:, C_MSB:C_MSB + 2048]), reads=[cst], writes=[msb_bf])
    cx.op("dve", lambda e: e.tensor_copy(out=mdsa_bf[:], in_=cst[:, C_MDSA:C_MDSA + 128]), reads=[cst], writes=[mdsa_bf])
    cactT = cx.sb(G, "cactT", [128, KC, NB], F32)
    vecs = cx.sb(G, "vecs", [128, NB, 6, KC], F32)
    cwT = cx.sb(G, "cwT", [128, 4, 2 * DFF // 128], F32)
    eps_t = cx.sb(G, "eps_t", [128, 1], F32)
    cx.op("dve", lambda e: e.memset(eps_t[:], cfg.EPS), writes=[eps_t])
    negpi = cx.sb(G, "negpi", [128, 1], F32)
    cx.op("dve", lambda e: e.memset(negpi[:], -float(np.pi)), writes=[negpi])
    one_t = cx.sb(G, "one_t", [128, 1], F32)
    cx.op("dve", lambda e: e.memset(one_t[:], 1.0), writes=[one_t])

    def phase_setup():
        with contextlib.ExitStack() as st:
            crow = cx.sb(st, "crow", [NB, D], F32)
            cx.dma("sp", crow[:], c_in, writes=[crow])
            crs = cx.sb(st, "crs", [NB, D], F32)
            cx.op("act", lambda e: e.activation(out=crs[:], in_=crow[:], func=AF.Silu), reads=[crow], writes=[crs])
            pt = cx.ps(st, "pt", [128, KC, NB], F32)
            for kc in range(KC):
                cx.op("pe", lambda e, kc=kc: e.transpose(out=pt[:, kc, :], in_=crs[:, kc * 128:(kc + 1) * 128], identity=ident_f[0:NB, C_IDENT:C_IDENT + NB]), reads=[crs, cst], writes=[pt])
            cx.op("dve", lambda e: e.tensor_copy(out=cactT[:], in_=pt[:]), reads=[pt], writes=[cactT])
            posi = cx.sb(st, "posi", [128, S], I32)
            posf = cx.sb(st, "posf", [128, S], F32)
            ang = cx.sb(st, "ang", [128, S], F32)
            a2 = cx.sb(st, "a2", [128, S], F32)
            ki_ = cx.sb(st, "ki_", [128, S], I32)
            kf = cx.sb(st, "kf", [128, S], F32)
            m = cx.sb(st, "m", [128, S], F32)
            tbr = Ring([cx.sb(st, f"tb_{i}", [128, S], F32) for i in range(2)])
            for b in range(NB):
                cx.dma("sp", posi[:], pos_in[b:b + 1, :].broadcast_to([128, S]), writes=[posi])
                cx.op("dve", lambda e: e.tensor_copy(out=posf[:], in_=posi[:]), reads=[posi], writes=[posf])
                for ti, (ccol, scol) in enumerate(((C_INVF128, C_SGN128), (C_INVF64, C_SGN64))):
                    cx.op("dve", lambda e: e.tensor_scalar(out=ang[:], in0=posf[:], scalar1=cst[:, ccol:ccol + 1], scalar2=None, op0=ALU.mult), reads=[posf, cst], writes=[ang])
                    for which in range(2):
                        shift = float(np.pi / 2) if which == 0 else 0.0
                        cx.op("dve", lambda e: e.tensor_scalar(out=ki_[:], in0=ang[:], scalar1=shift, scalar2=float(1.0 / (2 * np.pi)), op0=ALU.add, op1=ALU.mult), reads=[ang], writes=[ki_])
                        cx.op("dve", lambda e: e.tensor_copy(out=kf[:], in_=ki_[:]), reads=[ki_], writes=[kf])
                        cx.op("dve", lambda e: e.scalar_tensor_tensor(out=a2[:], in0=kf[:], scalar=-float(2 * np.pi), in1=ang[:], op0=ALU.mult, op1=ALU.add), reads=[kf, ang], writes=[a2])
                        if which == 0:
                            cx.op("dve", lambda e: e.tensor_scalar(out=a2[:], in0=a2[:], scalar1=shift, scalar2=None, op0=ALU.add), reads=[a2], writes=[a2])
                        cx.op("dve", lambda e: e.tensor_scalar(out=m[:], in0=a2[:], scalar1=float(np.pi), scalar2=-float(2 * np.pi), op0=ALU.is_gt, op1=ALU.mult), reads=[a2], writes=[m])
                        cx.op("dve", lambda e: e.tensor_tensor(out=a2[:], in0=a2[:], in1=m[:], op=ALU.add), reads=[a2, m], writes=[a2])
                        cx.op("dve", lambda e: e.tensor_scalar(out=m[:], in0=a2[:], scalar1=-float(np.pi), scalar2=float(2 * np.pi), op0=ALU.is_lt, op1=ALU.mult), reads=[a2], writes=[m])
                        cx.op("dve", lambda e: e.tensor_tensor(out=a2[:], in0=a2[:], in1=m[:], op=ALU.add), reads=[a2, m], writes=[a2])
                        cx.op("dve", lambda e: e.tensor_scalar(out=a2[:], in0=a2[:], scalar1=-3.1415925, scalar2=3.1415925, op0=ALU.max, op1=ALU.min), reads=[a2], writes=[a2])
                        tb_ = tbr.next()
                        cx.op("act", lambda e: e.activation(out=tb_[:], in_=a2[:], func=AF.Sin), reads=[a2], writes=[tb_])
                        if which == 1:
                            cx.op("dve", lambda e: e.tensor_scalar(out=tb_[:], in0=tb_[:], scalar1=cst[:, scol:scol + 1], scalar2=None, op0=ALU.mult), reads=[tb_, cst], writes=[tb_])
                        cx.dma("sp", ropeT[b, 2 * ti + which], tb_[:], reads=[tb_], writes=[db("ropeT")])
        cx.barrier()

    def load_fm_vec(st, dst_ap_fn, src_rows_ap, nrows, pst, tmp, reads_extra=()):
        cx.dma("sp", tmp[0:nrows, :], src_rows_ap, writes=[tmp], reads=list(reads_extra))
        cx.op("pe", lambda e: e.transpose(out=pst[:, 0:nrows], in_=tmp[0:nrows, :], identity=ident_f[0:nrows, C_IDENT:C_IDENT + nrows]), reads=[tmp, cst], writes=[pst])
        dst_ap_fn(pst)

    def phase_mod(l):
        NCH = 6 * D // 512
        with contextlib.ExitStack() as st:
            wr = Ring([cx.sb(st, f"wada{i}", [128, KC, 512], BF16) for i in range(2)])
            cab = cx.sb(st, "cab", [128, KC, NB], BF16)
            cx.op("dve", lambda e: e.tensor_copy(out=cab[:], in_=cactT[:]), reads=[cactT], writes=[cab])
            pr = Ring([cx.ps(st, f"pmod{i}", [NB, 512], F32) for i in range(2)])
            br = Ring([cx.sb(st, f"bada{i}", [NB, 512], F32) for i in range(2)])
            sr = Ring([cx.sb(st, f"smod{i}", [NB, 512], F32) for i in range(2)])
            wv = w_ada[l].rearrange("(kc p) n -> p kc n", p=128)
            for ch in range(NCH):
                wt = wr.next()
                cx.dma("pool", wt[:], wv[:, :, ch * 512:(ch + 1) * 512], writes=[wt])
                bt = br.next()
                cx.dma("sp", bt[:], b_ada[l:l + 1, ch * 512:(ch + 1) * 512].broadcast_to([NB, 512]), writes=[bt])
                ps = pr.next()
                for kc in range(KC):
                    cx.op("pe", lambda e, kc=kc: e.matmul(ps[:], cab[:, kc, :], wt[:, kc, :], start=(kc == 0), stop=(kc == KC - 1)), reads=[cab, wt], writes=[ps])
                sm = sr.next()
                cx.op("dve", lambda e: e.tensor_tensor(out=sm[:], in0=ps[:], in1=bt[:], op=ALU.add), reads=[ps, bt], writes=[sm])
                cx.dma("sp", modrow[:, ch * 512:(ch + 1) * 512], sm[:], reads=[sm], writes=[db("modrow")])
            tmp = cx.sb(st, "fmtmp", [128, 128], F32)
            pst = cx.ps(st, "fmps", [128, 128], F32)
            gm = cx.sb(st, "gm", [128, 2, KC], F32)
            for i, gsrc in enumerate((g_mix, g_ffn)):
                load_fm_vec(st, lambda p, i=i: cx.op("dve", lambda e: e.tensor_copy(out=gm[:, i, :], in_=p[:, 0:KC]), reads=[p], writes=[gm]),
                            gsrc[l].rearrange("(j p) -> j p", p=128), KC, pst, tmp)
            mt = cx.sb(st, "mt", [128, 6 * KC], F32)
            for b in range(NB):
                load_fm_vec(st, lambda p: cx.op("dve", lambda e: e.tensor_copy(out=mt[:], in_=p[:, 0:6 * KC]), reads=[p], writes=[mt]),
                            modrow[b].rearrange("(j p) -> j p", p=128), 6 * KC, pst, tmp, reads_extra=[db("modrow")])
                for s_, (ish, isc) in enumerate(((0, 1), (3, 4))):
                    cx.op("dve", lambda e, s_=s_, isc=isc: e.scalar_tensor_tensor(out=vecs[:, b, 3 * s_, :], in0=mt[:, isc * KC:(isc + 1) * KC], scalar=1.0, in1=gm[:, s_, :], op0=ALU.add, op1=ALU.mult), reads=[mt, gm], writes=[vecs])
                    cx.op("dve", lambda e, s_=s_, ish=ish: e.tensor_copy(out=vecs[:, b, 3 * s_ + 1, :], in_=mt[:, ish * KC:(ish + 1) * KC]), reads=[mt], writes=[vecs])
            NFC = 2 * DFF // 128
            for i in range(4):
                src = conv_w[l, i] if i < 3 else conv_b[l]
                load_fm_vec(st, lambda p, i=i: cx.op("dve", lambda e: e.tensor_copy(out=cwT[:, i, :], in_=p[:, 0:NFC]), reads=[p], writes=[cwT]),
                            src.rearrange("(j p) -> j p", p=128), NFC, pst, tmp)
        cx.barrier()

    def phase_norm(st, xsrc, b, t0, ntiles, which, hT, hbufs, nps=2):
        xr = Ring([cx.sb(st, f"nx{i}", [128, D], F32) for i in range(2)])
        jr = cx.sb(st, "njunk", [128, D], BF16)
        xnr = Ring([cx.sb(st, f"nxn{i}", [128, D], BF16) for i in range(2)])
        ssr = Ring([cx.sb(st, f"nss{i}", [128, 4], F32) for i in range(2)])
        NPB = max(1, (KC * 128 * 2) // 2048)
        ptr = Ring([cx.ps(st, f"npt{i}", [128, KC, 128], BF16) for i in range(nps)])
        for ti in range(ntiles):
            tt = t0 + ti
            xt = xr.next()
            cx.dma("sp", xt[:], xsrc[b, tt * 128:(tt + 1) * 128, :], reads=[db(("x", b, tt))], writes=[xt])
            ss = ssr.next()
            cx.op("act", lambda e: e.activation(out=jr[:], in_=xt[:], func=AF.Square, accum_out=ss[:, 0:1]), reads=[xt], writes=[jr, ss])
            cx.op("act", lambda e: e.activation(out=ss[:, 1:2], in_=ss[:, 0:1], func=AF.Ln, scale=1.0 / D, bias=eps_t[:]), reads=[ss, eps_t], writes=[ss])
            cx.op("act", lambda e: e.activation(out=ss[:, 2:3], in_=ss[:, 1:2], func=AF.Exp, scale=-0.5), reads=[ss], writes=[ss])
            xn = xnr.next()
            cx.op("dve", lambda e: e.tensor_scalar(out=xn[:], in0=xt[:], scalar1=ss[:, 2:3], scalar2=None, op0=ALU.mult), reads=[xt, ss], writes=[xn])
            pt = ptr.next()
            for kc in range(KC):
                cx.op("pe", lambda e, kc=kc: e.transpose(out=pt[:, kc, :], in_=xn[:, kc * 128:(kc + 1) * 128], identity=ident_bf[:]), reads=[xn, ident_bf], writes=[pt])
            hb = hbufs[ti]
            eng = "dve" if ti % 2 == 0 else "act"
            for kc in range(KC):
                o_ = hT[:, kc, ti * 128:(ti + 1) * 128]
                a_ = vecs[:, b, 3 * which, kc:kc + 1]
                b_ = vecs[:, b, 3 * which + 1, kc:kc + 1]
                if eng == "dve":
                    cx.op("dve", lambda e, o_=o_, a_=a_, b_=b_, kc=kc: e.tensor_scalar(out=o_, in0=pt[:, kc, :], scalar1=a_, scalar2=b_, op0=ALU.mult, op1=ALU.add), reads=[pt, vecs], writes=[hb])
                else:
                    cx.op("act", lambda e, o_=o_, a_=a_, b_=b_, kc=kc: e.activation(out=o_, in_=pt[:, kc, :], func=AF.Identity, scale=a_, bias=b_), reads=[pt, vecs], writes=[hb])

    def phase_proj(l, b, hT, hbufs):
        with contextlib.ExitStack() as st:
            rope = cx.sb(st, "rope", [128, 4, S], F32)
            cx.dma("sp", rope[:], ropeT[b].rearrange("f p s -> p f s"), reads=[db("ropeT")], writes=[rope])
            wr = Ring([cx.sb(st, f"pw{i}", [128, KC, 512], BF16) for i in range(2)])
            pr = Ring([cx.ps(st, f"pp{i}", [128, 512], F32) for i in range(6)])
            sr = Ring([cx.sb(st, f"pstg{i}", [128, 512], BF16) for i in range(4)])
            t1r = Ring([cx.sb(st, f"pt1{i}", [128, 512], F32) for i in range(2)])
            t2r = Ring([cx.sb(st, f"pt2{i}", [128, 512], F32) for i in range(2)])
            wv = w_in[l].rearrange("(kc p) n -> p kc n", p=128)
            nfm = len(cfg.fm)
            ci = 0
            while ci < nfm:
                ncw = min(4, nfm - ci)
                wt = wr.next()
                cx.dma("pool", wt[:, :, 0:ncw * 128], wv[:, :, ci * 128:(ci + ncw) * 128], writes=[wt])
                for tb in range(NTB):
                    hb = hbufs[tb * 4:(tb + 1) * 4]
                    cs = slice(tb * 512, (tb + 1) * 512)
                    u = 0
                    while u < ncw:
                        kind, j, mode = cfg.fm[ci + u]
                        dst = fm_dst[kind]
                        if mode is None:
                            ps = pr.next()
                            for kc in range(KC):
                                cx.op("pe", lambda e, kc=kc, u=u: e.matmul(ps[:], wt[:, kc, u * 128:(u + 1) * 128], hT[:, kc, cs], start=(kc == 0), stop=(kc == KC - 1)), reads=[wt] + hb, writes=[ps])
                            sg = sr.next()
                            if kind in ("ga", "gb"):
                                cx.op("act", lambda e: e.activation(out=sg[:], in_=ps[:], func=AF.Sigmoid), reads=[ps], writes=[sg])
                            else:
                                cx.op("act", lambda e: e.activation(out=sg[:], in_=ps[:], func=AF.Copy), reads=[ps], writes=[sg])
                            cx.dma("sp", dst[j, :, cs], sg[:], reads=[sg], writes=[db(kind)])
                            u += 1
                        else:
                            ps = pr.next()
                            for kc in range(KC):
                                cx.op("pe", lambda e, kc=kc, u=u: e.matmul(ps[:], wt[:, kc, u * 128:(u + 1) * 128], hT[:, kc, cs], start=(kc == 0), stop=(kc == KC - 1)), reads=[wt] + hb, writes=[ps])
                            f0 = 0 if mode == "r128" else 2
                            hw_ = 64 if mode == "r128" else 32
                            t1, t2 = t1r.next(), t2r.next()
                            cx.op("dve", lambda e: e.tensor_tensor(out=t1[:], in0=ps[:], in1=rope[:, f0, cs], op=ALU.mult), reads=[ps, rope], writes=[t1])
                            for p0 in range(0, 128, 2 * hw_):
                                lo, hi = slice(p0, p0 + hw_), slice(p0 + hw_, p0 + 2 * hw_)
                                cx.op("dve", lambda e, lo=lo, hi=hi: e.tensor_tensor(out=t2[lo, :], in0=ps[hi, :], in1=rope[lo, f0 + 1, cs], op=ALU.mult), reads=[ps, rope], writes=[t2])
                                cx.op("dve", lambda e, lo=lo, hi=hi: e.tensor_tensor(out=t2[hi, :], in0=ps[lo, :], in1=rope[hi, f0 + 1, cs], op=ALU.mult), reads=[ps, rope], writes=[t2])
                            sg = sr.next()
                            cx.op("dve", lambda e: e.tensor_tensor(out=sg[:], in0=t1[:], in1=t2[:], op=ALU.add), reads=[t1, t2], writes=[sg])
                            cx.dma("sp", dst[j, :, cs], sg[:], reads=[sg], writes=[db(kind)])
                            u += 1
                ci += ncw
            c0 = cfg.NFM
            ntm = 2 * HW
            cc = 0
            while cc < ntm:
                wt = wr.next()
                cx.dma("pool", wt[:], wv[:, :, c0 + cc:c0 + cc + 512], writes=[wt])
                for tt in range(NT):
                    ps = pr.next()
                    for kc in range(KC):
                        cx.op("pe", lambda e, kc=kc: e.matmul(ps[:], hT[:, kc, tt * 128:(tt + 1) * 128], wt[:, kc, :], start=(kc == 0), stop=(kc == KC - 1)), reads=[wt, hbufs[tt]], writes=[ps])
                    sg = sr.next()
                    cx.op("act", lambda e: e.activation(out=sg[:], in_=ps[:], func=AF.Copy), reads=[ps], writes=[sg])
                    cx.dma("sp", vab[tt * 128:(tt + 1) * 128, cc:cc + 512], sg[:], reads=[sg], writes=[db("vab")])
                cc += 512
            wt = wr.next()
            cx.dma("pool", wt[:, :, 0:IH], wv[:, :, c0 + ntm:c0 + ntm + IH], writes=[wt])
            wst = Ring([cx.sb(st, f"pwi{i}", [128, IH], F32) for i in range(2)])
            for tt in range(NT):
                ps = pr.next()
                for kc in range(KC):
                    cx.op("pe", lambda e, kc=kc: e.matmul(ps[:, 0:IH], hT[:, kc, tt * 128:(tt + 1) * 128], wt[:, kc, 0:IH], start=(kc == 0), stop=(kc == KC - 1)), reads=[wt, hbufs[tt]], writes=[ps])
                ws = wst.next()
                cx.op("dve", lambda e: e.tensor_copy(out=ws[:], in_=ps[:, 0:IH]), reads=[ps], writes=[ws])
                cx.dma("sp", wis[tt * 128:(tt + 1) * 128, :], ws[:], reads=[ws], writes=[db("wis")])
        cx.barrier()

    def phase_sb():
        scale = 128.0 ** -0.5
        with contextlib.ExitStack() as st:
            qr = Ring([cx.sb(st, f"sq{i}", [128, S], BF16) for i in range(2)])
            kr = Ring([cx.sb(st, f"sk{i}", [128, S], BF16) for i in range(2)])
            vr = Ring([cx.sb(st, f"sv{i}", [128, NT, 128], BF16) for i in range(2)])
            zr = Ring([cx.ps(st, f"sz{i}", [128, 512], F32) for i in range(3)])
            ar = Ring([cx.ps(st, f"sa{i}", [128, 512], F32) for i in range(3)])
            yr = Ring([cx.ps(st, f"sy{i}", [128, 512], F32) for i in range(2)])
            er = Ring([cx.sb(st, f"se{i}", [128, 512], F32) for i in range(3)])
            spr = Ring([cx.sb(st, f"ssp{i}", [128, 512], F32) for i in range(6)])
            spbr = Ring([cx.sb(st, f"sspb{i}", [128, 512], BF16) for i in range(6)])
            lsr = Ring([cx.sb(st, f"sls{i}", [128, 512], F32) for i in range(6)])
            lbr = Ring([cx.sb(st, f"slb{i}", [128, 512], BF16) for i in range(6)])
            lgr = Ring([cx.sb(st, f"slg{i}", [128, 512], F32) for i in range(6)])
            agr = Ring([cx.sb(st, f"sag{i}", [128, 512], F32) for i in range(4)])
            wtr = Ring([cx.sb(st, f"swt{i}", [128, 512], BF16) for i in range(6)])
            ysr = Ring([cx.sb(st, f"sys{i}", [128, 512], BF16) for i in range(2)])
            items = []
            for h in range(H):
                for g in range(NTB):
                    nblk = 4 * g + 4
                    for bi, sbk in enumerate(range(nblk - 1, -1, -1)):
                        items.append(dict(h=h, g=g, bi=bi, sbk=sbk, nblk=nblk))
            hs, gs = {}, {}

            def stA(it):
                h, g, bi, sbk = it["h"], it["g"], it["bi"], it["sbk"]
                cs = slice(g * 512, (g + 1) * 512)
                if g == 0 and bi == 0:
                    q, k, v = qr.next(), kr.next(), vr.next()
                    cx.dma("sp", q[:], qaT[h], reads=[db("qa")], writes=[q])
                    cx.dma("sp", k[:], kaT[h], reads=[db("ka")], writes=[k])
                    cx.dma("sp", v[:], vab[:, h * 128:(h + 1) * 128].rearrange("(st p) d -> p st d", p=128), reads=[db("vab")], writes=[v])
                    hs[h] = (q, k, v)
                q, k, v = hs[h]
                if bi == 0:
                    gs[(h, g)] = dict(yp=yr.next(), lsum=None)
                G_ = gs[(h, g)]
                r = sbk - 4 * g
                it["r"] = r
                zp = zr.next()
                cx.op("pe", lambda e: e.matmul(zp[:], k[:, sbk * 128:(sbk + 1) * 128], q[:, cs], start=True, stop=True), reads=[k, q], writes=[zp])
                e_ = er.next()
                cx.op("act", lambda e: e.activation(out=e_[:], in_=zp[:], func=AF.Exp, scale=scale), reads=[zp], writes=[e_])
                sp = spr.next()
                cx.op("act", lambda e: e.activation(out=sp[:], in_=e_[:], func=AF.Ln, bias=one_t[:]), reads=[e_, one_t], writes=[sp])
                lg = lgr.next()
                cx.op("dve", lambda e: e.scalar_tensor_tensor(out=lg[:], in0=zp[:], scalar=scale, in1=sp[:], op0=ALU.mult, op1=ALU.subtract), reads=[zp, sp], writes=[lg])
                it["lg"] = lg
                spb = spbr.next()
                if r >= 0:
                    cx.op("dve", lambda e: e.tensor_tensor(out=spb[:], in0=sp[:], in1=msb_bf[:, r * 512:(r + 1) * 512], op=ALU.mult), reads=[sp, msb_bf], writes=[spb])
                else:
                    cx.op("dve", lambda e: e.tensor_copy(out=spb[:], in_=sp[:]), reads=[sp], writes=[spb])
                it["spb"] = spb
                lsum_prev = G_["lsum"]
                it["lb"] = None
                if lsum_prev is not None:
                    lb = lbr.next()
                    if bi % 2 == 0:
                        cx.op("act", lambda e: e.activation(out=lb[:], in_=lsum_prev[:], func=AF.Copy), reads=[lsum_prev], writes=[lb])
                    else:
                        cx.op("pool", lambda e: e.tensor_copy(out=lb[:], in_=lsum_prev[:]), reads=[lsum_prev], writes=[lb])
                    it["lb"] = lb
                if sbk > 0:
                    ls = lsr.next()
                    if lsum_prev is None:
                        cx.op("pool", lambda e: e.tensor_copy(out=ls[:], in_=spb[:]), reads=[spb], writes=[ls])
                    else:
                        cx.op("pool", lambda e: e.tensor_tensor(out=ls[:], in0=lsum_prev[:], in1=spb[:], op=ALU.add), reads=[lsum_prev, spb], writes=[ls])
                    G_["lsum"] = ls

            def stB(it):
                r, spb, lb, lg = it["r"], it["spb"], it["lb"], it["lg"]
                ap_ = ar.next()
                cx.op("pe", lambda e: e.matmul(ap_[:], triU_bf[:], spb[:], start=True, stop=(lb is None)), reads=[triU_bf, spb], writes=[ap_])
                if lb is not None:
                    cx.op("pe", lambda e: e.matmul(ap_[:], ones_bf[:], lb[:], start=False, stop=True), reads=[ones_bf, lb], writes=[ap_])
                ag = agr.next()
                cx.op("dve", lambda e: e.tensor_tensor(out=ag[:], in0=lg[:], in1=ap_[:], op=ALU.subtract), reads=[lg, ap_], writes=[ag])
                wt_ = wtr.next()
                cx.op("act", lambda e: e.activation(out=wt_[:], in_=ag[:], func=AF.Exp), reads=[ag], writes=[wt_])
                if r >= 0:
                    cx.op("pool", lambda e: e.tensor_tensor(out=wt_[:], in0=wt_[:], in1=msb_bf[:, r * 512:(r + 1) * 512], op=ALU.mult), reads=[wt_, msb_bf], writes=[wt_])
                it["wt"] = wt_

            def stC(it):
                h, g, bi, sbk, nblk = it["h"], it["g"], it["bi"], it["sbk"], it["nblk"]
                cs = slice(g * 512, (g + 1) * 512)
                q, k, v = hs[h]
                yp = gs[(h, g)]["yp"]
                wt_ = it["wt"]
                cx.op("pe", lambda e: e.matmul(yp[:], v[:, sbk, :], wt_[:], start=(bi == 0), stop=(bi == nblk - 1)), reads=[v, wt_], writes=[yp])
                if bi == nblk - 1:
                    ys = ysr.next()
                    cx.op("act", lambda e: e.activation(out=ys[:], in_=yp[:], func=AF.Copy), reads=[yp], writes=[ys])
                    cx.dma("sp", yabT[0, h, :, cs], ys[:], reads=[ys], writes=[db("yab")])

            swpipe(items, [stA, lambda it: None, stB, lambda it: None, stC])
        cx.barrier()

    def phase_dsa():
        scale = 128.0 ** -0.5
        K = float(cfg.TOPK)
        with contextlib.ExitStack() as st:
            ki = cx.sb(st, "dki", [128, S], BF16)
            wi = cx.sb(st, "dwi", [128, NT, IH], F32)
            cx.dma("sp", ki[:], kiT[0], reads=[db("ki")], writes=[ki])
            cx.dma("sp", wi[:], wis.rearrange("(t p) h -> p t h", p=128), reads=[db("wis")], writes=[wi])
            MTs = [cx.sb(st, f"dMT{i}", [128, NT, 512], BF16) for i in range(2)]
            TS = []
            for z in range(4):
                TS.append(dict(
                    qi=cx.sb(st, f"dqi{z}", [128, IH // 2, 128], BF16),
                    diag=cx.sb(st, f"ddiag{z}", [128, IH, 128], BF16),
                    score=cx.sb(st, f"dscore{z}", [128, S], F32),
                    junk=cx.sb(st, f"djunk{z}", [128, S], BF16),
                    mask=cx.sb(st, f"dmask{z}", [128, S], BF16),
                    sm=cx.sb(st, f"dsm{z}", [128, 8], F32),
                    wtab=cx.sb(st, f"dwtab{z}", [128, NIT + 1], F32)))
            rr = Ring([cx.sb(st, f"drelu{i}", [128, 512], BF16) for i in range(4)])
            dpr = Ring([cx.ps(st, f"ddp{i}", [128, 512], F32) for i in range(4)])
            spr = Ring([cx.ps(st, f"dsp{i}", [128, 512], F32) for i in range(1)])
            tpr = Ring([cx.ps(st, f"dtp{i}", [128, 8, 128], BF16) for i in range(1)])
            lpr = dpr
            ypr = Ring([cx.ps(st, f"dyp{i}", [128, 512], F32) for i in range(1)])
            npr = Ring([cx.ps(st, f"dnp{i}", [128, 512], F32) for i in range(1)])
            qr = Ring([cx.sb(st, f"dq{i}", [128, 512], BF16) for i in range(2)])
            kr = Ring([cx.sb(st, f"dk{i}", [128, S], BF16) for i in range(2)])
            vr = Ring([cx.sb(st, f"dv{i}", [128, NT, 128], BF16) for i in range(2)])
            ebr = Ring([cx.sb(st, f"deb{i}", [128, 512], BF16) for i in range(4)])
            ptr_ = Ring([cx.sb(st, f"dpt{i}", [128, 512], BF16) for i in range(4)])
            yer = Ring([cx.sb(st, f"dye{i}", [128, 512], F32) for i in range(4)])
            ner = Ring([cx.sb(st, f"dne{i}", [128, 512], F32) for i in range(4)])
            ysr = Ring([cx.sb(st, f"dys{i}", [128, 512], BF16) for i in range(2)])

            pend = {}

            def idx_unit(g, half):
                MT = MTs[g % 2]
                if half == 0:
                    cx.op("pool", lambda e: e.memset(MT[:, 0:4 * g + 4, :], 0.0), writes=[MT])
                tiles = []
                for z in range(2):
                    il = 2 * half + z
                    i = 4 * g + il
                    tcs = slice(il * 128, (il + 1) * 128)
                    if i < 2:
                        for sbk in range(i + 1):
                            src = mdsa_bf if sbk == i else ones_bf
                            cx.op("pool", lambda e, sbk=sbk, src=src: e.tensor_copy(out=MT[:, sbk, tcs], in_=src[:]), reads=[src], writes=[MT])
                        continue
                    T = TS[2 * ((2 * g + half) % 2) + z]
                    tiles.append((T, i, tcs))
                    W = 128 * (i + 1)
                    qi, diag, score = T["qi"], T["diag"], T["score"]
                    cx.dma("sp", qi[:], qiT[:, :, i * 128:(i + 1) * 128].rearrange("j p s -> p j s"), reads=[db("qi")], writes=[qi])
                    for hh in range(IH):
                        cx.op("act", lambda e, hh=hh: e.activation(out=diag[:, hh, :], in_=ident_bf[:], func=AF.Copy, scale=wi[:, i, hh:hh + 1]), reads=[ident_bf, wi], writes=[diag])
                    nkc = (W + 511) // 512
                    for c in range(nkc):
                        cw = min(512, W - 512 * c)
                        sp_ = spr.next()

                        def iA(hh, c=c, cw=cw, qi=qi):
                            pb = 64 * (hh["hh"] % 2)
                            j = hh["hh"] // 2
                            dp = dpr.next()
                            cx.op("pe", lambda e: e.matmul(dp[:, 0:cw], qi[pb:pb + 64, j, :], ki[pb:pb + 64, c * 512:c * 512 + cw], start=True, stop=True), reads=[qi, ki], writes=[dp])
                            rl = rr.next()
                            cx.op("act", lambda e: e.activation(out=rl[:, 0:cw], in_=dp[:, 0:cw], func=AF.Relu), reads=[dp], writes=[rl])
                            hh["rl"] = rl

                        def iB(hh, cw=cw, sp_=sp_, diag=diag):
                            h_ = hh["hh"]
                            rl = hh["rl"]
                            cx.op("pe", lambda e: e.matmul(sp_[:, 0:cw], diag[:, h_, :], rl[:, 0:cw], start=(h_ == 0), stop=(h_ == IH - 1)), reads=[diag, rl], writes=[sp_])

                        swpipe([dict(hh=hh) for hh in range(IH)], [iA, lambda x: None, iB])
                        cx.op("act", lambda e: e.activation(out=score[:, c * 512:c * 512 + cw], in_=sp_[:, 0:cw], func=AF.Copy), reads=[sp_], writes=[score])
                pend[(g, half)] = tiles
                if not tiles:
                    return
                for T, i, tcs in tiles:
                    W = 128 * (i + 1)
                    s_, score, wtab = T["sm"], T["score"], T["wtab"]
                    cx.op("dve", lambda e: e.tensor_reduce(out=s_[:, 0:1], in_=score[:, 0:W], axis=AX.X, op=ALU.max), reads=[score], writes=[s_])
                    cx.op("dve", lambda e: e.tensor_reduce(out=s_[:, 1:2], in_=score[:, 0:W], axis=AX.X, op=ALU.min), reads=[score], writes=[s_])
                for T, i, tcs in tiles:
                    s_, score, wtab = T["sm"], T["score"], T["wtab"]
                    cx.op("dve", lambda e: e.tensor_tensor(out=s_[:, 2:3], in0=s_[:, 0:1], in1=s_[:, 1:2], op=ALU.subtract), reads=[s_], writes=[s_])
                    cx.op("dve", lambda e: e.tensor_tensor(out=score[:, i * 128:(i + 1) * 128], in0=score[:, i * 128:(i + 1) * 128], in1=cst[:, C_NEGTRI:C_NEGTRI + 128], op=ALU.add), reads=[score, cst], writes=[score])
                for T, i, tcs in tiles:
                    s_, wtab = T["sm"], T["wtab"]
                    cx.op("dve", lambda e: e.tensor_scalar(out=wtab[:], in0=cst[:, C_POW2:C_POW2 + NIT + 1], scalar1=s_[:, 2:3], scalar2=None, op0=ALU.mult), reads=[cst, s_], writes=[wtab])
                for zi, (T, i, tcs) in enumerate(tiles):
                    s_, wtab = T["sm"], T["wtab"]
                    cx.op("dve", lambda e: e.tensor_tensor(out=s_[:, 3:4], in0=s_[:, 1:2], in1=wtab[:, 0:1], op=ALU.add), reads=[s_, wtab], writes=[s_])
                    if zi == 1:
                        cx.op("dve", lambda e: e.tensor_scalar(out=s_[:, 7:8], in0=s_[:, 3:4], scalar1=-1.0, scalar2=None, op0=ALU.mult), reads=[s_], writes=[s_])
                for it in range(NIT):
                    for zi, (T, i, tcs) in enumerate(tiles):
                        W = 128 * (i + 1)
                        s_, score, junk = T["sm"], T["score"], T["junk"]
                        if zi == 0:
                            cx.op("dve", lambda e: e.tensor_scalar(out=junk[:, 0:W], in0=score[:, 0:W], scalar1=s_[:, 3:4], scalar2=None, op0=ALU.is_ge, op1=ALU.add, accum_out=s_[:, 4:5]), reads=[score, s_], writes=[junk, s_])
                        else:
                            cx.op("act", lambda e: e.activation(out=junk[:, 0:W], in_=score[:, 0:W], func=AF.Sign, bias=s_[:, 7:8], scale=1.0, accum_out=s_[:, 4:5]), reads=[score, s_], writes=[junk, s_])
                    for zi, (T, i, tcs) in enumerate(tiles):
                        W = 128 * (i + 1)
                        s_, wtab = T["sm"], T["wtab"]
                        thr_c = (K - 0.5) if zi == 0 else (2.0 * K - W - 0.5)
                        cx.op("dve", lambda e: e.scalar_tensor_tensor(out=s_[:, 5:6], in0=s_[:, 4:5], scalar=thr_c, in1=wtab[:, it:it + 1], op0=ALU.is_ge, op1=ALU.mult), reads=[s_, wtab], writes=[s_])
                    for zi, (T, i, tcs) in enumerate(tiles):
                        s_, wtab = T["sm"], T["wtab"]
                        cx.op("dve", lambda e: e.scalar_tensor_tensor(out=s_[:, 3:4], in0=s_[:, 5:6], scalar=wtab[:, it + 1:it + 2], in1=s_[:, 3:4], op0=ALU.subtract, op1=ALU.add), reads=[s_, wtab], writes=[s_])
                        if zi == 1:
                            cx.op("dve", lambda e: e.tensor_scalar(out=s_[:, 7:8], in0=s_[:, 3:4], scalar1=-1.0, scalar2=None, op0=ALU.mult), reads=[s_], writes=[s_])
                for T, i, tcs in tiles:
                    s_, wtab = T["sm"], T["wtab"]
                    cx.op("dve", lambda e: e.tensor_tensor(out=s_[:, 6:7], in0=s_[:, 3:4], in1=wtab[:, NIT:NIT + 1], op=ALU.subtract), reads=[s_, wtab], writes=[s_])
                for T, i, tcs in tiles:
                    W = 128 * (i + 1)
                    s_, score, mask = T["sm"], T["score"], T["mask"]
                    cx.op("dve", lambda e: e.tensor_scalar(out=mask[:, 0:W], in0=score[:, 0:W], scalar1=s_[:, 6:7], scalar2=None, op0=ALU.is_ge), reads=[score, s_], writes=[mask])

            def idx_fin(g, half):
                MT = MTs[g % 2]
                for T, i, tcs in pend.pop((g, half)):
                    mask = T["mask"]
                    for s0 in range(0, i + 1, 8):
                        n8 = min(8, i + 1 - s0)
                        tp = tpr.next()
                        for q_ in range(n8):
                            cx.op("pe", lambda e, q_=q_: e.transpose(out=tp[:, q_, :], in_=mask[:, (s0 + q_) * 128:(s0 + q_ + 1) * 128], identity=ident_bf[:]), reads=[mask, ident_bf], writes=[tp])
                        cx.op("act", lambda e: e.activation(out=MT[:, s0:s0 + n8, tcs], in_=tp[:, 0:n8, :], func=AF.Copy), reads=[tp], writes=[MT])

            def att_part(g, heads):
                MT = MTs[g % 2]
                cs = slice(g * 512, (g + 1) * 512)
                nblk = 4 * g + 4
                aitems = [dict(h=h, sbk=sbk) for h in heads for sbk in range(nblk)]
                ahs = {}

                def aA(it):
                    h, sbk = it["h"], it["sbk"]
                    if sbk == 0:
                        q, k, v = qr.next(), kr.next(), vr.next()
                        cx.dma("sp", q[:], qbT[h, :, cs], reads=[db("qb")], writes=[q])
                        cx.dma("sp", k[:, 0:nblk * 128], kbT[h, :, 0:nblk * 128], reads=[db("kb")], writes=[k])
                        cx.dma("sp", v[:, 0:nblk, :], vab[0:nblk * 128, HW + h * 128:HW + (h + 1) * 128].rearrange("(st p) d -> p st d", p=128), reads=[db("vab")], writes=[v])
                        ahs[h] = (q, k, v)
                    q, k, v = ahs[h]
                    lp = lpr.next()
                    cx.op("pe", lambda e: e.matmul(lp[:], k[:, sbk * 128:(sbk + 1) * 128], q[:], start=True, stop=True), reads=[k, q], writes=[lp])
                    eb = ebr.next()
                    cx.op("act", lambda e: e.activation(out=eb[:], in_=lp[:], func=AF.Exp, scale=scale), reads=[lp], writes=[eb])
                    pt = ptr_.next()
                    cx.op("pool", lambda e: e.tensor_tensor(out=pt[:], in0=eb[:], in1=MT[:, sbk, :], op=ALU.mult), reads=[eb, MT], writes=[pt])
                    it["pt"] = pt

                def aB(it):
                    h, sbk, pt = it["h"], it["sbk"], it["pt"]
                    q, k, v = ahs[h]
                    if sbk == 0:
                        ahs[(h, "y")] = (ypr.next(), npr.next())
                    yp, np_ = ahs[(h, "y")]
                    cx.op("pe", lambda e: e.matmul(yp[:], v[:, sbk, :], pt[:], start=(sbk == 0), stop=(sbk == nblk - 1)), reads=[v, pt], writes=[yp])
                    cx.op("pe", lambda e: e.matmul(np_[:], ones_bf[:], pt[:], start=(sbk == 0), stop=(sbk == nblk - 1)), reads=[ones_bf, pt], writes=[np_])
                    if sbk == nblk - 1:
                        ye, ne = yer.next(), ner.next()
                        cx.op("act", lambda e: e.activation(out=ye[:], in_=yp[:], func=AF.Copy), reads=[yp], writes=[ye])
                        cx.op("act", lambda e: e.activation(out=ne[:], in_=np_[:], func=AF.Copy), reads=[np_], writes=[ne])
                        ahs[(h, "e")] = (ye, ne)

                swpipe(aitems, [aA, lambda x: None, aB])
                for h in heads:
                    ye, ne = ahs[(h, "e")]
                    cx.op("dve", lambda e: e.reciprocal(out=ne[:], in_=ne[:]), reads=[ne], writes=[ne])
                    ys = ysr.next()
                    cx.op("dve", lambda e: e.tensor_tensor(out=ys[:], in0=ye[:], in1=ne[:], op=ALU.mult), reads=[ye, ne], writes=[ys])
                    cx.dma("sp", yabT[1, h, :, cs], ys[:], reads=[ys], writes=[db("yab")])

            idx_unit(0, 0)
            idx_unit(0, 1)
            idx_fin(0, 0)
            if NTB > 1:
                idx_unit(1, 0)
            idx_fin(0, 1)
            h1 = list(range(0, H // 2))
            h2 = list(range(H // 2, H))
            for g in range(NTB):
                att_part(g, h1)
                if g + 1 < NTB:
                    idx_unit(g + 1, 1)
                    idx_fin(g + 1, 0)
                att_part(g, h2)
                if g + 1 < NTB:
                    if g + 2 < NTB:
                        idx_unit(g + 2, 0)
                    idx_fin(g + 1, 1)
        cx.barrier()

    def phase_out(l, b):
        xsrc = x_in if l == 0 else xres
        with contextlib.ExitStack() as st0:
            mT = cx.sb(st0, "mT", [128, KC, S], BF16)
            mbufs = [Buf(mT.t, f"m{i}") for i in range(NTB)]
            with contextlib.ExitStack() as st:
                wa = cx.sb(st, "owa", [128, H, D], BF16)
                wb = cx.sb(st, "owb", [128, H, D], BF16)
                cx.dma("pool", wa[:], w_a[l].rearrange("(h p) n -> p h n", p=128), writes=[wa])
                cx.dma("pool", wb[:], w_b[l].rearrange("(h p) n -> p h n", p=128), writes=[wb])
                yar = Ring([cx.sb(st, f"oya{i}", [128, H, 512], BF16) for i in range(2)])
                ybr = Ring([cx.sb(st, f"oyb{i}", [128, H, 512], BF16) for i in range(2)])
                par = Ring([cx.ps(st, f"opa{i}", [128, 512], F32) for i in range(2)])
                pbr = Ring([cx.ps(st, f"opb{i}", [128, 512], F32) for i in range(2)])
                sar = Ring([cx.sb(st, f"osa{i}", [128, 512], BF16) for i in range(2)])
                sbr = Ring([cx.sb(st, f"osb{i}", [128, 512], BF16) for i in range(2)])
                m1r = Ring([cx.sb(st, f"om1{i}", [128, 512], F32) for i in range(2)])
                m2r = Ring([cx.sb(st, f"om2{i}", [128, 512], F32) for i in range(2)])
                for tb in range(NTB):
                    cs = slice(tb * 512, (tb + 1) * 512)
                    ya, yb = yar.next(), ybr.next()
                    cx.dma("sp", ya[:], yabT[0, :, :, cs].rearrange("h p s -> p h s"), reads=[db("yab")], writes=[ya])
                    cx.dma("sp", yb[:], yabT[1, :, :, cs].rearrange("h p s -> p h s"), reads=[db("yab")], writes=[yb])
                    for c in range(KC):
                        pa, pb = par.next(), pbr.next()
                        for h in range(H):
                            cx.op("pe", lambda e, h=h: e.matmul(pa[:], wa[:, h, c * 128:(c + 1) * 128], ya[:, h, :], start=(h == 0), stop=(h == H - 1)), reads=[wa, ya], writes=[pa])
                        for h in range(H):
                            cx.op("pe", lambda e, h=h: e.matmul(pb[:], wb[:, h, c * 128:(c + 1) * 128], yb[:, h, :], start=(h == 0), stop=(h == H - 1)), reads=[wb, yb], writes=[pb])
                        sa, sb_ = sar.next(), sbr.next()
                        cx.dma("sp", sa[:], sgaT[c, :, cs], reads=[db("ga")], writes=[sa])
                        cx.dma("sp", sb_[:], sgbT[c, :, cs], reads=[db("gb")], writes=[sb_])
                        m1, m2 = m1r.next(), m2r.next()
                        cx.op("dve", lambda e: e.tensor_tensor(out=m1[:], in0=pa[:], in1=sa[:], op=ALU.mult), reads=[pa, sa], writes=[m1])
                        cx.op("dve", lambda e: e.tensor_tensor(out=m2[:], in0=pb[:], in1=sb_[:], op=ALU.mult), reads=[pb, sb_], writes=[m2])
                        cx.op("dve", lambda e: e.tensor_tensor(out=mT[:, c, cs], in0=m1[:], in1=m2[:], op=ALU.add), reads=[m1, m2], writes=[mbufs[tb]])
            cx.barrier()
            with contextlib.ExitStack() as st:
                wor = Ring([cx.sb(st, f"owo{i}", [128, KC, 512], BF16) for i in range(2)])
                por = Ring([cx.ps(st, f"opo{i}", [128, 512], F32) for i in range(4)])
                gtb = cx.sb(st, "ogt", [128, D], F32)
                cx.dma("sp", gtb[:], modrow[b:b + 1, 2 * D:3 * D].broadcast_to([128, D]), reads=[db("modrow")], writes=[gtb])
                cx.op("dve", lambda e: e.tensor_scalar(out=gtb[:], in0=gtb[:], scalar1=1.0, scalar2=None, op0=ALU.add), reads=[gtb], writes=[gtb])
                xr = Ring([cx.sb(st, f"ox{i}", [128, 512], F32) for i in range(6)])
                tr = Ring([cx.sb(st, f"ot{i}", [128, 512], F32) for i in range(3)])
                wov = w_o[l].rearrange("(kc p) n -> p kc n", p=128)
                wos = {}

                def oA(u):
                    cg, tt = u["cg"], u["tt"]
                    if tt == 0:
                        wo = wor.next()
                        cx.dma("pool", wo[:], wov[:, :, cg * 512:(cg + 1) * 512], writes=[wo])
                        wos[cg] = wo
                    gs = slice(cg * 512, (cg + 1) * 512)
                    xt = xr.next()
                    cx.dma("sp", xt[:], xsrc[b, tt * 128:(tt + 1) * 128, gs], reads=[db(("x", b, tt, cg))], writes=[xt])
                    u["xt"] = xt

                def oB(u):
                    cg, tt, xt = u["cg"], u["tt"], u["xt"]
                    wo = wos[cg]
                    gs = slice(cg * 512, (cg + 1) * 512)
                    po = por.next()
                    for kc in range(KC):
                        cx.op("pe", lambda e, kc=kc: e.matmul(po[:], mT[:, kc, tt * 128:(tt + 1) * 128], wo[:, kc, :], start=(kc == 0), stop=(kc == KC - 1)), reads=[wo, mbufs[tt // 4]], writes=[po])
                    t_ = tr.next()
                    cx.op("dve", lambda e: e.tensor_tensor(out=t_[:], in0=po[:], in1=gtb[:, gs], op=ALU.mult), reads=[po, gtb], writes=[t_])
                    cx.op("dve", lambda e: e.tensor_tensor(out=xt[:], in0=xt[:], in1=t_[:], op=ALU.add), reads=[xt, t_], writes=[xt])
                    cx.dma("sp", xres[b, tt * 128:(tt + 1) * 128, gs], xt[:], reads=[xt], writes=[db(("x", b, tt, cg)), db(("x", b, tt))])

                swpipe([dict(cg=cg, tt=tt) for cg in range(D // 512) for tt in range(NT)], [oA, lambda u: None, lambda u: None, oB])
        cx.barrier()

    def phase_ffn(l, b):
        TH = min(1024, S)
        NHF = S // TH
        NTBH = TH // 512
        NJ = DFF // 128
        FH = 4 if (NJ % 4 == 0 and NJ >= 8) else (2 if (NJ % 2 == 0 and NJ >= 4) else 1)
        JH = NJ // FH
        NFC = 2 * NJ
        JG = 3 if JH >= 3 else JH
        with contextlib.ExitStack() as st:
            hT = cx.sb(st, "fhT", [128, KC, TH], BF16)
            hbufs = [Buf(hT.t, f"fh{i}") for i in range(TH // 128)]
            gT = cx.sb(st, "fgT", [128, JH, TH], BF16)
            gbufs = [Buf(gT.t, f"fg{i}") for i in range(NTBH)]
            halo = cx.sb(st, "fhalo", [128, NFC, 2], F32)
            cx.op("dve", lambda e: e.memset(halo[:], 0.0), writes=[halo])
            gtb = cx.sb(st, "fgt", [128, D], F32)
            cx.dma("sp", gtb[:], modrow[b:b + 1, 5 * D:6 * D].broadcast_to([128, D]), reads=[db("modrow")], writes=[gtb])
            cx.op("dve", lambda e: e.tensor_scalar(out=gtb[:], in0=gtb[:], scalar1=1.0, scalar2=None, op0=ALU.add), reads=[gtb], writes=[gtb])
            wuv = w_up[l].rearrange("(kc p) n -> p kc n", p=128)
            wdv = w_down[l].rearrange("(j p) n -> p j n", p=128)
            for hf in range(NHF):
                with contextlib.ExitStack() as st2:
                    phase_norm(st2, xres, b, hf * (TH // 128), TH // 128, 1, hT, hbufs)
                cx.barrier()
                with contextlib.ExitStack() as st2:
                    wur = Ring([cx.sb(st2, f"fwu{i}", [128, KC, 2, JG * 128], BF16) for i in range(2)])
                    pur = Ring([cx.ps(st2, f"fpu{i}", [128, 512], F32) for i in range(4)])
                    ucr = Ring([cx.sb(st2, f"fuc{i}", [128, 514], F32) for i in range(4)])
                    t0r = Ring([cx.sb(st2, f"ft0{i}", [128, 512], F32) for i in range(2)])
                    t1r = Ring([cx.sb(st2, f"ft1{i}", [128, 512], F32) for i in range(2)])
                    t2r = Ring([cx.sb(st2, f"ft2{i}", [128, 512], F32) for i in range(4)])
                    sar = Ring([cx.sb(st2, f"fsa{i}", [128, 512], F32) for i in range(2)])
                    wdr = Ring([cx.sb(st2, f"fwd{i}", [128, JH, 512], BF16) for i in range(2)])
                    pdr = Ring([cx.ps(st2, f"fpd{i}", [128, 512], F32) for i in range(4)])
                    xr = Ring([cx.sb(st2, f"fx{i}", [128, 512], F32) for i in range(6)])
                    tr = Ring([cx.sb(st2, f"ftt{i}", [128, 512], F32) for i in range(3)])
                    for fh in range(FH):
                        prev_uc = {}
                        wus = {}

                        def fA(u, fh=fh, prev_uc=prev_uc, wus=wus):
                            jj, tb = u["jj"], u["tb"]
                            j = fh * JH + jj
                            if tb == 0 and jj % JG == 0:
                                ng = min(JG, JH - jj)
                                wu = wur.next()
                                cx.dma("pool", wu[:, :, 0, 0:ng * 128], wuv[:, :, j * 128:(j + ng) * 128], writes=[wu])
                                cx.dma("pool", wu[:, :, 1, 0:ng * 128], wuv[:, :, DFF + j * 128:DFF + (j + ng) * 128], writes=[wu])
                                wus[jj // JG] = wu
                            wu = wus[jj // JG]
                            jo = (jj % JG) * 128
                            cs = slice(tb * 512, (tb + 1) * 512)
                            res = []
                            for ab in range(2):
                                ch = ab * NJ + j
                                pu = pur.next()
                                for kc in range(KC):
                                    cx.op("pe", lambda e, kc=kc, ab=ab: e.matmul(pu[:], wu[:, kc, ab, jo:jo + 128], hT[:, kc, cs], start=(kc == 0), stop=(kc == KC - 1)), reads=[wu] + hbufs[tb * 4:(tb + 1) * 4], writes=[pu])
                                uc = ucr.next()
                                if tb == 0:
                                    cx.op("dve", lambda e, ch=ch: e.tensor_copy(out=uc[:, 0:2], in_=halo[:, ch, :]), reads=[halo], writes=[uc])
                                else:
                                    pu_ = prev_uc[(jj, ab)]
                                    cx.op("dve", lambda e, pu_=pu_: e.tensor_copy(out=uc[:, 0:2], in_=pu_[:, 512:514]), reads=[pu_], writes=[uc])
                                cx.op("act", lambda e: e.activation(out=uc[:, 2:514], in_=pu[:], func=AF.Copy), reads=[pu], writes=[uc])
                                if tb == NTBH - 1:
                                    cx.op("dve", lambda e, ch=ch: e.tensor_copy(out=halo[:, ch, :], in_=uc[:, 512:514]), reads=[uc], writes=[halo])
                                prev_uc[(jj, ab)] = uc
                                t0 = t0r.next()
                                cx.op("act", lambda e, ch=ch: e.activation(out=t0[:], in_=pu[:], func=AF.Identity, scale=cwT[:, 2, ch:ch + 1], bias=cwT[:, 3, ch:ch + 1]), reads=[pu, cwT], writes=[t0])
                                t1 = t1r.next()
                                cx.op("dve", lambda e, ch=ch: e.scalar_tensor_tensor(out=t1[:], in0=uc[:, 1:513], scalar=cwT[:, 1, ch:ch + 1], in1=t0[:], op0=ALU.mult, op1=ALU.add), reads=[uc, cwT, t0], writes=[t1])
                                t2 = t2r.next()
                                cx.op("dve", lambda e, ch=ch: e.scalar_tensor_tensor(out=t2[:], in0=uc[:, 0:512], scalar=cwT[:, 0, ch:ch + 1], in1=t1[:], op0=ALU.mult, op1=ALU.add), reads=[uc, cwT, t1], writes=[t2])
                                res.append(t2)
                            u["res"] = res

                        def fB(u):
                            jj, tb, res = u["jj"], u["tb"], u["res"]
                            cs = slice(tb * 512, (tb + 1) * 512)
                            sa = sar.next()
                            cx.op("act", lambda e: e.activation(out=sa[:], in_=res[0][:], func=AF.Silu), reads=[res[0]], writes=[sa])
                            cx.op("dve", lambda e: e.tensor_tensor(out=gT[:, jj, cs], in0=sa[:], in1=res[1][:], op=ALU.mult), reads=[sa, res[1]], writes=[gbufs[tb]])

                        swpipe([dict(jj=jj, tb=tb) for jj in range(JH) for tb in range(NTBH)], [fA, fB])
                        wds = {}

                        def dA(u, fh=fh, wds=wds):
                            cg, tl = u["cg"], u["tl"]
                            if tl == 0:
                                wd = wdr.next()
                                cx.dma("pool", wd[:], wdv[:, fh * JH:(fh + 1) * JH, cg * 512:(cg + 1) * 512], writes=[wd])
                                wds[cg] = wd
                            gs = slice(cg * 512, (cg + 1) * 512)
                            tt = hf * (TH // 128) + tl
                            xt = xr.next()
                            cx.dma("sp", xt[:], xres[b, tt * 128:(tt + 1) * 128, gs], reads=[db(("x", b, tt, cg)), db(("x", b, tt))], writes=[xt])
                            u["xt"] = xt

                        def dB(u, wds=wds):
                            cg, tl, xt = u["cg"], u["tl"], u["xt"]
                            wd = wds[cg]
                            gs = slice(cg * 512, (cg + 1) * 512)
                            tt = hf * (TH // 128) + tl
                            pd = pdr.next()
                            for jj in range(JH):
                                cx.op("pe", lambda e, jj=jj: e.matmul(pd[:], gT[:, jj, tl * 128:(tl + 1) * 128], wd[:, jj, :], start=(jj == 0), stop=(jj == JH - 1)), reads=[wd, gbufs[tl // 4]], writes=[pd])
                            t_ = tr.next()
                            cx.op("dve", lambda e: e.tensor_tensor(out=t_[:], in0=pd[:], in1=gtb[:, gs], op=ALU.mult), reads=[pd, gtb], writes=[t_])
                            cx.op("dve", lambda e: e.tensor_tensor(out=xt[:], in0=xt[:], in1=t_[:], op=ALU.add), reads=[xt, t_], writes=[xt])
                            cx.dma("sp", xres[b, tt * 128:(tt + 1) * 128, gs], xt[:], reads=[xt], writes=[db(("x", b, tt, cg)), db(("x", b, tt))])

                        swpipe([dict(cg=cg, tl=tl) for cg in range(D // 512) for tl in range(TH // 128)], [dA, lambda u: None, lambda u: None, dB])
                cx.barrier()
        cx.barrier()

    def phase_final(xsrc):
        with contextlib.ExitStack() as st:
            gfb = cx.sb(st, "gfb", [128, D], F32)
            cx.dma("sp", gfb[:], g_final.rearrange("(o d) -> o d", o=1).broadcast_to([128, D]), writes=[gfb])
            xr = Ring([cx.sb(st, f"zx{i}", [128, D], F32) for i in range(3)])
            jr = cx.sb(st, "zjunk", [128, D], BF16)
            ssr = Ring([cx.sb(st, f"zss{i}", [128, 4], F32) for i in range(2)])
            for b in range(NB):
                for tt in range(NT):
                    xt = xr.next()
                    cx.dma("sp", xt[:], xsrc[b, tt * 128:(tt + 1) * 128, :], reads=[db(("x", b, tt))], writes=[xt])
                    ss = ssr.next()
                    cx.op("act", lambda e: e.activation(out=jr[:], in_=xt[:], func=AF.Square, accum_out=ss[:, 0:1]), reads=[xt], writes=[jr, ss])
                    cx.op("act", lambda e: e.activation(out=ss[:, 1:2], in_=ss[:, 0:1], func=AF.Ln, scale=1.0 / D, bias=eps_t[:]), reads=[ss, eps_t], writes=[ss])
                    cx.op("act", lambda e: e.activation(out=ss[:, 2:3], in_=ss[:, 1:2], func=AF.Exp, scale=-0.5), reads=[ss], writes=[ss])
                    cx.op("dve", lambda e: e.scalar_tensor_tensor(out=xt[:], in0=xt[:], scalar=ss[:, 2:3], in1=gfb[:], op0=ALU.mult, op1=ALU.mult), reads=[xt, ss, gfb], writes=[xt])
                    cx.dma("sp", out[b, tt * 128:(tt + 1) * 128, :], xt[:], reads=[xt], writes=[db("out")])

    phases = phases or ("setup", "mod", "norm", "proj", "sb", "dsa", "out", "ffn", "final")
    cx.barrier()
    if "setup" in phases:
        phase_setup()
    for l in layers:
        if "mod" in phases:
            phase_mod(l)
        for b in range(NB):
            xsrc = x_in if l == 0 else xres
            with contextlib.ExitStack() as stA:
                hT = cx.sb(stA, "hT", [128, KC, S], BF16)
                hbufs = [Buf(hT.t, f"h{i}") for i in range(NT)]
                if "norm" in phases:
                    with contextlib.ExitStack() as st2:
                        phase_norm(st2, xsrc, b, 0, NT, 0, hT, hbufs)
                    cx.barrier()
                if "proj" in phases:
                    phase_proj(l, b, hT, hbufs)
            if "sb" in phases:
                phase_sb()
            if "dsa" in phases:
                phase_dsa()
            if "out" in phases:
                phase_out(l, b)
            if "ffn" in phases:
                phase_ffn(l, b)
    if "final" in phases:
        phase_final(xres)
    cx.finish()
    G.close()
    print("instructions emitted:", cx.ninst)
    return nc


def make_in_maps(cfg, inputs, n_cores):
    NB = cfg.NB
    cols = cfg.ext_cols()
    w_in_ext = np.ascontiguousarray(inputs["w_in"][:, :, cols])
    consts = make_consts(cfg)
    shared = {k: np.ascontiguousarray(inputs[k]) for k in
              ("w_a", "w_b", "w_o", "w_ada", "b_ada", "g_mix", "g_ffn", "w_up", "conv_w", "conv_b", "w_down", "g_final")}
    shared["w_in_ext"] = w_in_ext
    shared["consts"] = consts
    maps = []
    for i in range(n_cores):
        m = dict(shared)
        m["x"] = np.ascontiguousarray(inputs["x"][i * NB:(i + 1) * NB])
        m["c"] = np.ascontiguousarray(inputs["c"][i * NB:(i + 1) * NB])
        m["positions"] = np.ascontiguousarray(inputs["positions"][i * NB:(i + 1) * NB]).astype(np.int32)
        maps.append(m)
    return maps


_CACHE = {}


def kernel(**inputs):
    cfg = Cfg()
    n_cores = 8
    if "nc" not in _CACHE:
        _CACHE["nc"] = build(cfg)
    nc = _CACHE["nc"]
    maps = make_in_maps(cfg, inputs, n_cores)
    res = run_bass_kernel_spmd(nc, maps, core_ids=list(range(n_cores)))
    return np.concatenate([np.asarray(r["out"]) for r in res.results], axis=0).astype(np.float32)
```

```python
import contextlib
import numpy as np
import concourse.bass as bass
import concourse.mybir as mybir
from concourse.bass_utils import run_bass_kernel_spmd

F32 = mybir.dt.float32
BF16 = mybir.dt.bfloat16
I32 = mybir.dt.int32
AF = mybir.ActivationFunctionType
ALU = mybir.AluOpType
AX = mybir.AxisListType
BIG = 1.0e30
NIT = 24


class Cfg:
    def __init__(self, D=2048, S=2048, H=8, IH=16, DFF=5632, L=4, TOPK=256, NB=2):
        self.D, self.S, self.H, self.IH, self.DFF, self.L, self.NB = D, S, H, IH, DFF, L, NB
        self.TOPK = min(TOPK, S // 4)
        self.KC = D // 128
        self.NT = S // 128
        self.NTB = S // 512
        self.HW = H * 128
        self.EPS = 1e-6
        self.THETA = 10000.0
        fm = []
        for h in range(H):
            fm.append(("qa", h, None))
        for h in range(H):
            fm.append(("ka", h, None))
        for h in range(H):
            fm.append(("qb", h, "r128"))
        for h in range(H):
            fm.append(("kb", h, "r128"))
        for j in range(IH // 2):
            fm.append(("qi", j, "r64"))
        fm.append(("ki", 0, "r64"))
        for j in range(self.KC):
            fm.append(("ga", j, None))
        for j in range(self.KC):
            fm.append(("gb", j, None))
        self.fm = fm
        self.NFM = len(fm) * 128
        self.NTM = 2 * self.HW + IH
        self.NEXT = self.NFM + self.NTM
        o = {}
        off = 0
        for name, sz in (("qa", self.HW), ("ka", self.HW), ("va", self.HW), ("qb", self.HW),
                         ("kb", self.HW), ("vb", self.HW), ("qi", IH * 64), ("ki", 64),
                         ("wi", IH), ("ga", D), ("gb", D)):
            o[name] = off
            off += sz
        self.off = o
        self.IN_WIDTH = off

    def ext_cols(self):
        idx = []
        sw128 = np.concatenate([np.arange(64, 128), np.arange(0, 64)])
        sw64 = np.concatenate([np.arange(32, 64), np.arange(0, 32), np.arange(96, 128), np.arange(64, 96)])
        for kind, j, mode in self.fm:
            if kind == "ki":
                base = self.off["ki"] + np.concatenate([np.arange(64), np.arange(64)])
            else:
                base = self.off[kind] + j * 128 + np.arange(128)
            if mode == "sw":
                base = base[sw128] if kind in ("qb", "kb") else base[sw64]
            idx.append(base)
        idx.append(self.off["va"] + np.arange(self.HW))
        idx.append(self.off["vb"] + np.arange(self.HW))
        idx.append(self.off["wi"] + np.arange(self.IH))
        return np.concatenate(idx)


C_INVF128, C_INVF64, C_SGN128, C_SGN64 = 0, 1, 2, 3
C_IDENT = 8
C_TRIU = C_IDENT + 128
C_MSB = C_TRIU + 128
C_MDSA = C_MSB + 2048
C_NEGTRI = C_MDSA + 128
C_POW2 = C_NEGTRI + 128
C_SEL = C_POW2 + 32
NCONST = C_SEL + 256


def make_consts(cfg):
    c = np.zeros((128, NCONST), np.float32)
    p = np.arange(128)
    c[:, C_INVF128] = 1.0 / (cfg.THETA ** (np.arange(0, 128, 2, dtype=np.float32) / 128.0))[p % 64]
    c[:, C_INVF64] = 1.0 / (cfg.THETA ** (np.arange(0, 64, 2, dtype=np.float32) / 64.0))[p % 32]
    c[:, C_SGN128] = np.where(p < 64, -1.0, 1.0)
    c[:, C_SGN64] = np.where((p % 64) < 32, -1.0, 1.0)
    f = np.arange(128)
    c[:, C_IDENT:C_IDENT + 128] = (p[:, None] == f[None, :])
    c[:, C_TRIU:C_TRIU + 128] = (p[:, None] > f[None, :])
    f5 = np.arange(512)
    for r in range(4):
        c[:, C_MSB + 512 * r:C_MSB + 512 * (r + 1)] = ((128 * r + p[:, None]) < f5[None, :])
    c[:, C_MDSA:C_MDSA + 128] = (p[:, None] <= f[None, :])
    c[:, C_NEGTRI:C_NEGTRI + 128] = np.where(f[None, :] > p[:, None], -BIG, 0.0)
    c[:, C_POW2:C_POW2 + NIT + 1] = (2.0 ** -(np.arange(NIT + 1) + 1.0))[None, :]
    c[0, C_SEL:C_SEL + 128] = 1.0
    c[1, C_SEL + 128:C_SEL + 256] = 1.0
    return c


class Buf:
    __slots__ = ("t", "w", "r", "name")

    def __init__(self, t=None, name=""):
        self.t = t
        self.w = {}
        self.r = {}
        self.name = name

    def __getitem__(self, idx):
        return self.t[idx]


class Ctx:
    ENG = ("pe", "act", "dve", "pool", "sp")

    def __init__(self, nc):
        self.nc = nc
        self.es = contextlib.ExitStack()
        self.h = {"pe": nc.tensor, "act": nc.scalar, "dve": nc.vector,
                  "pool": nc.gpsimd, "sp": nc.sync}
        self.sem, self.cnt = {}, {}
        self.seen = {e: {} for e in self.ENG}
        for e in self.ENG:
            self.sem[e] = self.es.enter_context(nc.semaphore("s_" + e))
            self.cnt[e] = 0
        self.dq = {}
        self.dqi = {}
        for q, n in (("sp", 24), ("pool", 8)):
            self.dq[q] = []
            self.dqi[q] = 0
            for j in range(n):
                k = f"d_{q}{j}"
                self.sem[k] = self.es.enter_context(nc.semaphore("s_" + k))
                self.cnt[k] = 0
                self.dq[q].append(k)
        self.ninst = 0
        self.uid = 0

    def sb(self, stack, name, shape, dt):
        self.uid += 1
        return Buf(stack.enter_context(self.nc.sbuf_tensor(f"{name}_{self.uid}", shape, dt)), name)

    def ps(self, stack, name, shape, dt=F32):
        self.uid += 1
        return Buf(stack.enter_context(self.nc.psum_tensor(f"{name}_{self.uid}", shape, dt)), name)

    def _wait(self, eng, deps):
        seen = self.seen[eng]
        for k, v in deps.items():
            if seen.get(k, 0) < v:
                self.h[eng].wait_ge(self.sem[k], v)
                seen[k] = v
                self.ninst += 1

    def _deps(self, eng, reads, writes):
        deps = {}

        def add(d, skip_same):
            for k, v in d.items():
                if skip_same and k == eng:
                    continue
                if deps.get(k, 0) < v:
                    deps[k] = v
        for b in reads:
            add(b.w, False)
        for b in writes:
            add(b.w, True)
            add(b.r, True)
        return deps

    def op(self, eng, fn, reads=(), writes=()):
        self._wait(eng, self._deps(eng, reads, writes))
        ins = fn(self.h[eng])
        self.cnt[eng] += 1
        ins.then_inc(self.sem[eng], 1)
        self.ninst += 1
        c = self.cnt[eng]
        for b in writes:
            b.w[eng] = c
            b.r = {}
        for b in reads:
            b.r[eng] = c
        return ins

    def dma(self, q, out_ap, in_ap, reads=(), writes=(), **kw):
        k = self.dq[q][self.dqi[q] % len(self.dq[q])]
        self.dqi[q] += 1
        deps = self._deps("dma", reads, writes)
        if self.cnt[k] > 0:
            deps[k] = max(deps.get(k, 0), self.cnt[k])
        self._wait(q, deps)
        ins = self.h[q].dma_start(out=out_ap, in_=in_ap, **kw)
        self.cnt[k] += 16
        ins.then_inc(self.sem[k], 16)
        self.ninst += 1
        c = self.cnt[k]
        for b in writes:
            b.w[k] = c
            b.r = {}
        for b in reads:
            b.r[k] = c

    def barrier(self):
        allk = dict(self.cnt)
        for e in self.ENG:
            self._wait(e, {k: v for k, v in allk.items() if k != e and v > 0})

    def finish(self):
        self.barrier()
        self.es.close()


class Ring:
    def __init__(self, bufs):
        self.b = bufs
        self.i = 0

    def next(self):
        b = self.b[self.i % len(self.b)]
        self.i += 1
        return b


def swpipe(items, stages):
    n, ns = len(items), len(stages)
    for it in range(n + ns - 1):
        for si, st in enumerate(stages):
            k = it - si
            if 0 <= k < n:
                st(items[k])


def build(cfg, layers=None, phases=None, debug_out=()):
    D, S, H, IH, KC, NT, NTB, HW, DFF, NB = cfg.D, cfg.S, cfg.H, cfg.IH, cfg.KC, cfg.NT, cfg.NTB, cfg.HW, cfg.DFF, cfg.NB
    L = cfg.L
    layers = list(range(L)) if layers is None else layers
    nc = bass.Bass("TRN2", target_bir_lowering=False)
    cx = Ctx(nc)

    def din(name, shape, dt=F32):
        return nc.dram_tensor(name, shape, dt, kind="ExternalInput").ap()

    def dscr(name, shape, dt):
        kind = "ExternalOutput" if name in debug_out else "Internal"
        return nc.dram_tensor(name, shape, dt, kind=kind).ap()

    x_in = din("x", [NB, S, D])
    c_in = din("c", [NB, D])
    pos_in = din("positions", [NB, S], I32)
    w_in = din("w_in_ext", [L, D, cfg.NEXT])
    w_a = din("w_a", [L, HW, D])
    w_b = din("w_b", [L, HW, D])
    w_o = din("w_o", [L, D, D])
    w_ada = din("w_ada", [L, D, 6 * D])
    b_ada = din("b_ada", [L, 6 * D])
    g_mix = din("g_mix", [L, D])
    g_ffn = din("g_ffn", [L, D])
    w_up = din("w_up", [L, D, 2 * DFF])
    conv_w = din("conv_w", [L, 3, 2 * DFF])
    conv_b = din("conv_b", [L, 2 * DFF])
    w_down = din("w_down", [L, DFF, D])
    g_final = din("g_final", [D])
    consts = din("consts", [128, NCONST])
    out = nc.dram_tensor("out", [NB, S, D], F32, kind="ExternalOutput").ap()

    xres = dscr("xres", [NB, S, D], F32)
    modrow = dscr("modrow", [NB, 6 * D], F32)
    ropeT = dscr("ropeT", [NB, 4, 128, S], F32)
    qaT = dscr("qaT", [H, 128, S], BF16)
    kaT = dscr("kaT", [H, 128, S], BF16)
    qbT = dscr("qbT", [H, 128, S], BF16)
    kbT = dscr("kbT", [H, 128, S], BF16)
    qiT = dscr("qiT", [IH // 2, 128, S], BF16)
    kiT = dscr("kiT", [1, 128, S], BF16)
    sgaT = dscr("sgaT", [KC, 128, S], BF16)
    sgbT = dscr("sgbT", [KC, 128, S], BF16)
    vab = dscr("vab", [S, 2 * HW], BF16)
    wis = dscr("wis", [S, IH], F32)
    yabT = dscr("yabT", [2, H, 128, S], BF16)
    fm_dst = {"qa": qaT, "ka": kaT, "qb": qbT, "kb": kbT, "qi": qiT, "ki": kiT, "ga": sgaT, "gb": sgbT}

    dbufs = {}

    def db(key):
        if key not in dbufs:
            dbufs[key] = Buf(None, str(key))
        return dbufs[key]

    G = contextlib.ExitStack()
    cst = cx.sb(G, "cst", [128, NCONST], F32)
    cx.dma("sp", cst[:], consts, writes=[cst])
    ident_bf = cx.sb(G, "ident_bf", [128, 128], BF16)
    ident_f = Buf(cst.t, "identf")
    triU_bf = cx.sb(G, "triU_bf", [128, 128], BF16)
    ones_bf = cx.sb(G, "ones_bf", [128, 128], BF16)
    msb_bf = cx.sb(G, "msb_bf", [128, 2048], BF16)
    mdsa_bf = cx.sb(G, "mdsa_bf", [128, 128], BF16)
    cx.op("dve", lambda e: e.tensor_copy(out=ident_bf[:], in_=cst[:, C_IDENT:C_IDENT + 128]), reads=[cst], writes=[ident_bf])
    cx.op("dve", lambda e: e.tensor_copy(out=triU_bf[:], in_=cst[:, C_TRIU:C_TRIU + 128]), reads=[cst], writes=[triU_bf])
    cx.op("dve", lambda e: e.memset(ones_bf[:], 1.0), writes=[ones_bf])
    cx.op("dve", lambda e: e.tensor_copy(out=msb_bf[:], in_=cst[:, C_MSB:C_MSB + 2048]), reads=[cst], writes=[msb_bf])
    cx.op("dve", lambda e: e.tensor_copy(out=mdsa_bf[:], in_=cst[:, C_MDSA:C_MDSA + 128]), reads=[cst], writes=[mdsa_bf])
    cactT = cx.sb(G, "cactT", [128, KC, NB], F32)
    vecs = cx.sb(G, "vecs", [128, NB, 6, KC], F32)
    cwT = cx.sb(G, "cwT", [128, 4, 2 * DFF // 128], F32)
    eps_t = cx.sb(G, "eps_t", [128, 1], F32)
    cx.op("dve", lambda e: e.memset(eps_t[:], cfg.EPS), writes=[eps_t])
    negpi = cx.sb(G, "negpi", [128, 1], F32)
    cx.op("dve", lambda e: e.memset(negpi[:], -float(np.pi)), writes=[negpi])
    one_t = cx.sb(G, "one_t", [128, 1], F32)
    cx.op("dve", lambda e: e.memset(one_t[:], 1.0), writes=[one_t])

    def phase_setup():
        with contextlib.ExitStack() as st:
            crow = cx.sb(st, "crow", [NB, D], F32)
            cx.dma("sp", crow[:], c_in, writes=[crow])
            crs = cx.sb(st, "crs", [NB, D], F32)
            cx.op("act", lambda e: e.activation(out=crs[:], in_=crow[:], func=AF.Silu), reads=[crow], writes=[crs])
            pt = cx.ps(st, "pt", [128, KC, NB], F32)
            for kc in range(KC):
                cx.op("pe", lambda e, kc=kc: e.transpose(out=pt[:, kc, :], in_=crs[:, kc * 128:(kc + 1) * 128], identity=ident_f[0:NB, C_IDENT:C_IDENT + NB]), reads=[crs, cst], writes=[pt])
            cx.op("dve", lambda e: e.tensor_copy(out=cactT[:], in_=pt[:]), reads=[pt], writes=[cactT])
            posi = cx.sb(st, "posi", [128, S], I32)
            posf = cx.sb(st, "posf", [128, S], F32)
            ang = cx.sb(st, "ang", [128, S], F32)
            a2 = cx.sb(st, "a2", [128, S], F32)
            ki_ = cx.sb(st, "ki_", [128, S], I32)
            kf = cx.sb(st, "kf", [128, S], F32)
            m = cx.sb(st, "m", [128, S], F32)
            tbr = Ring([cx.sb(st, f"tb_{i}", [128, S], F32) for i in range(2)])
            for b in range(NB):
                cx.dma("sp", posi[:], pos_in[b:b + 1, :].broadcast_to([128, S]), writes=[posi])
                cx.op("dve", lambda e: e.tensor_copy(out=posf[:], in_=posi[:]), reads=[posi], writes=[posf])
                for ti, (ccol, scol) in enumerate(((C_INVF128, C_SGN128), (C_INVF64, C_SGN64))):
                    cx.op("dve", lambda e: e.tensor_scalar(out=ang[:], in0=posf[:], scalar1=cst[:, ccol:ccol + 1], scalar2=None, op0=ALU.mult), reads=[posf, cst], writes=[ang])
                    for which in range(2):
                        shift = float(np.pi / 2) if which == 0 else 0.0
                        cx.op("dve", lambda e: e.tensor_scalar(out=ki_[:], in0=ang[:], scalar1=shift, scalar2=float(1.0 / (2 * np.pi)), op0=ALU.add, op1=ALU.mult), reads=[ang], writes=[ki_])
                        cx.op("dve", lambda e: e.tensor_copy(out=kf[:], in_=ki_[:]), reads=[ki_], writes=[kf])
                        cx.op("dve", lambda e: e.scalar_tensor_tensor(out=a2[:], in0=kf[:], scalar=-float(2 * np.pi), in1=ang[:], op0=ALU.mult, op1=ALU.add), reads=[kf, ang], writes=[a2])
                        if which == 0:
                            cx.op("dve", lambda e: e.tensor_scalar(out=a2[:], in0=a2[:], scalar1=shift, scalar2=None, op0=ALU.add), reads=[a2], writes=[a2])
                        cx.op("dve", lambda e: e.tensor_scalar(out=m[:], in0=a2[:], scalar1=float(np.pi), scalar2=-float(2 * np.pi), op0=ALU.is_gt, op1=ALU.mult), reads=[a2], writes=[m])
                        cx.op("dve", lambda e: e.tensor_tensor(out=a2[:], in0=a2[:], in1=m[:], op=ALU.add), reads=[a2, m], writes=[a2])
                        cx.op("dve", lambda e: e.tensor_scalar(out=m[:], in0=a2[:], scalar1=-float(np.pi), scalar2=float(2 * np.pi), op0=ALU.is_lt, op1=ALU.mult), reads=[a2], writes=[m])
                        cx.op("dve", lambda e: e.tensor_tensor(out=a2[:], in0=a2[:], in1=m[:], op=ALU.add), reads=[a2, m], writes=[a2])
                        cx.op("dve", lambda e: e.tensor_scalar(out=a2[:], in0=a2[:], scalar1=-3.1415925, scalar2=3.1415925, op0=ALU.max, op1=ALU.min), reads=[a2], writes=[a2])
                        tb_ = tbr.next()
                        cx.op("act", lambda e: e.activation(out=tb_[:], in_=a2[:], func=AF.Sin), reads=[a2], writes=[tb_])
                        if which == 1:
                            cx.op("dve", lambda e: e.tensor_scalar(out=tb_[:], in0=tb_[:], scalar1=cst[:, scol:scol + 1], scalar2=None, op0=ALU.mult), reads=[tb_, cst], writes=[tb_])
                        cx.dma("sp", ropeT[b, 2 * ti + which], tb_[:], reads=[tb_], writes=[db("ropeT")])
        cx.barrier()

    def load_fm_vec(st, dst_ap_fn, src_rows_ap, nrows, pst, tmp, reads_extra=()):
        cx.dma("sp", tmp[0:nrows, :], src_rows_ap, writes=[tmp], reads=list(reads_extra))
        cx.op("pe", lambda e: e.transpose(out=pst[:, 0:nrows], in_=tmp[0:nrows, :], identity=ident_f[0:nrows, C_IDENT:C_IDENT + nrows]), reads=[tmp, cst], writes=[pst])
        dst_ap_fn(pst)

    def phase_mod(l):
        NCH = 6 * D // 512
        with contextlib.ExitStack() as st:
            wr = Ring([cx.sb(st, f"wada{i}", [128, KC, 512], BF16) for i in range(2)])
            cab = cx.sb(st, "cab", [128, KC, NB], BF16)
            cx.op("dve", lambda e: e.tensor_copy(out=cab[:], in_=cactT[:]), reads=[cactT], writes=[cab])
            pr = Ring([cx.ps(st, f"pmod{i}", [NB, 512], F32) for i in range(2)])
            br = Ring([cx.sb(st, f"bada{i}", [NB, 512], F32) for i in range(2)])
            sr = Ring([cx.sb(st, f"smod{i}", [NB, 512], F32) for i in range(2)])
            wv = w_ada[l].rearrange("(kc p) n -> p kc n", p=128)
            for ch in range(NCH):
                wt = wr.next()
                cx.dma("pool", wt[:], wv[:, :, ch * 512:(ch + 1) * 512], writes=[wt])
                bt = br.next()
                cx.dma("sp", bt[:], b_ada[l:l + 1, ch * 512:(ch + 1) * 512].broadcast_to([NB, 512]), writes=[bt])
                ps = pr.next()
                for kc in range(KC):
                    cx.op("pe", lambda e, kc=kc: e.matmul(ps[:], cab[:, kc, :], wt[:, kc, :], start=(kc == 0), stop=(kc == KC - 1)), reads=[cab, wt], writes=[ps])
                sm = sr.next()
                cx.op("dve", lambda e: e.tensor_tensor(out=sm[:], in0=ps[:], in1=bt[:], op=ALU.add), reads=[ps, bt], writes=[sm])
                cx.dma("sp", modrow[:, ch * 512:(ch + 1) * 512], sm[:], reads=[sm], writes=[db("modrow")])
            tmp = cx.sb(st, "fmtmp", [128, 128], F32)
            pst = cx.ps(st, "fmps", [128, 128], F32)
            gm = cx.sb(st, "gm", [128, 2, KC], F32)
            for i, gsrc in enumerate((g_mix, g_ffn)):
                load_fm_vec(st, lambda p, i=i: cx.op("dve", lambda e: e.tensor_copy(out=gm[:, i, :], in_=p[:, 0:KC]), reads=[p], writes=[gm]),
                            gsrc[l].rearrange("(j p) -> j p", p=128), KC, pst, tmp)
            mt = cx.sb(st, "mt", [128, 6 * KC], F32)
            for b in range(NB):
                load_fm_vec(st, lambda p: cx.op("dve", lambda e: e.tensor_copy(out=mt[:], in_=p[:, 0:6 * KC]), reads=[p], writes=[mt]),
                            modrow[b].rearrange("(j p) -> j p", p=128), 6 * KC, pst, tmp, reads_extra=[db("modrow")])
                for s_, (ish, isc) in enumerate(((0, 1), (3, 4))):
                    cx.op("dve", lambda e, s_=s_, isc=isc: e.scalar_tensor_tensor(out=vecs[:, b, 3 * s_, :], in0=mt[:, isc * KC:(isc + 1) * KC], scalar=1.0, in1=gm[:, s_, :], op0=ALU.add, op1=ALU.mult), reads=[mt, gm], writes=[vecs])
                    cx.op("dve", lambda e, s_=s_, ish=ish: e.tensor_copy(out=vecs[:, b, 3 * s_ + 1, :], in_=mt[:, ish * KC:(ish + 1) * KC]), reads=[mt], writes=[vecs])
            NFC = 2 * DFF // 128
            for i in range(4):
                src = conv_w[l, i] if i < 3 else conv_b[l]
                load_fm_vec(st, lambda p, i=i: cx.op("dve", lambda e: e.tensor_copy(out=cwT[:, i, :], in_=p[:, 0:NFC]), reads=[p], writes=[cwT]),
                            src.rearrange("(j p) -> j p", p=128), NFC, pst, tmp)
        cx.barrier()

    def phase_norm(st, xsrc, b, t0, ntiles, which, hT, hbufs, nps=2):
        xr = Ring([cx.sb(st, f"nx{i}", [128, D], F32) for i in range(2)])
        jr = cx.sb(st, "njunk", [128, D], BF16)
        xnr = Ring([cx.sb(st, f"nxn{i}", [128, D], BF16) for i in range(2)])
        ssr = Ring([cx.sb(st, f"nss{i}", [128, 4], F32) for i in range(2)])
        NPB = max(1, (KC * 128 * 2) // 2048)
        ptr = Ring([cx.ps(st, f"npt{i}", [128, KC, 128], BF16) for i in range(nps)])
        for ti in range(ntiles):
            tt = t0 + ti
            xt = xr.next()
            cx.dma("sp", xt[:], xsrc[b, tt * 128:(tt + 1) * 128, :], reads=[db(("x", b, tt))], writes=[xt])
            ss = ssr.next()
            cx.op("act", lambda e: e.activation(out=jr[:], in_=xt[:], func=AF.Square, accum_out=ss[:, 0:1]), reads=[xt], writes=[jr, ss])
            cx.op("act", lambda e: e.activation(out=ss[:, 1:2], in_=ss[:, 0:1], func=AF.Ln, scale=1.0 / D, bias=eps_t[:]), reads=[ss, eps_t], writes=[ss])
            cx.op("act", lambda e: e.activation(out=ss[:, 2:3], in_=ss[:, 1:2], func=AF.Exp, scale=-0.5), reads=[ss], writes=[ss])
            xn = xnr.next()
            cx.op("dve", lambda e: e.tensor_scalar(out=xn[:], in0=xt[:], scalar1=ss[:, 2:3], scalar2=None, op0=ALU.mult), reads=[xt, ss], writes=[xn])
            pt = ptr.next()
            for kc in range(KC):
                cx.op("pe", lambda e, kc=kc: e.transpose(out=pt[:, kc, :], in_=xn[:, kc * 128:(kc + 1) * 128], identity=ident_bf[:]), reads=[xn, ident_bf], writes=[pt])
            hb = hbufs[ti]
            eng = "dve" if ti % 2 == 0 else "act"
            for kc in range(KC):
                o_ = hT[:, kc, ti * 128:(ti + 1) * 128]
                a_ = vecs[:, b, 3 * which, kc:kc + 1]
                b_ = vecs[:, b, 3 * which + 1, kc:kc + 1]
                if eng == "dve":
                    cx.op("dve", lambda e, o_=o_, a_=a_, b_=b_, kc=kc: e.tensor_scalar(out=o_, in0=pt[:, kc, :], scalar1=a_, scalar2=b_, op0=ALU.mult, op1=ALU.add), reads=[pt, vecs], writes=[hb])
                else:
                    cx.op("act", lambda e, o_=o_, a_=a_, b_=b_, kc=kc: e.activation(out=o_, in_=pt[:, kc, :], func=AF.Identity, scale=a_, bias=b_), reads=[pt, vecs], writes=[hb])

    def phase_proj(l, b, hT, hbufs):
        with contextlib.ExitStack() as st:
            rope = cx.sb(st, "rope", [128, 4, S], F32)
            cx.dma("sp", rope[:], ropeT[b].rearrange("f p s -> p f s"), reads=[db("ropeT")], writes=[rope])
            wr = Ring([cx.sb(st, f"pw{i}", [128, KC, 512], BF16) for i in range(2)])
            pr = Ring([cx.ps(st, f"pp{i}", [128, 512], F32) for i in range(8)])
            sr = Ring([cx.sb(st, f"pstg{i}", [128, 512], BF16) for i in range(6)])
            t1r = Ring([cx.sb(st, f"pt1{i}", [128, 512], F32) for i in range(2)])
            t2r = Ring([cx.sb(st, f"pt2{i}", [128, 512], F32) for i in range(2)])
            wv = w_in[l].rearrange("(kc p) n -> p kc n", p=128)
            nfm = len(cfg.fm)
            ci = 0
            while ci < nfm:
                ncw = min(4, nfm - ci)
                wt = wr.next()
                cx.dma("pool", wt[:, :, 0:ncw * 128], wv[:, :, ci * 128:(ci + ncw) * 128], writes=[wt])
                for tb in range(NTB):
                    hb = hbufs[tb * 4:(tb + 1) * 4]
                    cs = slice(tb * 512, (tb + 1) * 512)
                    u = 0
                    while u < ncw:
                        kind, j, mode = cfg.fm[ci + u]
                        dst = fm_dst[kind]
                        if mode is None:
                            ps = pr.next()
                            for kc in range(KC):
                                cx.op("pe", lambda e, kc=kc, u=u: e.matmul(ps[:], wt[:, kc, u * 128:(u + 1) * 128], hT[:, kc, cs], start=(kc == 0), stop=(kc == KC - 1)), reads=[wt] + hb, writes=[ps])
                            sg = sr.next()
                            if kind in ("ga", "gb"):
                                cx.op("act", lambda e: e.activation(out=sg[:], in_=ps[:], func=AF.Sigmoid), reads=[ps], writes=[sg])
                            else:
                                cx.op("act", lambda e: e.activation(out=sg[:], in_=ps[:], func=AF.Copy), reads=[ps], writes=[sg])
                            cx.dma("sp", dst[j, :, cs], sg[:], reads=[sg], writes=[db(kind)])
                            u += 1
                        else:
                            ps = pr.next()
                            for kc in range(KC):
                                cx.op("pe", lambda e, kc=kc, u=u: e.matmul(ps[:], wt[:, kc, u * 128:(u + 1) * 128], hT[:, kc, cs], start=(kc == 0), stop=(kc == KC - 1)), reads=[wt] + hb, writes=[ps])
                            f0 = 0 if mode == "r128" else 2
                            hw_ = 64 if mode == "r128" else 32
                            t1, t2 = t1r.next(), t2r.next()
                            cx.op("dve", lambda e: e.tensor_tensor(out=t1[:], in0=ps[:], in1=rope[:, f0, cs], op=ALU.mult), reads=[ps, rope], writes=[t1])
                            for p0 in range(0, 128, 2 * hw_):
                                lo, hi = slice(p0, p0 + hw_), slice(p0 + hw_, p0 + 2 * hw_)
                                cx.op("dve", lambda e, lo=lo, hi=hi: e.tensor_tensor(out=t2[lo, :], in0=ps[hi, :], in1=rope[lo, f0 + 1, cs], op=ALU.mult), reads=[ps, rope], writes=[t2])
                                cx.op("dve", lambda e, lo=lo, hi=hi: e.tensor_tensor(out=t2[hi, :], in0=ps[lo, :], in1=rope[hi, f0 + 1, cs], op=ALU.mult), reads=[ps, rope], writes=[t2])
                            sg = sr.next()
                            cx.op("dve", lambda e: e.tensor_tensor(out=sg[:], in0=t1[:], in1=t2[:], op=ALU.add), reads=[t1, t2], writes=[sg])
                            cx.dma("sp", dst[j, :, cs], sg[:], reads=[sg], writes=[db(kind)])
                            u += 1
                ci += ncw
            c0 = cfg.NFM
            ntm = 2 * HW
            cc = 0
            while cc < ntm:
                wt = wr.next()
                cx.dma("pool", wt[:], wv[:, :, c0 + cc:c0 + cc + 512], writes=[wt])
                for tt in range(NT):
                    ps = pr.next()
                    for kc in range(KC):
                        cx.op("pe", lambda e, kc=kc: e.matmul(ps[:], hT[:, kc, tt * 128:(tt + 1) * 128], wt[:, kc, :], start=(kc == 0), stop=(kc == KC - 1)), reads=[wt, hbufs[tt]], writes=[ps])
                    sg = sr.next()
                    cx.op("act", lambda e: e.activation(out=sg[:], in_=ps[:], func=AF.Copy), reads=[ps], writes=[sg])
                    cx.dma("sp", vab[tt * 128:(tt + 1) * 128, cc:cc + 512], sg[:], reads=[sg], writes=[db("vab")])
                cc += 512
            wt = wr.next()
            cx.dma("pool", wt[:, :, 0:IH], wv[:, :, c0 + ntm:c0 + ntm + IH], writes=[wt])
            wst = Ring([cx.sb(st, f"pwi{i}", [128, IH], F32) for i in range(2)])
            for tt in range(NT):
                ps = pr.next()
                for kc in range(KC):
                    cx.op("pe", lambda e, kc=kc: e.matmul(ps[:, 0:IH], hT[:, kc, tt * 128:(tt + 1) * 128], wt[:, kc, 0:IH], start=(kc == 0), stop=(kc == KC - 1)), reads=[wt, hbufs[tt]], writes=[ps])
                ws = wst.next()
                cx.op("dve", lambda e: e.tensor_copy(out=ws[:], in_=ps[:, 0:IH]), reads=[ps], writes=[ws])
                cx.dma("sp", wis[tt * 128:(tt + 1) * 128, :], ws[:], reads=[ws], writes=[db("wis")])
        cx.barrier()

    def phase_sb():
        scale = 128.0 ** -0.5
        with contextlib.ExitStack() as st:
            qr = Ring([cx.sb(st, f"sq{i}", [128, S], BF16) for i in range(2)])
            kr = Ring([cx.sb(st, f"sk{i}", [128, S], BF16) for i in range(2)])
            vr = Ring([cx.sb(st, f"sv{i}", [128, NT, 128], BF16) for i in range(2)])
            zr = Ring([cx.ps(st, f"sz{i}", [128, 512], F32) for i in range(3)])
            ar = Ring([cx.ps(st, f"sa{i}", [128, 512], F32) for i in range(3)])
            yr = Ring([cx.ps(st, f"sy{i}", [128, 512], F32) for i in range(2)])
            er = Ring([cx.sb(st, f"se{i}", [128, 512], F32) for i in range(3)])
            spr = Ring([cx.sb(st, f"ssp{i}", [128, 512], F32) for i in range(4)])
            spbr = Ring([cx.sb(st, f"sspb{i}", [128, 512], BF16) for i in range(4)])
            lsr = Ring([cx.sb(st, f"sls{i}", [128, 512], F32) for i in range(4)])
            lbr = Ring([cx.sb(st, f"slb{i}", [128, 512], BF16) for i in range(4)])
            lgr = Ring([cx.sb(st, f"slg{i}", [128, 512], F32) for i in range(4)])
            agr = Ring([cx.sb(st, f"sag{i}", [128, 512], F32) for i in range(3)])
            wtr = Ring([cx.sb(st, f"swt{i}", [128, 512], BF16) for i in range(4)])
            ysr = Ring([cx.sb(st, f"sys{i}", [128, 512], BF16) for i in range(2)])
            items = []
            for h in range(H):
                for g in range(NTB):
                    nblk = 4 * g + 4
                    for bi, sbk in enumerate(range(nblk - 1, -1, -1)):
                        items.append(dict(h=h, g=g, bi=bi, sbk=sbk, nblk=nblk))
            hs, gs = {}, {}

            def stA(it):
                h, g, bi, sbk = it["h"], it["g"], it["bi"], it["sbk"]
                cs = slice(g * 512, (g + 1) * 512)
                if g == 0 and bi == 0:
                    q, k, v = qr.next(), kr.next(), vr.next()
                    cx.dma("sp", q[:], qaT[h], reads=[db("qa")], writes=[q])
                    cx.dma("sp", k[:], kaT[h], reads=[db("ka")], writes=[k])
                    cx.dma("sp", v[:], vab[:, h * 128:(h + 1) * 128].rearrange("(st p) d -> p st d", p=128), reads=[db("vab")], writes=[v])
                    hs[h] = (q, k, v)
                q, k, v = hs[h]
                if bi == 0:
                    gs[(h, g)] = dict(yp=yr.next(), lsum=None)
                G_ = gs[(h, g)]
                r = sbk - 4 * g
                it["r"] = r
                zp = zr.next()
                cx.op("pe", lambda e: e.matmul(zp[:], k[:, sbk * 128:(sbk + 1) * 128], q[:, cs], start=True, stop=True), reads=[k, q], writes=[zp])
                e_ = er.next()
                cx.op("act", lambda e: e.activation(out=e_[:], in_=zp[:], func=AF.Exp, scale=scale), reads=[zp], writes=[e_])
                sp = spr.next()
                cx.op("act", lambda e: e.activation(out=sp[:], in_=e_[:], func=AF.Ln, bias=one_t[:]), reads=[e_, one_t], writes=[sp])
                lg = lgr.next()
                cx.op("dve", lambda e: e.scalar_tensor_tensor(out=lg[:], in0=zp[:], scalar=scale, in1=sp[:], op0=ALU.mult, op1=ALU.subtract), reads=[zp, sp], writes=[lg])
                it["lg"] = lg
                spb = spbr.next()
                if r >= 0:
                    cx.op("dve", lambda e: e.tensor_tensor(out=spb[:], in0=sp[:], in1=msb_bf[:, r * 512:(r + 1) * 512], op=ALU.mult), reads=[sp, msb_bf], writes=[spb])
                else:
                    cx.op("dve", lambda e: e.tensor_copy(out=spb[:], in_=sp[:]), reads=[sp], writes=[spb])
                it["spb"] = spb
                lsum_prev = G_["lsum"]
                it["lb"] = None
                if lsum_prev is not None:
                    lb = lbr.next()
                    if bi % 2 == 0:
                        cx.op("act", lambda e: e.activation(out=lb[:], in_=lsum_prev[:], func=AF.Copy), reads=[lsum_prev], writes=[lb])
                    else:
                        cx.op("pool", lambda e: e.tensor_copy(out=lb[:], in_=lsum_prev[:]), reads=[lsum_prev], writes=[lb])
                    it["lb"] = lb
                if sbk > 0:
                    ls = lsr.next()
                    if lsum_prev is None:
                        cx.op("pool", lambda e: e.tensor_copy(out=ls[:], in_=spb[:]), reads=[spb], writes=[ls])
                    else:
                        cx.op("pool", lambda e: e.tensor_tensor(out=ls[:], in0=lsum_prev[:], in1=spb[:], op=ALU.add), reads=[lsum_prev, spb], writes=[ls])
                    G_["lsum"] = ls

            def stB(it):
                r, spb, lb, lg = it["r"], it["spb"], it["lb"], it["lg"]
                ap_ = ar.next()
                cx.op("pe", lambda e: e.matmul(ap_[:], triU_bf[:], spb[:], start=True, stop=(lb is None)), reads=[triU_bf, spb], writes=[ap_])
                if lb is not None:
                    cx.op("pe", lambda e: e.matmul(ap_[:], ones_bf[:], lb[:], start=False, stop=True), reads=[ones_bf, lb], writes=[ap_])
                ag = agr.next()
                cx.op("dve", lambda e: e.tensor_tensor(out=ag[:], in0=lg[:], in1=ap_[:], op=ALU.subtract), reads=[lg, ap_], writes=[ag])
                wt_ = wtr.next()
                cx.op("act", lambda e: e.activation(out=wt_[:], in_=ag[:], func=AF.Exp), reads=[ag], writes=[wt_])
                if r >= 0:
                    cx.op("pool", lambda e: e.tensor_tensor(out=wt_[:], in0=wt_[:], in1=msb_bf[:, r * 512:(r + 1) * 512], op=ALU.mult), reads=[wt_, msb_bf], writes=[wt_])
                it["wt"] = wt_

            def stC(it):
                h, g, bi, sbk, nblk = it["h"], it["g"], it["bi"], it["sbk"], it["nblk"]
                cs = slice(g * 512, (g + 1) * 512)
                q, k, v = hs[h]
                yp = gs[(h, g)]["yp"]
                wt_ = it["wt"]
                cx.op("pe", lambda e: e.matmul(yp[:], v[:, sbk, :], wt_[:], start=(bi == 0), stop=(bi == nblk - 1)), reads=[v, wt_], writes=[yp])
                if bi == nblk - 1:
                    ys = ysr.next()
                    cx.op("act", lambda e: e.activation(out=ys[:], in_=yp[:], func=AF.Copy), reads=[yp], writes=[ys])
                    cx.dma("sp", yabT[0, h, :, cs], ys[:], reads=[ys], writes=[db("yab")])

            swpipe(items, [stA, stB, stC])
        cx.barrier()

    def phase_dsa():
        scale = 128.0 ** -0.5
        K = float(cfg.TOPK)
        with contextlib.ExitStack() as st:
            ki = cx.sb(st, "dki", [128, S], BF16)
            wi = cx.sb(st, "dwi", [128, NT, IH], F32)
            cx.dma("sp", ki[:], kiT[0], reads=[db("ki")], writes=[ki])
            cx.dma("sp", wi[:], wis.rearrange("(t p) h -> p t h", p=128), reads=[db("wis")], writes=[wi])
            MTs = [cx.sb(st, f"dMT{i}", [128, NT, 512], BF16) for i in range(2)]
            TS = []
            for z in range(4):
                TS.append(dict(
                    qi=cx.sb(st, f"dqi{z}", [128, IH // 2, 128], BF16),
                    diag=cx.sb(st, f"ddiag{z}", [128, IH, 128], BF16),
                    score=cx.sb(st, f"dscore{z}", [128, S], F32),
                    junk=cx.sb(st, f"djunk{z}", [128, S], BF16),
                    mask=cx.sb(st, f"dmask{z}", [128, S], BF16),
                    sm=cx.sb(st, f"dsm{z}", [128, 8], F32),
                    wtab=cx.sb(st, f"dwtab{z}", [128, NIT + 1], F32)))
            rr = Ring([cx.sb(st, f"drelu{i}", [128, 512], BF16) for i in range(4)])
            dpr = Ring([cx.ps(st, f"ddp{i}", [128, 512], F32) for i in range(4)])
            spr = Ring([cx.ps(st, f"dsp{i}", [128, 512], F32) for i in range(1)])
            tpr = Ring([cx.ps(st, f"dtp{i}", [128, 8, 128], BF16) for i in range(1)])
            lpr = dpr
            ypr = Ring([cx.ps(st, f"dyp{i}", [128, 512], F32) for i in range(1)])
            npr = Ring([cx.ps(st, f"dnp{i}", [128, 512], F32) for i in range(1)])
            qr = Ring([cx.sb(st, f"dq{i}", [128, 512], BF16) for i in range(2)])
            kr = Ring([cx.sb(st, f"dk{i}", [128, S], BF16) for i in range(2)])
            vr = Ring([cx.sb(st, f"dv{i}", [128, NT, 128], BF16) for i in range(2)])
            ebr = Ring([cx.sb(st, f"deb{i}", [128, 512], BF16) for i in range(4)])
            ptr_ = Ring([cx.sb(st, f"dpt{i}", [128, 512], BF16) for i in range(4)])
            yer = Ring([cx.sb(st, f"dye{i}", [128, 512], F32) for i in range(4)])
            ner = Ring([cx.sb(st, f"dne{i}", [128, 512], F32) for i in range(4)])
            ysr = Ring([cx.sb(st, f"dys{i}", [128, 512], BF16) for i in range(2)])

            pend = {}

            def idx_unit(g, half):
                MT = MTs[g % 2]
                if half == 0:
                    cx.op("pool", lambda e: e.memset(MT[:, 0:4 * g + 4, :], 0.0), writes=[MT])
                tiles = []
                for z in range(2):
                    il = 2 * half + z
                    i = 4 * g + il
                    tcs = slice(il * 128, (il + 1) * 128)
                    if i < 2:
                        for sbk in range(i + 1):
                            src = mdsa_bf if sbk == i else ones_bf
                            cx.op("pool", lambda e, sbk=sbk, src=src: e.tensor_copy(out=MT[:, sbk, tcs], in_=src[:]), reads=[src], writes=[MT])
                        continue
                    T = TS[2 * ((2 * g + half) % 2) + z]
                    tiles.append((T, i, tcs))
                    W = 128 * (i + 1)
                    qi, diag, score = T["qi"], T["diag"], T["score"]
                    cx.dma("sp", qi[:], qiT[:, :, i * 128:(i + 1) * 128].rearrange("j p s -> p j s"), reads=[db("qi")], writes=[qi])
                    for hh in range(IH):
                        cx.op("act", lambda e, hh=hh: e.activation(out=diag[:, hh, :], in_=ident_bf[:], func=AF.Copy, scale=wi[:, i, hh:hh + 1]), reads=[ident_bf, wi], writes=[diag])
                    nkc = (W + 511) // 512
                    for c in range(nkc):
                        cw = min(512, W - 512 * c)
                        sp_ = spr.next()

                        def iA(hh, c=c, cw=cw, qi=qi):
                            pb = 64 * (hh["hh"] % 2)
                            j = hh["hh"] // 2
                            dp = dpr.next()
                            cx.op("pe", lambda e: e.matmul(dp[:, 0:cw], qi[pb:pb + 64, j, :], ki[pb:pb + 64, c * 512:c * 512 + cw], start=True, stop=True), reads=[qi, ki], writes=[dp])
                            rl = rr.next()
                            cx.op("act", lambda e: e.activation(out=rl[:, 0:cw], in_=dp[:, 0:cw], func=AF.Relu), reads=[dp], writes=[rl])
                            hh["rl"] = rl

                        def iB(hh, cw=cw, sp_=sp_, diag=diag):
                            h_ = hh["hh"]
                            rl = hh["rl"]
                            cx.op("pe", lambda e: e.matmul(sp_[:, 0:cw], diag[:, h_, :], rl[:, 0:cw], start=(h_ == 0), stop=(h_ == IH - 1)), reads=[diag, rl], writes=[sp_])

                        swpipe([dict(hh=hh) for hh in range(IH)], [iA, lambda x: None, iB])
                        cx.op("act", lambda e: e.activation(out=score[:, c * 512:c * 512 + cw], in_=sp_[:, 0:cw], func=AF.Copy), reads=[sp_], writes=[score])
                pend[(g, half)] = tiles
                if not tiles:
                    return
                for T, i, tcs in tiles:
                    W = 128 * (i + 1)
                    s_, score, wtab = T["sm"], T["score"], T["wtab"]
                    cx.op("dve", lambda e: e.tensor_reduce(out=s_[:, 0:1], in_=score[:, 0:W], axis=AX.X, op=ALU.max), reads=[score], writes=[s_])
                    cx.op("dve", lambda e: e.tensor_reduce(out=s_[:, 1:2], in_=score[:, 0:W], axis=AX.X, op=ALU.min), reads=[score], writes=[s_])
                for T, i, tcs in tiles:
                    s_, score, wtab = T["sm"], T["score"], T["wtab"]
                    cx.op("dve", lambda e: e.tensor_tensor(out=s_[:, 2:3], in0=s_[:, 0:1], in1=s_[:, 1:2], op=ALU.subtract), reads=[s_], writes=[s_])
                    cx.op("dve", lambda e: e.tensor_tensor(out=score[:, i * 128:(i + 1) * 128], in0=score[:, i * 128:(i + 1) * 128], in1=cst[:, C_NEGTRI:C_NEGTRI + 128], op=ALU.add), reads=[score, cst], writes=[score])
                for T, i, tcs in tiles:
                    s_, wtab = T["sm"], T["wtab"]
                    cx.op("dve", lambda e: e.tensor_scalar(out=wtab[:], in0=cst[:, C_POW2:C_POW2 + NIT + 1], scalar1=s_[:, 2:3], scalar2=None, op0=ALU.mult), reads=[cst, s_], writes=[wtab])
                for zi, (T, i, tcs) in enumerate(tiles):
                    s_, wtab = T["sm"], T["wtab"]
                    cx.op("dve", lambda e: e.tensor_tensor(out=s_[:, 3:4], in0=s_[:, 1:2], in1=wtab[:, 0:1], op=ALU.add), reads=[s_, wtab], writes=[s_])
                    if zi == 1:
                        cx.op("dve", lambda e: e.tensor_scalar(out=s_[:, 7:8], in0=s_[:, 3:4], scalar1=-1.0, scalar2=None, op0=ALU.mult), reads=[s_], writes=[s_])
                for it in range(NIT):
                    for zi, (T, i, tcs) in enumerate(tiles):
                        W = 128 * (i + 1)
                        s_, score, junk = T["sm"], T["score"], T["junk"]
                        if zi == 0:
                            cx.op("dve", lambda e: e.tensor_scalar(out=junk[:, 0:W], in0=score[:, 0:W], scalar1=s_[:, 3:4], scalar2=None, op0=ALU.is_ge, op1=ALU.add, accum_out=s_[:, 4:5]), reads=[score, s_], writes=[junk, s_])
                        else:
                            cx.op("act", lambda e: e.activation(out=junk[:, 0:W], in_=score[:, 0:W], func=AF.Sign, bias=s_[:, 7:8], scale=1.0, accum_out=s_[:, 4:5]), reads=[score, s_], writes=[junk, s_])
                    for zi, (T, i, tcs) in enumerate(tiles):
                        W = 128 * (i + 1)
                        s_, wtab = T["sm"], T["wtab"]
                        thr_c = (K - 0.5) if zi == 0 else (2.0 * K - W - 0.5)
                        cx.op("dve", lambda e: e.scalar_tensor_tensor(out=s_[:, 5:6], in0=s_[:, 4:5], scalar=thr_c, in1=wtab[:, it:it + 1], op0=ALU.is_ge, op1=ALU.mult), reads=[s_, wtab], writes=[s_])
                    for zi, (T, i, tcs) in enumerate(tiles):
                        s_, wtab = T["sm"], T["wtab"]
                        cx.op("dve", lambda e: e.scalar_tensor_tensor(out=s_[:, 3:4], in0=s_[:, 5:6], scalar=wtab[:, it + 1:it + 2], in1=s_[:, 3:4], op0=ALU.subtract, op1=ALU.add), reads=[s_, wtab], writes=[s_])
                        if zi == 1:
                            cx.op("dve", lambda e: e.tensor_scalar(out=s_[:, 7:8], in0=s_[:, 3:4], scalar1=-1.0, scalar2=None, op0=ALU.mult), reads=[s_], writes=[s_])
                for T, i, tcs in tiles:
                    s_, wtab = T["sm"], T["wtab"]
                    cx.op("dve", lambda e: e.tensor_tensor(out=s_[:, 6:7], in0=s_[:, 3:4], in1=wtab[:, NIT:NIT + 1], op=ALU.subtract), reads=[s_, wtab], writes=[s_])
                for T, i, tcs in tiles:
                    W = 128 * (i + 1)
                    s_, score, mask = T["sm"], T["score"], T["mask"]
                    cx.op("dve", lambda e: e.tensor_scalar(out=mask[:, 0:W], in0=score[:, 0:W], scalar1=s_[:, 6:7], scalar2=None, op0=ALU.is_ge), reads=[score, s_], writes=[mask])

            def idx_fin(g, half):
                MT = MTs[g % 2]
                for T, i, tcs in pend.pop((g, half)):
                    mask = T["mask"]
                    for s0 in range(0, i + 1, 8):
                        n8 = min(8, i + 1 - s0)
                        tp = tpr.next()
                        for q_ in range(n8):
                            cx.op("pe", lambda e, q_=q_: e.transpose(out=tp[:, q_, :], in_=mask[:, (s0 + q_) * 128:(s0 + q_ + 1) * 128], identity=ident_bf[:]), reads=[mask, ident_bf], writes=[tp])
                        cx.op("act", lambda e: e.activation(out=MT[:, s0:s0 + n8, tcs], in_=tp[:, 0:n8, :], func=AF.Copy), reads=[tp], writes=[MT])

            def att_part(g, heads):
                MT = MTs[g % 2]
                cs = slice(g * 512, (g + 1) * 512)
                nblk = 4 * g + 4
                aitems = [dict(h=h, sbk=sbk) for h in heads for sbk in range(nblk)]
                ahs = {}

                def aA(it):
                    h, sbk = it["h"], it["sbk"]
                    if sbk == 0:
                        q, k, v = qr.next(), kr.next(), vr.next()
                        cx.dma("sp", q[:], qbT[h, :, cs], reads=[db("qb")], writes=[q])
                        cx.dma("sp", k[:, 0:nblk * 128], kbT[h, :, 0:nblk * 128], reads=[db("kb")], writes=[k])
                        cx.dma("sp", v[:, 0:nblk, :], vab[0:nblk * 128, HW + h * 128:HW + (h + 1) * 128].rearrange("(st p) d -> p st d", p=128), reads=[db("vab")], writes=[v])
                        ahs[h] = (q, k, v)
                    q, k, v = ahs[h]
                    lp = lpr.next()
                    cx.op("pe", lambda e: e.matmul(lp[:], k[:, sbk * 128:(sbk + 1) * 128], q[:], start=True, stop=True), reads=[k, q], writes=[lp])
                    eb = ebr.next()
                    cx.op("act", lambda e: e.activation(out=eb[:], in_=lp[:], func=AF.Exp, scale=scale), reads=[lp], writes=[eb])
                    pt = ptr_.next()
                    cx.op("pool", lambda e: e.tensor_tensor(out=pt[:], in0=eb[:], in1=MT[:, sbk, :], op=ALU.mult), reads=[eb, MT], writes=[pt])
                    it["pt"] = pt

                def aB(it):
                    h, sbk, pt = it["h"], it["sbk"], it["pt"]
                    q, k, v = ahs[h]
                    if sbk == 0:
                        ahs[(h, "y")] = (ypr.next(), npr.next())
                    yp, np_ = ahs[(h, "y")]
                    cx.op("pe", lambda e: e.matmul(yp[:], v[:, sbk, :], pt[:], start=(sbk == 0), stop=(sbk == nblk - 1)), reads=[v, pt], writes=[yp])
                    cx.op("pe", lambda e: e.matmul(np_[:], ones_bf[:], pt[:], start=(sbk == 0), stop=(sbk == nblk - 1)), reads=[ones_bf, pt], writes=[np_])
                    if sbk == nblk - 1:
                        ye, ne = yer.next(), ner.next()
                        cx.op("act", lambda e: e.activation(out=ye[:], in_=yp[:], func=AF.Copy), reads=[yp], writes=[ye])
                        cx.op("act", lambda e: e.activation(out=ne[:], in_=np_[:], func=AF.Copy), reads=[np_], writes=[ne])
                        ahs[(h, "e")] = (ye, ne)

                swpipe(aitems, [aA, lambda x: None, aB])
                for h in heads:
                    ye, ne = ahs[(h, "e")]
                    cx.op("dve", lambda e: e.reciprocal(out=ne[:], in_=ne[:]), reads=[ne], writes=[ne])
                    ys = ysr.next()
                    cx.op("dve", lambda e: e.tensor_tensor(out=ys[:], in0=ye[:], in1=ne[:], op=ALU.mult), reads=[ye, ne], writes=[ys])
                    cx.dma("sp", yabT[1, h, :, cs], ys[:], reads=[ys], writes=[db("yab")])

            idx_unit(0, 0)
            idx_unit(0, 1)
            idx_fin(0, 0)
            if NTB > 1:
                idx_unit(1, 0)
            idx_fin(0, 1)
            h1 = list(range(0, H // 2))
            h2 = list(range(H // 2, H))
            for g in range(NTB):
                att_part(g, h1)
                if g + 1 < NTB:
                    idx_unit(g + 1, 1)
                    idx_fin(g + 1, 0)
                att_part(g, h2)
                if g + 1 < NTB:
                    if g + 2 < NTB:
                        idx_unit(g + 2, 0)
                    idx_fin(g + 1, 1)
        cx.barrier()

    def phase_out(l, b):
        xsrc = x_in if l == 0 else xres
        with contextlib.ExitStack() as st0:
            mT = cx.sb(st0, "mT", [128, KC, S], BF16)
            mbufs = [Buf(mT.t, f"m{i}") for i in range(NTB)]
            with contextlib.ExitStack() as st:
                wa = cx.sb(st, "owa", [128, H, D], BF16)
                wb = cx.sb(st, "owb", [128, H, D], BF16)
                cx.dma("pool", wa[:], w_a[l].rearrange("(h p) n -> p h n", p=128), writes=[wa])
                cx.dma("pool", wb[:], w_b[l].rearrange("(h p) n -> p h n", p=128), writes=[wb])
                yar = Ring([cx.sb(st, f"oya{i}", [128, H, 512], BF16) for i in range(2)])
                ybr = Ring([cx.sb(st, f"oyb{i}", [128, H, 512], BF16) for i in range(2)])
                par = Ring([cx.ps(st, f"opa{i}", [128, 512], F32) for i in range(4)])
                pbr = Ring([cx.ps(st, f"opb{i}", [128, 512], F32) for i in range(4)])
                sar = Ring([cx.sb(st, f"osa{i}", [128, 512], BF16) for i in range(4)])
                sbr = Ring([cx.sb(st, f"osb{i}", [128, 512], BF16) for i in range(4)])
                m1r = Ring([cx.sb(st, f"om1{i}", [128, 512], F32) for i in range(3)])
                m2r = Ring([cx.sb(st, f"om2{i}", [128, 512], F32) for i in range(3)])
                for tb in range(NTB):
                    cs = slice(tb * 512, (tb + 1) * 512)
                    ya, yb = yar.next(), ybr.next()
                    cx.dma("sp", ya[:], yabT[0, :, :, cs].rearrange("h p s -> p h s"), reads=[db("yab")], writes=[ya])
                    cx.dma("sp", yb[:], yabT[1, :, :, cs].rearrange("h p s -> p h s"), reads=[db("yab")], writes=[yb])
                    for c in range(KC):
                        pa, pb = par.next(), pbr.next()
                        for h in range(H):
                            cx.op("pe", lambda e, h=h: e.matmul(pa[:], wa[:, h, c * 128:(c + 1) * 128], ya[:, h, :], start=(h == 0), stop=(h == H - 1)), reads=[wa, ya], writes=[pa])
                        for h in range(H):
                            cx.op("pe", lambda e, h=h: e.matmul(pb[:], wb[:, h, c * 128:(c + 1) * 128], yb[:, h, :], start=(h == 0), stop=(h == H - 1)), reads=[wb, yb], writes=[pb])
                        sa, sb_ = sar.next(), sbr.next()
                        cx.dma("sp", sa[:], sgaT[c, :, cs], reads=[db("ga")], writes=[sa])
                        cx.dma("sp", sb_[:], sgbT[c, :, cs], reads=[db("gb")], writes=[sb_])
                        m1, m2 = m1r.next(), m2r.next()
                        cx.op("dve", lambda e: e.tensor_tensor(out=m1[:], in0=pa[:], in1=sa[:], op=ALU.mult), reads=[pa, sa], writes=[m1])
                        cx.op("dve", lambda e: e.tensor_tensor(out=m2[:], in0=pb[:], in1=sb_[:], op=ALU.mult), reads=[pb, sb_], writes=[m2])
                        cx.op("dve", lambda e: e.tensor_tensor(out=mT[:, c, cs], in0=m1[:], in1=m2[:], op=ALU.add), reads=[m1, m2], writes=[mbufs[tb]])
            cx.barrier()
            with contextlib.ExitStack() as st:
                wor = Ring([cx.sb(st, f"owo{i}", [128, KC, 512], BF16) for i in range(2)])
                por = Ring([cx.ps(st, f"opo{i}", [128, 512], F32) for i in range(8)])
                gtb = cx.sb(st, "ogt", [128, D], F32)
                cx.dma("sp", gtb[:], modrow[b:b + 1, 2 * D:3 * D].broadcast_to([128, D]), reads=[db("modrow")], writes=[gtb])
                cx.op("dve", lambda e: e.tensor_scalar(out=gtb[:], in0=gtb[:], scalar1=1.0, scalar2=None, op0=ALU.add), reads=[gtb], writes=[gtb])
                xr = Ring([cx.sb(st, f"ox{i}", [128, 512], F32) for i in range(6)])
                tr = Ring([cx.sb(st, f"ot{i}", [128, 512], F32) for i in range(3)])
                wov = w_o[l].rearrange("(kc p) n -> p kc n", p=128)
                wos = {}

                def oA(u):
                    cg, tt = u["cg"], u["tt"]
                    if tt == 0:
                        wo = wor.next()
                        cx.dma("pool", wo[:], wov[:, :, cg * 512:(cg + 1) * 512], writes=[wo])
                        wos[cg] = wo
                    gs = slice(cg * 512, (cg + 1) * 512)
                    xt = xr.next()
                    cx.dma("sp", xt[:], xsrc[b, tt * 128:(tt + 1) * 128, gs], reads=[db(("x", b, tt, cg))], writes=[xt])
                    u["xt"] = xt

                def oB(u):
                    cg, tt, xt = u["cg"], u["tt"], u["xt"]
                    wo = wos[cg]
                    gs = slice(cg * 512, (cg + 1) * 512)
                    po = por.next()
                    for kc in range(KC):
                        cx.op("pe", lambda e, kc=kc: e.matmul(po[:], mT[:, kc, tt * 128:(tt + 1) * 128], wo[:, kc, :], start=(kc == 0), stop=(kc == KC - 1)), reads=[wo, mbufs[tt // 4]], writes=[po])
                    t_ = tr.next()
                    cx.op("dve", lambda e: e.tensor_tensor(out=t_[:], in0=po[:], in1=gtb[:, gs], op=ALU.mult), reads=[po, gtb], writes=[t_])
                    cx.op("dve", lambda e: e.tensor_tensor(out=xt[:], in0=xt[:], in1=t_[:], op=ALU.add), reads=[xt, t_], writes=[xt])
                    cx.dma("sp", xres[b, tt * 128:(tt + 1) * 128, gs], xt[:], reads=[xt], writes=[db(("x", b, tt, cg)), db(("x", b, tt))])

                swpipe([dict(cg=cg, tt=tt) for cg in range(D // 512) for tt in range(NT)], [oA, lambda u: None, lambda u: None, oB])
        cx.barrier()

    def phase_ffn(l, b):
        TH = min(1024, S)
        NHF = S // TH
        NTBH = TH // 512
        NJ = DFF // 128
        FH = 4 if (NJ % 4 == 0 and NJ >= 8) else (2 if (NJ % 2 == 0 and NJ >= 4) else 1)
        JH = NJ // FH
        NFC = 2 * NJ
        JG = 3 if JH >= 3 else JH
        with contextlib.ExitStack() as st:
            hT = cx.sb(st, "fhT", [128, KC, TH], BF16)
            hbufs = [Buf(hT.t, f"fh{i}") for i in range(TH // 128)]
            gT = cx.sb(st, "fgT", [128, JH, TH], BF16)
            gbufs = [Buf(gT.t, f"fg{i}") for i in range(NTBH)]
            halo = cx.sb(st, "fhalo", [128, NFC, 2], F32)
            cx.op("dve", lambda e: e.memset(halo[:], 0.0), writes=[halo])
            gtb = cx.sb(st, "fgt", [128, D], F32)
            cx.dma("sp", gtb[:], modrow[b:b + 1, 5 * D:6 * D].broadcast_to([128, D]), reads=[db("modrow")], writes=[gtb])
            cx.op("dve", lambda e: e.tensor_scalar(out=gtb[:], in0=gtb[:], scalar1=1.0, scalar2=None, op0=ALU.add), reads=[gtb], writes=[gtb])
            wuv = w_up[l].rearrange("(kc p) n -> p kc n", p=128)
            wdv = w_down[l].rearrange("(j p) n -> p j n", p=128)
            for hf in range(NHF):
                with contextlib.ExitStack() as st2:
                    phase_norm(st2, xres, b, hf * (TH // 128), TH // 128, 1, hT, hbufs)
                cx.barrier()
                with contextlib.ExitStack() as st2:
                    wur = Ring([cx.sb(st2, f"fwu{i}", [128, KC, 2, JG * 128], BF16) for i in range(2)])
                    pur = Ring([cx.ps(st2, f"fpu{i}", [128, 512], F32) for i in range(4)])
                    ucr = Ring([cx.sb(st2, f"fuc{i}", [128, 514], F32) for i in range(4)])
                    t0r = Ring([cx.sb(st2, f"ft0{i}", [128, 512], F32) for i in range(2)])
                    t1r = Ring([cx.sb(st2, f"ft1{i}", [128, 512], F32) for i in range(2)])
                    t2r = Ring([cx.sb(st2, f"ft2{i}", [128, 512], F32) for i in range(4)])
                    sar = Ring([cx.sb(st2, f"fsa{i}", [128, 512], F32) for i in range(2)])
                    wdr = Ring([cx.sb(st2, f"fwd{i}", [128, JH, 512], BF16) for i in range(2)])
                    pdr = Ring([cx.ps(st2, f"fpd{i}", [128, 512], F32) for i in range(4)])
                    xr = Ring([cx.sb(st2, f"fx{i}", [128, 512], F32) for i in range(6)])
                    tr = Ring([cx.sb(st2, f"ftt{i}", [128, 512], F32) for i in range(3)])
                    for fh in range(FH):
                        prev_uc = {}
                        wus = {}

                        def fA(u, fh=fh, prev_uc=prev_uc, wus=wus):
                            jj, tb = u["jj"], u["tb"]
                            j = fh * JH + jj
                            if tb == 0 and jj % JG == 0:
                                ng = min(JG, JH - jj)
                                wu = wur.next()
                                cx.dma("pool", wu[:, :, 0, 0:ng * 128], wuv[:, :, j * 128:(j + ng) * 128], writes=[wu])
                                cx.dma("pool", wu[:, :, 1, 0:ng * 128], wuv[:, :, DFF + j * 128:DFF + (j + ng) * 128], writes=[wu])
                                wus[jj // JG] = wu
                            wu = wus[jj // JG]
                            jo = (jj % JG) * 128
                            cs = slice(tb * 512, (tb + 1) * 512)
                            res = []
                            for ab in range(2):
                                ch = ab * NJ + j
                                pu = pur.next()
                                for kc in range(KC):
                                    cx.op("pe", lambda e, kc=kc, ab=ab: e.matmul(pu[:], wu[:, kc, ab, jo:jo + 128], hT[:, kc, cs], start=(kc == 0), stop=(kc == KC - 1)), reads=[wu] + hbufs[tb * 4:(tb + 1) * 4], writes=[pu])
                                uc = ucr.next()
                                if tb == 0:
                                    cx.op("dve", lambda e, ch=ch: e.tensor_copy(out=uc[:, 0:2], in_=halo[:, ch, :]), reads=[halo], writes=[uc])
                                else:
                                    pu_ = prev_uc[(jj, ab)]
                                    cx.op("dve", lambda e, pu_=pu_: e.tensor_copy(out=uc[:, 0:2], in_=pu_[:, 512:514]), reads=[pu_], writes=[uc])
                                cx.op("act", lambda e: e.activation(out=uc[:, 2:514], in_=pu[:], func=AF.Copy), reads=[pu], writes=[uc])
                                if tb == NTBH - 1:
                                    cx.op("dve", lambda e, ch=ch: e.tensor_copy(out=halo[:, ch, :], in_=uc[:, 512:514]), reads=[uc], writes=[halo])
                                prev_uc[(jj, ab)] = uc
                                t0 = t0r.next()
                                cx.op("act", lambda e, ch=ch: e.activation(out=t0[:], in_=pu[:], func=AF.Identity, scale=cwT[:, 2, ch:ch + 1], bias=cwT[:, 3, ch:ch + 1]), reads=[pu, cwT], writes=[t0])
                                t1 = t1r.next()
                                cx.op("dve", lambda e, ch=ch: e.scalar_tensor_tensor(out=t1[:], in0=uc[:, 1:513], scalar=cwT[:, 1, ch:ch + 1], in1=t0[:], op0=ALU.mult, op1=ALU.add), reads=[uc, cwT, t0], writes=[t1])
                                t2 = t2r.next()
                                cx.op("dve", lambda e, ch=ch: e.scalar_tensor_tensor(out=t2[:], in0=uc[:, 0:512], scalar=cwT[:, 0, ch:ch + 1], in1=t1[:], op0=ALU.mult, op1=ALU.add), reads=[uc, cwT, t1], writes=[t2])
                                res.append(t2)
                            u["res"] = res

                        def fB(u):
                            jj, tb, res = u["jj"], u["tb"], u["res"]
                            cs = slice(tb * 512, (tb + 1) * 512)
                            sa = sar.next()
                            cx.op("act", lambda e: e.activation(out=sa[:], in_=res[0][:], func=AF.Silu), reads=[res[0]], writes=[sa])
                            cx.op("dve", lambda e: e.tensor_tensor(out=gT[:, jj, cs], in0=sa[:], in1=res[1][:], op=ALU.mult), reads=[sa, res[1]], writes=[gbufs[tb]])

                        swpipe([dict(jj=jj, tb=tb) for jj in range(JH) for tb in range(NTBH)], [fA, fB])
                        wds = {}

                        def dA(u, fh=fh, wds=wds):
                            cg, tl = u["cg"], u["tl"]
                            if tl == 0:
                                wd = wdr.next()
                                cx.dma("pool", wd[:], wdv[:, fh * JH:(fh + 1) * JH, cg * 512:(cg + 1) * 512], writes=[wd])
                                wds[cg] = wd
                            gs = slice(cg * 512, (cg + 1) * 512)
                            tt = hf * (TH // 128) + tl
                            xt = xr.next()
                            cx.dma("sp", xt[:], xres[b, tt * 128:(tt + 1) * 128, gs], reads=[db(("x", b, tt, cg)), db(("x", b, tt))], writes=[xt])
                            u["xt"] = xt

                        def dB(u, wds=wds):
                            cg, tl, xt = u["cg"], u["tl"], u["xt"]
                            wd = wds[cg]
                            gs = slice(cg * 512, (cg + 1) * 512)
                            tt = hf * (TH // 128) + tl
                            pd = pdr.next()
                            for jj in range(JH):
                                cx.op("pe", lambda e, jj=jj: e.matmul(pd[:], gT[:, jj, tl * 128:(tl + 1) * 128], wd[:, jj, :], start=(jj == 0), stop=(jj == JH - 1)), reads=[wd, gbufs[tl // 4]], writes=[pd])
                            t_ = tr.next()
                            cx.op("dve", lambda e: e.tensor_tensor(out=t_[:], in0=pd[:], in1=gtb[:, gs], op=ALU.mult), reads=[pd, gtb], writes=[t_])
                            cx.op("dve", lambda e: e.tensor_tensor(out=xt[:], in0=xt[:], in1=t_[:], op=ALU.add), reads=[xt, t_], writes=[xt])
                            cx.dma("sp", xres[b, tt * 128:(tt + 1) * 128, gs], xt[:], reads=[xt], writes=[db(("x", b, tt, cg)), db(("x", b, tt))])

                        swpipe([dict(cg=cg, tl=tl) for cg in range(D // 512) for tl in range(TH // 128)], [dA, lambda u: None, lambda u: None, dB])
                cx.barrier()
        cx.barrier()

    def phase_final(xsrc):
        with contextlib.ExitStack() as st:
            gfb = cx.sb(st, "gfb", [128, D], F32)
            cx.dma("sp", gfb[:], g_final.rearrange("(o d) -> o d", o=1).broadcast_to([128, D]), writes=[gfb])
            xr = Ring([cx.sb(st, f"zx{i}", [128, D], F32) for i in range(3)])
            jr = cx.sb(st, "zjunk", [128, D], BF16)
            ssr = Ring([cx.sb(st, f"zss{i}", [128, 4], F32) for i in range(2)])
            for b in range(NB):
                for tt in range(NT):
                    xt = xr.next()
                    cx.dma("sp", xt[:], xsrc[b, tt * 128:(tt + 1) * 128, :], reads=[db(("x", b, tt))], writes=[xt])
                    ss = ssr.next()
                    cx.op("act", lambda e: e.activation(out=jr[:], in_=xt[:], func=AF.Square, accum_out=ss[:, 0:1]), reads=[xt], writes=[jr, ss])
                    cx.op("act", lambda e: e.activation(out=ss[:, 1:2], in_=ss[:, 0:1], func=AF.Ln, scale=1.0 / D, bias=eps_t[:]), reads=[ss, eps_t], writes=[ss])
                    cx.op("act", lambda e: e.activation(out=ss[:, 2:3], in_=ss[:, 1:2], func=AF.Exp, scale=-0.5), reads=[ss], writes=[ss])
                    cx.op("dve", lambda e: e.scalar_tensor_tensor(out=xt[:], in0=xt[:], scalar=ss[:, 2:3], in1=gfb[:], op0=ALU.mult, op1=ALU.mult), reads=[xt, ss, gfb], writes=[xt])
                    cx.dma("sp", out[b, tt * 128:(tt + 1) * 128, :], xt[:], reads=[xt], writes=[db("out")])

    phases = phases or ("setup", "mod", "norm", "proj", "sb", "dsa", "out", "ffn", "final")
    cx.barrier()
    if "setup" in phases:
        phase_setup()
    for l in layers:
        if "mod" in phases:
            phase_mod(l)
        for b in range(NB):
            xsrc = x_in if l == 0 else xres
            with contextlib.ExitStack() as stA:
                hT = cx.sb(stA, "hT", [128, KC, S], BF16)
                hbufs = [Buf(hT.t, f"h{i}") for i in range(NT)]
                if "norm" in phases:
                    with contextlib.ExitStack() as st2:
                        phase_norm(st2, xsrc, b, 0, NT, 0, hT, hbufs)
                    cx.barrier()
                if "proj" in phases:
                    phase_proj(l, b, hT, hbufs)
            if "sb" in phases:
                phase_sb()
            if "dsa" in phases:
                phase_dsa()
            if "out" in phases:
                phase_out(l, b)
            if "ffn" in phases:
                phase_ffn(l, b)
    if "final" in phases:
        phase_final(xres)
    cx.finish()
    G.close()
    print("instructions emitted:", cx.ninst)
    return nc


def make_in_maps(cfg, inputs, n_cores):
    NB = cfg.NB
    cols = cfg.ext_cols()
    w_in_ext = np.ascontiguousarray(inputs["w_in"][:, :, cols])
    consts = make_consts(cfg)
    shared = {k: np.ascontiguousarray(inputs[k]) for k in
              ("w_a", "w_b", "w_o", "w_ada", "b_ada", "g_mix", "g_ffn", "w_up", "conv_w", "conv_b", "w_down", "g_final")}
    shared["w_in_ext"] = w_in_ext
    shared["consts"] = consts
    maps = []
    for i in range(n_cores):
        m = dict(shared)
        m["x"] = np.ascontiguousarray(inputs["x"][i * NB:(i + 1) * NB])
        m["c"] = np.ascontiguousarray(inputs["c"][i * NB:(i + 1) * NB])
        m["positions"] = np.ascontiguousarray(inputs["positions"][i * NB:(i + 1) * NB]).astype(np.int32)
        maps.append(m)
    return maps


_CACHE = {}


def kernel(**inputs):
    cfg = Cfg()
    n_cores = 8
    if "nc" not in _CACHE:
        _CACHE["nc"] = build(cfg)
    nc = _CACHE["nc"]
    maps = make_in_maps(cfg, inputs, n_cores)
    res = run_bass_kernel_spmd(nc, maps, core_ids=list(range(n_cores)))
    return np.concatenate([np.asarray(r["out"]) for r in res.results], axis=0).astype(np.float32)
```
